# Optimizing a Trainium2 kernel written in Bass

```python
import math
import jax, jax.numpy as jnp
from jax import lax
import numpy as np

D_MODEL = 1024
BATCH = 8
SEQ = 4096
DEPTH = 2

GRID_W = 64
CTX_LEN = 256
HEAD_DIM = 64
GROUP_WIDTH = D_MODEL // 4
D_MIX = 4 * GROUP_WIDTH
BLOCK = 128
WINDOW = 128
A_HEADS = GROUP_WIDTH // HEAD_DIM
A_KV = A_HEADS // 2
B_HEADS = GROUP_WIDTH // HEAD_DIM
B_KV = B_HEADS // 2
C_HEADS = GROUP_WIDTH // HEAD_DIM
C_DK = HEAD_DIM
C_DV = HEAD_DIM
C_CHUNK = 128
M_HEADS = GROUP_WIDTH // HEAD_DIM
M_HEADDIM = HEAD_DIM
M_GROUPS = 2
M_DSTATE = 128
M_CONV = 3
M_CHUNK = 128
M_WIDTH = M_HEADS * M_HEADDIM
CONV_CH = M_WIDTH + 2 * M_GROUPS * M_DSTATE
D_FF = 4 * D_MODEL
ROPE_BASE = 10000.0
EPS = 1e-6
SPLIT_SIZES = (
    A_HEADS * HEAD_DIM, A_KV * HEAD_DIM, A_KV * HEAD_DIM,
    B_HEADS * HEAD_DIM, B_KV * HEAD_DIM, B_KV * HEAD_DIM,
    C_HEADS * C_DK, C_HEADS * C_DK, C_HEADS * C_DV, C_HEADS * C_DV,
    4 * C_HEADS,
    M_WIDTH, M_WIDTH, M_GROUPS * M_DSTATE, M_GROUPS * M_DSTATE,
    2 * M_HEADS,
)
SPLIT_POINTS = tuple(int(s) for s in np.cumsum(SPLIT_SIZES)[:-1])
N_IN = int(sum(SPLIT_SIZES))

kernel_name = 'hybrid_parallel_head_groups_dit'


def rmsnorm(x, g):
    xf = x.astype(jnp.float32)
    y = xf * lax.rsqrt(jnp.mean(xf * xf, axis=-1, keepdims=True) + EPS)
    return (y * g.astype(jnp.float32)).astype(x.dtype)


def axial_rope_tables(rows, dtype):
    row = jnp.repeat(jnp.arange(rows, dtype=jnp.float32), GRID_W)
    col = jnp.tile(jnp.arange(GRID_W, dtype=jnp.float32), rows)
    half = HEAD_DIM // 2
    inv_freq = ROPE_BASE ** (-jnp.arange(0, half, 2, dtype=jnp.float32) / half)
    ang = jnp.concatenate([row[:, None] * inv_freq, col[:, None] * inv_freq], axis=-1)
    return jnp.cos(ang).astype(dtype), jnp.sin(ang).astype(dtype)


def apply_axial_rope(x, cos, sin):
    half = HEAD_DIM // 2
    quarter = half // 2
    parts = []
    for axis in range(2):
        xa = x[..., axis * half:(axis + 1) * half]
        ca = cos[:, None, axis * quarter:(axis + 1) * quarter]
        sa = sin[:, None, axis * quarter:(axis + 1) * quarter]
        x1, x2 = xa[..., :quarter], xa[..., quarter:]
        parts += [x1 * ca - x2 * sa, x2 * ca + x1 * sa]
    return jnp.concatenate(parts, axis=-1)


def attend(qblk, keys, vals, mask=None, sink=None):
    s = jnp.einsum('bqgrd,bkgd->bgrqk', qblk, keys).astype(jnp.float32)
    if mask is not None:
        s = jnp.where(mask, s, -jnp.inf)
    if sink is not None:
        s_sink = jnp.broadcast_to(sink.astype(jnp.float32)[None, :, :, None, None], s.shape[:-1] + (1,))
        p = jax.nn.softmax(jnp.concatenate([s, s_sink], axis=-1), axis=-1)[..., :-1]
    else:
        p = jax.nn.softmax(s, axis=-1)
    return jnp.einsum('bgrqk,bkgd->bqgrd', p.astype(vals.dtype), vals)


def window_attention(lat, ctxp, sink, cos, sin, need_ctx):
    q, k, v = lat
    qc, kc, vc = ctxp
    bsz, n_tok = q.shape[:2]
    n_ctx = qc.shape[1]
    nb = n_tok // BLOCK
    rep = A_HEADS // A_KV
    scale = HEAD_DIM ** -0.5
    q = apply_axial_rope(q.reshape(bsz, n_tok, A_HEADS, HEAD_DIM), cos, sin) * scale
    k = apply_axial_rope(k.reshape(bsz, n_tok, A_KV, HEAD_DIM), cos, sin)
    v = v.reshape(bsz, n_tok, A_KV, HEAD_DIM)
    kc = kc.reshape(bsz, n_ctx, A_KV, HEAD_DIM)
    vc = vc.reshape(bsz, n_ctx, A_KV, HEAD_DIM)
    sink = sink.reshape(A_KV, rep)

    def band(t):
        tb = t.reshape(bsz, nb, BLOCK, A_KV, HEAD_DIM)
        pad = jnp.zeros_like(tb[:, :1])
        prev = jnp.concatenate([pad, tb[:, :-1]], axis=1)
        nxt = jnp.concatenate([tb[:, 1:], pad], axis=1)
        return jnp.moveaxis(jnp.concatenate([prev, tb, nxt], axis=2), 1, 0)

    qb = jnp.moveaxis(q.reshape(bsz, nb, BLOCK, A_KV, rep, HEAD_DIM), 1, 0)
    blk = jnp.arange(nb)[:, None, None]
    q_pos = blk * BLOCK + jnp.arange(BLOCK)[None, :, None]
    k_pos = (blk - 1) * BLOCK + jnp.arange(3 * BLOCK)[None, None, :]
    band_mask = (jnp.abs(q_pos - k_pos) <= WINDOW) & (k_pos >= 0) & (k_pos < n_tok)
    ctx_mask = jnp.ones((BLOCK, n_ctx), dtype=bool)

    def latent_block(args):
        qblk, kb, vb, m = args
        return attend(qblk, jnp.concatenate([kb, kc], axis=1), jnp.concatenate([vb, vc], axis=1),
                      jnp.concatenate([m, ctx_mask], axis=-1), sink)

    y = lax.map(latent_block, (qb, band(k), band(v), band_mask))
    y = jnp.moveaxis(y, 0, 1).reshape(bsz, n_tok, A_HEADS * HEAD_DIM)
    y_ctx = None
    if need_ctx:
        qch = qc.reshape(bsz, n_ctx, A_KV, rep, HEAD_DIM) * scale
        y_ctx = attend(qch, kc, vc, sink=sink).reshape(bsz, n_ctx, A_HEADS * HEAD_DIM)
    return y, y_ctx


def dense_attention(lat, ctxp, g_q, g_k, cos, sin, need_ctx):
    q, k, v = lat
    qc, kc, vc = ctxp
    bsz, n_tok = q.shape[:2]
    n_ctx = qc.shape[1]
    nb = n_tok // BLOCK
    rep = B_HEADS // B_KV
    scale = HEAD_DIM ** -0.5
    q = apply_axial_rope(rmsnorm(q.reshape(bsz, n_tok, B_HEADS, HEAD_DIM), g_q), cos, sin) * scale
    k = apply_axial_rope(rmsnorm(k.reshape(bsz, n_tok, B_KV, HEAD_DIM), g_k), cos, sin)
    v = v.reshape(bsz, n_tok, B_KV, HEAD_DIM)
    kc = rmsnorm(kc.reshape(bsz, n_ctx, B_KV, HEAD_DIM), g_k)
    vc = vc.reshape(bsz, n_ctx, B_KV, HEAD_DIM)
    keys = jnp.concatenate([kc, k], axis=1)
    vals = jnp.concatenate([vc, v], axis=1)
    qb = jnp.moveaxis(q.reshape(bsz, nb, BLOCK, B_KV, rep, HEAD_DIM), 1, 0)
    y = lax.map(lambda qblk: attend(qblk, keys, vals), qb)
    y = jnp.moveaxis(y, 0, 1).reshape(bsz, n_tok, B_HEADS * HEAD_DIM)
    y_ctx = None
    if need_ctx:
        qch = rmsnorm(qc.reshape(bsz, n_ctx, B_KV, rep, HEAD_DIM), g_q) * scale
        y_ctx = attend(qch, kc, vc).reshape(bsz, n_ctx, B_HEADS * HEAD_DIM)
    return y, y_ctx


def flip_time(t, direction):
    return t[:, ::-1] if direction == 1 else t


def mlstm_chunk_scan(q, k, v, ig, fg, state):
    bsz, n_tok, nh, _ = q.shape
    nc = n_tok // C_CHUNK

    def chunks(t):
        return jnp.moveaxis(t.reshape(bsz, nc, C_CHUNK, *t.shape[2:]), 1, 0)

    logf = jax.nn.log_sigmoid(fg)
    tril = jnp.tril(jnp.ones((C_CHUNK, C_CHUNK), dtype=bool))[None, :, :, None]

    def body(carry, inp):
        c_mat, n_vec, m_prev = carry
        qc, kc, vc, ic, lfc = inp
        b = jnp.cumsum(lfc, axis=1)
        dmat = jnp.where(tril, b[:, :, None, :] - b[:, None, :, :] + ic[:, None, :, :], -jnp.inf)
        m_prior = b + m_prev[:, None, :]
        m_t = jnp.maximum(m_prior, dmat.max(axis=2))
        w = jnp.exp(dmat - m_t[:, :, None, :])
        s = jnp.einsum('bthd,bshd->btsh', qc, kc) * w
        decay_prior = jnp.exp(m_prior - m_t)
        num = jnp.einsum('btsh,bshv->bthv', s, vc) + decay_prior[..., None] * jnp.einsum('bthd,bhvd->bthv', qc, c_mat)
        den = s.sum(axis=2) + decay_prior * jnp.einsum('bthd,bhd->bth', qc, n_vec)
        h = num / jnp.maximum(jnp.abs(den), jnp.exp(-m_t))[..., None]
        b_end = b[:, -1]
        g = b_end[:, None, :] - b + ic
        m_new = jnp.maximum(b_end + m_prev, g.max(axis=1))
        wk = jnp.exp(g - m_new[:, None, :])
        carry_decay = jnp.exp(b_end + m_prev - m_new)
        c_new = carry_decay[..., None, None] * c_mat + jnp.einsum('bsh,bshv,bshd->bhvd', wk, vc, kc)
        n_new = carry_decay[..., None] * n_vec + jnp.einsum('bsh,bshd->bhd', wk, kc)
        return (c_new, n_new, m_new), h

    state, hs = lax.scan(body, state, (chunks(q), chunks(k), chunks(v), chunks(ig), chunks(logf)))
    return jnp.moveaxis(hs, 0, 1).reshape(bsz, n_tok, nh, v.shape[-1]), state


def mlstm_mixer(lat, ctxp, b_i, b_f, g_head, need_ctx):
    f32 = jnp.float32

    def heads(q, k, v, gates):
        bsz, n = q.shape[:2]
        qh = q.reshape(bsz, n, C_HEADS, C_DK).astype(f32) * C_DK ** -0.5
        kh = k.reshape(bsz, n, C_HEADS, C_DK).astype(f32)
        vh = v.reshape(bsz, n, C_HEADS, C_DV).astype(f32)
        g = gates.reshape(bsz, n, 4, C_HEADS).astype(f32)
        ig = g[:, :, 0::2] + b_i.astype(f32)
        fg = g[:, :, 1::2] + b_f.astype(f32)
        return qh, kh, vh, ig, fg

    q, k, v, o, gates = lat
    qc, kc, vc, oc, gatesc = ctxp
    lat_h = heads(q, k, v, gates)
    ctx_h = heads(qc, kc, vc, gatesc)
    bsz = q.shape[0]
    outs_lat, outs_ctx = [], []
    for d in range(2):
        state0 = (jnp.zeros((bsz, C_HEADS, C_DV, C_DK), f32), jnp.zeros((bsz, C_HEADS, C_DK), f32),
                  jnp.zeros((bsz, C_HEADS), f32))
        cq, ck, cv, ci, cf = ctx_h
        hc, st = mlstm_chunk_scan(flip_time(cq, d), flip_time(ck, d), flip_time(cv, d),
                                  flip_time(ci[:, :, d], d), flip_time(cf[:, :, d], d), state0)
        lq, lk, lv, li, lf = lat_h
        hl, _ = mlstm_chunk_scan(flip_time(lq, d), flip_time(lk, d), flip_time(lv, d),
                                 flip_time(li[:, :, d], d), flip_time(lf[:, :, d], d), st)
        outs_lat.append(flip_time(hl, d))
        outs_ctx.append(flip_time(hc, d))

    def finish(h, og):
        b, n = h.shape[:2]
        hn = rmsnorm(h, g_head.reshape(C_HEADS, C_DV)).reshape(b, n, C_HEADS * C_DV)
        return (hn * jax.nn.sigmoid(og.astype(f32))).astype(og.dtype)

    y = finish(outs_lat[0] + outs_lat[1], o)
    y_ctx = finish(outs_ctx[0] + outs_ctx[1], oc) if need_ctx else None
    return y, y_ctx


def depthwise_conv(u, w, b):
    ch = u.shape[-1]
    y = lax.conv_general_dilated(u, w[:, None, :].astype(u.dtype), window_strides=(1,),
                                 padding=[(M_CONV // 2, M_CONV // 2)],
                                 dimension_numbers=('NWC', 'WIO', 'NWC'), feature_group_count=ch)
    return y + b.astype(u.dtype)


def ssd_chunk_scan(x, dt, a, bmat, cmat, h0):
    bsz, n_tok, nh, hp = x.shape
    nc = n_tok // M_CHUNK

    def chunks(t):
        return jnp.moveaxis(t.reshape(bsz, nc, M_CHUNK, *t.shape[2:]), 1, 0)

    tril = jnp.tril(jnp.ones((M_CHUNK, M_CHUNK), dtype=bool))[None, :, :, None]

    def body(h, inp):
        xc, dtc, bc, cc = inp
        cum = jnp.cumsum(dtc * a, axis=1)
        decay = jnp.exp(jnp.where(tril, cum[:, :, None, :] - cum[:, None, :, :], -jnp.inf))
        s = jnp.einsum('bthn,bshn->btsh', cc, bc) * decay * dtc[:, None, :, :]
        y = jnp.einsum('btsh,bshp->bthp', s, xc) + jnp.exp(cum)[..., None] * jnp.einsum('bthn,bhpn->bthp', cc, h)
        w_end = jnp.exp(cum[:, -1:, :] - cum) * dtc
        h_new = jnp.exp(cum[:, -1, :])[:, :, None, None] * h + jnp.einsum('bsh,bshp,bshn->bhpn', w_end, xc, bc)
        return h_new, y

    h_fin, ys = lax.scan(body, h0, (chunks(x), chunks(dt), chunks(bmat), chunks(cmat)))
    return jnp.moveaxis(ys, 0, 1).reshape(bsz, n_tok, nh, hp), h_fin


def mamba_mixer(lat, ctxp, conv_w, conv_b, a_log, dt_bias, d_skip, g_ssm, need_ctx):
    f32 = jnp.float32

    def prep(xm, bm, cm, dt):
        bsz, n = xm.shape[:2]
        u = jax.nn.silu(depthwise_conv(jnp.concatenate([xm, bm, cm], axis=-1), conv_w, conv_b)).astype(f32)
        xs, bs, cs = jnp.split(u, [M_WIDTH, M_WIDTH + M_GROUPS * M_DSTATE], axis=-1)
        rep = M_HEADS // M_GROUPS
        xs = xs.reshape(bsz, n, M_HEADS, M_HEADDIM)
        bs = jnp.repeat(bs.reshape(bsz, n, M_GROUPS, M_DSTATE), rep, axis=2)
        cs = jnp.repeat(cs.reshape(bsz, n, M_GROUPS, M_DSTATE), rep, axis=2)
        dts = jax.nn.softplus(dt.reshape(bsz, n, 2, M_HEADS).astype(f32) + dt_bias.astype(f32))
        return xs, bs, cs, dts

    xm, z, bm, cm, dt = lat
    xmc, zc, bmc, cmc, dtc = ctxp
    lat_p = prep(xm, bm, cm, dt)
    ctx_p = prep(xmc, bmc, cmc, dtc)
    a = -jnp.exp(a_log.astype(f32))
    bsz = xm.shape[0]
    outs_lat, outs_ctx = [], []
    for d in range(2):
        h0 = jnp.zeros((bsz, M_HEADS, M_HEADDIM, M_DSTATE), f32)
        cx, cb, cc, cdt = ctx_p
        yc, st = ssd_chunk_scan(flip_time(cx, d), flip_time(cdt[:, :, d], d), a[d],
                                flip_time(cb, d), flip_time(cc, d), h0)
        lx, lb, lc, ldt = lat_p
        yl, _ = ssd_chunk_scan(flip_time(lx, d), flip_time(ldt[:, :, d], d), a[d],
                               flip_time(lb, d), flip_time(lc, d), st)
        outs_lat.append(flip_time(yl, d))
        outs_ctx.append(flip_time(yc, d))

    def finish(y, xs, zz):
        b, n = zz.shape[:2]
        y = (y + d_skip.astype(f32)[:, None] * xs).reshape(b, n, M_WIDTH)
        return rmsnorm(y * jax.nn.silu(zz.astype(f32)), g_ssm).astype(zz.dtype)

    y = finish(outs_lat[0] + outs_lat[1], lat_p[0], z)
    y_ctx = finish(outs_ctx[0] + outs_ctx[1], ctx_p[0], zc) if need_ctx else None
    return y, y_ctx


def sq_relu_ffn(h, w1, w2):
    return jnp.square(jax.nn.relu(h @ w1)) @ w2


def setup_inputs(seed: int = 0) -> dict:
    key = jax.random.key(seed)
    ks = jax.random.split(key, 26)
    f32 = jnp.float32

    def nrm(k, shape, s):
        return jax.random.normal(k, shape, f32) * s

    dt0 = jnp.exp(jax.random.uniform(ks[18], (DEPTH, 2, M_HEADS), f32, math.log(1e-3), math.log(1e-1)))
    return {
        'x': nrm(ks[0], (BATCH, SEQ, D_MODEL), 1.0),
        'c': nrm(ks[1], (BATCH, D_MODEL), 1.0),
        'ctx': nrm(ks[2], (BATCH, CTX_LEN, D_MODEL), 1.0),
        'c_ctx': nrm(ks[3], (D_MODEL,), 1.0),
        'w_ada': nrm(ks[4], (DEPTH, D_MODEL, 6 * D_MODEL), 0.5 * D_MODEL ** -0.5),
        'b_ada': nrm(ks[5], (DEPTH, 6 * D_MODEL), 0.02),
        'g_norm1': 1.0 + nrm(ks[6], (DEPTH, D_MODEL), 0.02),
        'g_norm2': 1.0 + nrm(ks[7], (DEPTH, D_MODEL), 0.02),
        'w_in': nrm(ks[8], (DEPTH, D_MODEL, N_IN), D_MODEL ** -0.5),
        'sink_a': nrm(ks[9], (DEPTH, A_HEADS), 1.0),
        'g_q_b': 1.0 + nrm(ks[10], (DEPTH, HEAD_DIM), 0.02),
        'g_k_b': 1.0 + nrm(ks[11], (DEPTH, HEAD_DIM), 0.02),
        'b_igate': nrm(ks[12], (DEPTH, 2, C_HEADS), 0.1),
        'b_fgate': jnp.linspace(3.0, 6.0, C_HEADS, dtype=f32)[None, None, :] + nrm(ks[13], (DEPTH, 2, C_HEADS), 0.1),
        'g_mlstm': 1.0 + nrm(ks[14], (DEPTH, C_HEADS * C_DV), 0.02),
        'conv_w': nrm(ks[15], (DEPTH, M_CONV, CONV_CH), M_CONV ** -0.5),
        'conv_b': nrm(ks[16], (DEPTH, CONV_CH), 0.02),
        'a_log': jnp.log(jax.random.uniform(ks[17], (DEPTH, 2, M_HEADS), f32, 1.0, 16.0)),
        'dt_bias': dt0 + jnp.log(-jnp.expm1(-dt0)),
        'd_skip': 1.0 + nrm(ks[19], (DEPTH, M_HEADS), 0.1),
        'g_ssm': 1.0 + nrm(ks[20], (DEPTH, M_WIDTH), 0.02),
        'w_out': nrm(ks[21], (DEPTH, D_MIX, D_MODEL), D_MIX ** -0.5),
        'w_ff1': nrm(ks[22], (DEPTH, D_MODEL, D_FF), D_MODEL ** -0.5),
        'w_ff2': nrm(ks[23], (DEPTH, D_FF, D_MODEL), D_FF ** -0.5),
        'g_final': 1.0 + nrm(ks[24], (D_MODEL,), 0.02),
    }


def reference(x, c, ctx, c_ctx, w_ada, b_ada, g_norm1, g_norm2, w_in, sink_a, g_q_b, g_k_b,
              b_igate, b_fgate, g_mlstm, conv_w, conv_b, a_log, dt_bias, d_skip, g_ssm,
              w_out, w_ff1, w_ff2, g_final):
    n_tok = x.shape[1]
    rows = n_tok // GRID_W
    cos, sin = axial_rope_tables(rows, x.dtype)
    xc = ctx
    for layer in range(DEPTH):
        need_ctx = layer < DEPTH - 1
        mod = jax.nn.silu(c) @ w_ada[layer] + b_ada[layer]
        mod_c = jax.nn.silu(c_ctx) @ w_ada[layer] + b_ada[layer]
        sh1, sc1, gt1, sh2, sc2, gt2 = jnp.split(mod[:, None, :], 6, axis=-1)
        sh1c, sc1c, gt1c, sh2c, sc2c, gt2c = jnp.split(mod_c[None, None, :], 6, axis=-1)

        h = rmsnorm(x, g_norm1[layer]) * (1 + sc1) + sh1
        hc = rmsnorm(xc, g_norm1[layer]) * (1 + sc1c) + sh1c
        p = jnp.split(h @ w_in[layer], SPLIT_POINTS, axis=-1)
        pc = jnp.split(hc @ w_in[layer], SPLIT_POINTS, axis=-1)
        ya, ya_c = window_attention(p[0:3], pc[0:3], sink_a[layer], cos, sin, need_ctx)
        yb, yb_c = dense_attention(p[3:6], pc[3:6], g_q_b[layer], g_k_b[layer], cos, sin, need_ctx)
        ym, ym_c = mlstm_mixer(p[6:11], pc[6:11], b_igate[layer], b_fgate[layer], g_mlstm[layer], need_ctx)
        yd, yd_c = mamba_mixer(p[11:16], pc[11:16], conv_w[layer], conv_b[layer], a_log[layer],
                               dt_bias[layer], d_skip[layer], g_ssm[layer], need_ctx)
        x = x + gt1 * (jnp.concatenate([ya, yb, ym, yd], axis=-1) @ w_out[layer])
        h2 = rmsnorm(x, g_norm2[layer]) * (1 + sc2) + sh2
        x = x + gt2 * sq_relu_ffn(h2, w_ff1[layer], w_ff2[layer])

        if need_ctx:
            xc = xc + gt1c * (jnp.concatenate([ya_c, yb_c, ym_c, yd_c], axis=-1) @ w_out[layer])
            h2c = rmsnorm(xc, g_norm2[layer]) * (1 + sc2c) + sh2c
            xc = xc + gt2c * sq_relu_ffn(h2c, w_ff1[layer], w_ff2[layer])
    return rmsnorm(x, g_final)
```

```python
import numpy as np
import ml_dtypes
import concourse.bass as bass
import concourse.mybir as mybir
from concourse.bass_utils import run_bass_kernel_spmd

F32 = mybir.dt.float32
BF16 = mybir.dt.bfloat16
AF = mybir.ActivationFunctionType
ALU = mybir.AluOpType
AX = mybir.AxisListType

EPOCH = 12000
STRICT = True


class Reg:
    __slots__ = ("w", "r", "psum")

    def __init__(self, psum=False):
        self.w = None
        self.r = {}
        self.psum = psum


class View:
    __slots__ = ("reg", "ap")

    def __init__(self, reg, ap):
        self.reg = reg
        self.ap = ap


class _Keyed:
    def __init__(self, tile, key):
        self.tile = tile
        self.key = key

    def _reg(self):
        t = self.tile
        reg = t.regs.get(self.key)
        if reg is None:
            reg = t.regs[self.key] = Reg(t if t.is_psum else None)
        return reg

    def __getitem__(self, idx):
        return View(self._reg(), self.tile.t[idx])

    def ap(self, ap):
        return View(self._reg(), ap)


class Tile:
    def __init__(self, t, is_psum=False):
        self.t = t
        self.is_psum = is_psum
        self.bank_readers = {}
        self.regs = {}
        self.F = int(np.prod(t.shape[1:]))

    def v(self, key):
        return _Keyed(self, key)

    def __getitem__(self, idx):
        return _Keyed(self, None)[idx]

    def raw(self, p0, npart, off, dims, key=None):
        ap = bass.AP(self.t, p0 * self.F + off, [[self.F, npart]] + [list(d) for d in dims])
        return _Keyed(self, key).ap(ap)


class Prog:
    def __init__(self, nc):
        self.nc = nc
        self.eng = {"pe": nc.tensor, "act": nc.scalar, "dve": nc.vector, "pool": nc.gpsimd, "sp": nc.sync}
        self.seq = {e: 0 for e in self.eng}
        self.sems = {e: [] for e in self.eng}
        self.seen = {e: {} for e in self.eng}
        self.dma_pool = {}
        self._cms = []
        self._scopes = []
        self.ninst = 0
        for q, n in (("sp", 16), ("pool", 8), ("act", 4)):
            sl = []
            for i in range(n):
                sl.append([self._sem(f"d_{q}_{i}"), 0])
            self.dma_pool[q] = [sl, 0]

    def _sem(self, name):
        cm = self.nc.semaphore(name)
        s = cm.__enter__()
        self._cms.append(cm)
        return s

    def _alloc(self, cm, is_psum=False):
        t = cm.__enter__()
        if self._scopes:
            self._scopes[-1].append(cm)
        else:
            self._cms.append(cm)
        return Tile(t, is_psum)

    def sbuf(self, name, shape, dtype):
        return self._alloc(self.nc.sbuf_tensor(name + f"_{self.ninst}", list(shape), dtype))

    def psum(self, name, shape, dtype=F32):
        return self._alloc(self.nc.psum_tensor(name + f"_{self.ninst}", list(shape), dtype), True)

    def dram(self, name, shape, dtype, kind="Internal"):
        t = self.nc.dram_tensor(name, list(shape), dtype, kind=kind)
        return Tile(t)

    def scope_begin(self):
        self._scopes.append([])

    def scope_end(self):
        self.barrier()
        cms = self._scopes.pop()
        for cm in reversed(cms):
            cm.__exit__(None, None, None)

    def _esem(self, e, seq):
        ep = (seq - 1) // EPOCH
        while len(self.sems[e]) <= ep:
            self.sems[e].append(self._sem(f"s_{e}_{len(self.sems[e])}"))
        return self.sems[e][ep], (seq - 1) % EPOCH + 1

    def _need(self, e, dep):
        if dep[0] == "e":
            _, e2, seq = dep
            key = ("e", e2)
            if self.seen[e].get(key, 0) >= seq:
                return None
            self.seen[e][key] = seq
            return self._esem(e2, seq)
        _, q, si, val = dep
        key = ("d", q, si)
        if self.seen[e].get(key, 0) >= val:
            return None
        self.seen[e][key] = val
        return (self.dma_pool[q][0][si][0], val)

    def _wait(self, e, dep):
        n = self._need(e, dep)
        if n is not None:
            self.eng[e].wait_ge(n[0], n[1])
            self.ninst += 1

    def _deps(self, e, outs, ins):
        deps = []
        for v in ins:
            if v.reg.w is not None:
                deps.append(v.reg.w)
            if v.reg.psum is not None:
                for e2, val in v.reg.psum.bank_readers.items():
                    if e2 != e:
                        deps.append(("e", e2, val))
        for v in outs:
            w = v.reg.w
            if w is not None:
                if not (w[0] == "e" and w[1] == e and (e == "pe" or not STRICT)):
                    deps.append(w)
            for k, val in v.reg.r.items():
                if k[0] == "e":
                    if k[1] == e and not STRICT:
                        continue
                    deps.append(("e", k[1], val))
                else:
                    deps.append(("d", k[1], k[2], val))
        best = {}
        for d in deps:
            k = d[:2] if d[0] == "e" else d[:3]
            if k not in best or d[-1] > best[k][-1]:
                best[k] = d
        needs = []
        for d in best.values():
            n = self._need(e, d)
            if n is not None:
                needs.append(n)
        return needs

    def op(self, e, fn, outs, ins, embed=True):
        self.nops = getattr(self, 'nops', 0) + 1
        if self.nops > DBG.get('maxops', 10 ** 9):
            return None
        needs = self._deps(e, outs, ins)
        emb = None
        if embed and e != "pe" and needs:
            emb = needs.pop()
        for sem, val in needs:
            self.eng[e].wait_ge(sem, val)
            self.ninst += 1
        inst = fn()
        if emb is not None:
            inst._wait_ge(emb[0], emb[1])
        self.seq[e] += 1
        seq = self.seq[e]
        sem, val = self._esem(e, seq)
        inst.then_inc(sem, 1)
        self.ninst += 1
        me = ("e", e, seq)
        for v in ins:
            v.reg.r[("e", e)] = seq
            if v.reg.psum is not None:
                v.reg.psum.bank_readers[e] = seq
        for v in outs:
            v.reg.w = me
            v.reg.r = {}
        return inst

    def dma(self, q, out, in_, **kw):
        e = q
        if getattr(self, 'nops', 0) > DBG.get('maxops', 10 ** 9):
            return None
        for sem_, val_ in self._deps(e, [out], [in_]):
            self.eng[e].wait_ge(sem_, val_)
            self.ninst += 1
        pool = self.dma_pool[q]
        si = pool[1] % len(pool[0])
        pool[1] += 1
        slot = pool[0][si]
        if slot[1] > 0:
            self._wait(e, ("d", q, si, slot[1]))
        slot[1] += 16
        inst = self.eng[e].dma_start(out=out.ap, in_=in_.ap, **kw)
        inst.then_inc(slot[0], 16)
        self.ninst += 1
        in_.reg.r[("d", q, si)] = slot[1]
        out.reg.w = ("d", q, si, slot[1])
        out.reg.r = {}
        return inst

    def barrier(self):
        for e in self.eng:
            for q, (sl, _) in self.dma_pool.items():
                for si, (sem, val) in enumerate(sl):
                    if val > 0:
                        self._wait(e, ("d", q, si, val))
            for e2 in self.eng:
                if self.seq[e2] > 0:
                    self._wait(e, ("e", e2, self.seq[e2]))

    def finish(self):
        self.barrier()

    def mm(self, out, lhsT, rhs, start=True, stop=True):
        return self.op("pe", lambda: self.nc.tensor.matmul(out.ap, lhsT.ap, rhs.ap, start=start, stop=stop),
                       [out], [lhsT, rhs])

    def tr(self, out, in_, ident):
        return self.op("pe", lambda: self.nc.tensor.transpose(out.ap, in_.ap, ident.ap), [out], [in_, ident])

    def act(self, out, in_, func, bias=None, scale=1.0, accum_out=None):
        ins = [in_]
        kw = {}
        if bias is not None:
            if isinstance(bias, View):
                ins.append(bias)
                kw["bias"] = bias.ap
            else:
                kw["bias"] = bias
        if isinstance(scale, View):
            ins.append(scale)
            kw["scale"] = scale.ap
        else:
            kw["scale"] = scale
        outs = [out]
        if accum_out is not None:
            outs.append(accum_out)
            kw["accum_out"] = accum_out.ap
        return self.op("act", lambda: self.nc.scalar.activation(out=out.ap, in_=in_.ap, func=func, **kw), outs, ins,
                       embed=(accum_out is None))

    def tt(self, out, in0, in1, op, eng="dve"):
        E = self.eng[eng]
        return self.op(eng, lambda: E.tensor_tensor(out=out.ap, in0=in0.ap, in1=in1.ap, op=op), [out], [in0, in1])

    def ts(self, out, in0, s1, op0, s2=None, op1=None, eng="dve"):
        E = self.eng[eng]
        ins = [in0]
        a1, a2 = s1, s2
        if isinstance(s1, View):
            ins.append(s1)
            a1 = s1.ap
        if isinstance(s2, View):
            ins.append(s2)
            a2 = s2.ap
        kw = {}
        if op1 is not None:
            kw["op1"] = op1
        return self.op(eng, lambda: E.tensor_scalar(out=out.ap, in0=in0.ap, scalar1=a1, scalar2=a2, op0=op0, **kw),
                       [out], ins)

    def stt(self, out, in0, s, in1, op0, op1):
        E = self.nc.vector
        ins = [in0, in1]
        a = s
        if isinstance(s, View):
            ins.append(s)
            a = s.ap
        return self.op("dve", lambda: E.scalar_tensor_tensor(out=out.ap, in0=in0.ap, scalar=a, in1=in1.ap,
                                                              op0=op0, op1=op1), [out], ins)

    def copy(self, out, in_, eng="dve"):
        if eng == "act":
            return self.op("act", lambda: self.nc.scalar.copy(out=out.ap, in_=in_.ap), [out], [in_])
        E = self.eng[eng]
        return self.op(eng, lambda: E.tensor_copy(out=out.ap, in_=in_.ap), [out], [in_])

    def memset(self, out, val, eng="dve"):
        E = self.eng[eng]
        return self.op(eng, lambda: E.memset(out.ap, val), [out], [])

    def recip(self, out, in_):
        return self.op("dve", lambda: self.nc.vector.reciprocal(out=out.ap, in_=in_.ap), [out], [in_])

    def reduce(self, out, in_, op, axis=AX.X):
        return self.op("dve", lambda: self.nc.vector.tensor_reduce(out=out.ap, in_=in_.ap, axis=axis, op=op),
                       [out], [in_])

    def scan(self, out, d0, d1, initial, op0, op1):
        ins = [d0, d1]
        a = initial
        if isinstance(initial, View):
            ins.append(initial)
            a = initial.ap
        return self.op("dve", lambda: self.nc.vector.tensor_tensor_scan(out=out.ap, data0=d0.ap, data1=d1.ap,
                                                                        initial=a, op0=op0, op1=op1), [out], ins)


L = 2
D = 1024
NCTX = 256
NLAT = 4096
T = NCTX + NLAT
NT = T // 128
NIN = 3096
DFF = 4096
EPS = 1e-6
BLOCKS = [(0, 2)] + [(2 + 4 * j, 4) for j in range(8)]
NEG = -30000.0
DBG = {}

PC_G1, PC_G2, PC_CW, PC_CB, PC_GML, PC_GSSM, PC_DSK, PC_N = 0, 8, 16, 34, 40, 42, 44, 46
PB_GQ, PB_GK, PB_SINK, PB_GB, PB_ALOG, PB_DTB, PB_N = 0, 64, 128, 132, 148, 156, 164


def build_program(nlayers=L, stop=None, dbg=False):
    nc = bass.Bass("TRN2", target_bir_lowering=False)
    P = Prog(nc)
    EI = "ExternalInput"
    xin = P.dram("xin", [T, D], F32, EI)
    ccol_d = P.dram("ccol", [128, 16], F32, EI)
    w_ada = P.dram("w_ada", [L, D, 6 * D], F32, EI)
    badac_d = P.dram("bada_col", [128, L * 48], F32, EI)
    badag_d = P.dram("bada_gt", [128, L * 2 * D], F32, EI)
    w_in = P.dram("w_in", [L, D, NIN], F32, EI)
    w_out = P.dram("w_out", [L, D, D], F32, EI)
    w_ff1 = P.dram("w_ff1", [L, D, DFF], F32, EI)
    w_ff2 = P.dram("w_ff2", [L, DFF, D], F32, EI)
    pcol_d = P.dram("pcol", [128, L * PC_N], F32, EI)
    pbc_d = P.dram("pbc", [128, L * PB_N], F32, EI)
    gfin_d = P.dram("gfin", [128, D], F32, EI)
    identf_d = P.dram("ident_f", [128, 128], F32, EI)
    negm_d = P.dram("negm", [128, 2 * 128], F32, EI)
    tri_d = P.dram("tri", [128, 2 * 128], F32, EI)
    sel_d = P.dram("sel65", [128, 128], F32, EI)
    oblk_d = P.dram("onesblk", [128, 128], F32, EI)
    eye_d = P.dram("eye8x", [8, 8 * 128], F32, EI)
    rope_d = P.dram("rope", [128, NT * 64], F32, EI)
    wmask_d = P.dram("wmask", [128, 6 * 512], F32, EI)
    out_d = P.dram("out", [NLAT, D], F32, "ExternalOutput")
    okind = "ExternalOutput" if dbg else "Internal"
    Xs = P.dram("Xs", [T, D], F32, okind)
    hTs = P.dram("hTs", [D, T], BF16, okind)
    yTs = P.dram("yTs", [D, T], BF16, okind)

    ident_f = P.sbuf("ident_f", [128, 128], F32)
    ident_b = P.sbuf("ident_b", [128, 128], BF16)
    ones_f = P.sbuf("ones_f", [128, 128], F32)
    zeros_f = P.sbuf("zeros_f", [128, 128], F32)
    negm = P.sbuf("negm", [128, 2, 128], F32)
    tri = P.sbuf("tri", [128, 2, 128], F32)
    sel65 = P.sbuf("sel65", [128, 128], F32)
    onesblk = P.sbuf("onesblk", [128, 128], F32)
    eye8x = P.sbuf("eye8x", [8, 8, 128], F32)
    pcol = P.sbuf("pcol", [128, L * PC_N], F32)
    pbc = P.sbuf("pbc", [128, L * PB_N], F32)
    ccol = P.sbuf("ccol", [128, 16], F32)
    badac = P.sbuf("badac", [128, L * 48], F32)
    modv = P.sbuf("modv", [128, 4, 8, 2], F32)
    gtb = P.sbuf("gtb", [128, 2, 2, D], F32)
    esink = P.sbuf("esink", [128, L * 4], F32)
    abc = P.sbuf("abc", [128, L * 8], F32)

    P.dma("sp", ident_f[:, :], identf_d[:, :])
    P.dma("pool", ident_b[:, :], identf_d[:, :])
    P.dma("sp", negm[:, :, :], negm_d.raw(0, 128, 0, [[128, 2], [1, 128]]))
    P.dma("sp", tri[:, :, :], tri_d.raw(0, 128, 0, [[128, 2], [1, 128]]))
    P.dma("sp", sel65[:, :], sel_d[:, :])
    P.dma("sp", onesblk[:, :], oblk_d[:, :])
    P.dma("sp", eye8x[:, :, :], eye_d.raw(0, 8, 0, [[128, 8], [1, 128]]))
    P.dma("sp", pcol[:, :], pcol_d[:, :])
    P.dma("sp", pbc[:, :], pbc_d[:, :])
    P.dma("sp", ccol[:, :], ccol_d[:, :])
    P.dma("sp", badac[:, :], badac_d[:, :])
    P.memset(ones_f[:, :], 1.0)
    P.memset(zeros_f[:, :], 0.0)
    for l in range(L):
        P.act(esink[:, l * 4:(l + 1) * 4], pbc[:, l * PB_N + PB_SINK:l * PB_N + PB_SINK + 4], AF.Exp)
        P.act(abc[:, l * 8:(l + 1) * 8], pbc[:, l * PB_N + PB_ALOG:l * PB_N + PB_ALOG + 8], AF.Exp)
        P.ts(abc[:, l * 8:(l + 1) * 8], abc[:, l * 8:(l + 1) * 8], -1.0, ALU.mult)

    def dview(tile, key, ap):
        return tile.v(key).ap(ap)

    def hT_view(c0, c1):
        return hTs.t.ap().rearrange("(kc p) t -> p kc t", p=128)[:, :, c0:c1]

    def wcast(dst, wt, l, c0, c1, rows=D):
        src = wt.t.ap()[l].rearrange("(kc p) n -> p kc n", p=128)
        npc = (c1 - c0 + 511) // 512
        step = (c1 - c0 + npc - 1) // npc
        for a in range(c0, c1, step):
            b = min(c1, a + step)
            P.dma("pool", dst.v(None).ap(dst.t[:, :, a - c0:b - c0]), wt.v(l).ap(src[:, :, a:b]))

    def phase_mod(l):
        P.scope_begin()
        wA = [P.sbuf(f"wA{i}", [128, 8, 512], BF16) for i in range(2)]
        sc = P.sbuf("sc", [128, 16], F32)
        scb = P.sbuf("scb", [128, 16], BF16)
        screp = P.sbuf("screp", [128, 16, 128], BF16)
        bgt = P.sbuf("bgt", [128, 2, D], F32)
        mcol = P.sbuf("mcol", [128, 48, 2], F32)
        psC = P.psum("psC", [128, 96], F32)
        psB = [P.psum(f"psBm{i}", [128, 512], F32) for i in range(2)]
        P.dma("sp", bgt[:, :, :], badag_d.raw(0, 128, l * 2 * D, [[D, 2], [1, D]]))
        P.act(sc[:, :], ccol[:, :], AF.Silu)
        P.copy(scb[:, :], sc[:, :])
        P.copy(screp[:, :, :], sc.raw(0, 128, 0, [[1, 16], [0, 128]]))
        src = w_ada.t.ap()[l].rearrange("(kc p) n -> p kc n", p=128)
        nb = 0
        for pc in range(12):
            w = wA[pc % 2]
            P.dma("pool", w[:, :, :], w_ada.v(l).ap(src[:, :, pc * 512:(pc + 1) * 512]))
            which = pc // 2
            if which in (2, 5):
                for wi in range(2):
                    ps = psB[nb % 2]
                    nb += 1
                    for kc in range(8):
                        P.mm(ps[:, :], screp[:, wi * 8 + kc, :], w[:, kc, :], start=(kc == 0), stop=(kc == 7))
                    gi = 0 if which == 2 else 1
                    half = pc % 2
                    P.tt(gtb[:, gi, wi, half * 512:(half + 1) * 512], ps[:, :], bgt[:, gi, half * 512:(half + 1) * 512],
                         ALU.add)
            else:
                for cc in range(4):
                    ch = pc * 4 + cc
                    for kc in range(8):
                        P.mm(psC[:, ch * 2:ch * 2 + 2], w[:, kc, cc * 128:(cc + 1) * 128],
                             scb.raw(0, 128, kc, [[8, 2]]), start=(kc == 0), stop=(kc == 7))
        for which in (0, 1, 3, 4):
            c0 = which * 8
            P.tt(mcol[:, c0:c0 + 8, :], psC.raw(0, 128, c0 * 2, [[2, 8], [1, 2]]),
                 badac.raw(0, 128, l * 48 + c0, [[1, 8], [0, 2]]), ALU.add)
        for k, (gcol, scw, shw) in enumerate(((PC_G1, 1, 0), (PC_G2, 4, 3))):
            gap = pcol.raw(0, 128, l * PC_N + gcol, [[1, 8], [0, 2]])
            P.stt(modv[:, 2 * k, :, :], mcol[:, scw * 8:scw * 8 + 8, :], 1.0, gap, ALU.add, ALU.mult)
            P.copy(modv[:, 2 * k + 1, :, :], mcol[:, shw * 8:shw * 8 + 8, :])
        P.scope_end()

    def norm_block(Xtile, nt, ci, mi, hb, ss, xh, psT, junk):
        for n in range(nt):
            P.act(junk[:, :], Xtile[:, n, :], AF.Square, accum_out=ss[:, n:n + 1])
        P.act(ss[:, 4:4 + nt], ss[:, 0:nt], AF.Ln, scale=1.0 / D, bias=EPS)
        P.act(ss[:, 4:4 + nt], ss[:, 4:4 + nt], AF.Exp, scale=-0.5)
        def evac(n):
            pst = psT[n % 2]
            for kc in range(8):
                P.ts(hb[:, kc, n * 128:(n + 1) * 128], pst[:, kc * 128:(kc + 1) * 128],
                     modv[:, mi, kc, ci:ci + 1], ALU.mult, modv[:, mi + 1, kc, ci:ci + 1], ALU.add)

        for n in range(nt):
            x_ = xh[n % 2]
            P.ts(x_[:, :], Xtile[:, n, :], ss[:, 4 + n:5 + n], ALU.mult)
            pst = psT[n % 2]
            for kc in range(8):
                P.tr(pst[:, kc * 128:(kc + 1) * 128], x_[:, kc * 128:(kc + 1) * 128], ident_f[:, :])
            if n > 0:
                evac(n - 1)
        evac(nt - 1)

    def x_src(l, bi):
        t0, nt = BLOCKS[bi]
        src = xin if l == 0 else Xs
        return src.v(bi).ap(src.t.ap()[t0 * 128:(t0 + nt) * 128, :].rearrange("(n p) d -> p n d", p=128))

    def phase_norm(l):
        P.scope_begin()
        xb = [P.sbuf(f"xb{i}", [128, 4, D], F32) for i in range(2)]
        junk = P.sbuf("junk", [128, D], BF16)
        xh = [P.sbuf(f"xh{i}", [128, D], F32) for i in range(2)]
        hb = [P.sbuf(f"hb{i}", [128, 8, 512], BF16) for i in range(2)]
        ss = [P.sbuf(f"ss{i}", [128, 8], F32) for i in range(2)]
        psT = [P.psum(f"psTn{i}", [128, D], F32) for i in range(2)]
        for bi, (t0, nt) in enumerate(BLOCKS):
            W = nt * 128
            X = xb[bi % 2]
            P.dma("sp", X[:, 0:nt, :], x_src(l, bi))
            norm_block(X, nt, 1 if bi == 0 else 0, 0, hb[bi % 2], ss[bi % 2], xh, psT, junk)
            P.dma("sp", hTs.v(bi).ap(hT_view(t0 * 128, t0 * 128 + W)), hb[bi % 2][:, :, 0:W])
        P.scope_end()

    def phase_attn(l, mixer, need_ctx):
        P.scope_begin()
        cbase = 0 if mixer == 0 else 512
        wq = P.sbuf("wq", [128, 8, 512], BF16)
        rope = P.sbuf("rope", [128, NT, 64], F32)
        P.dma("sp", rope[:, :, :], rope_d.raw(0, 128, 0, [[64, NT], [1, 64]]))
        wcast(wq, w_in, l, cbase, cbase + 512)
        qT = P.sbuf("qT", [128, 2, T], BF16)
        kTA = P.sbuf("kTA", [128, 2, T], BF16)
        kTB = P.sbuf("kTB", [128, 2, T], BF16)
        va = P.sbuf("va", [128, NT, 2, 65], BF16)
        hb = [P.sbuf(f"hba{i}", [128, 8, 512], BF16) for i in range(2)]
        sq = P.sbuf("sq", [128, 384], F32)
        st6 = P.sbuf("st6", [128, 12], F32)
        qk = [P.sbuf(f"qk{i}", [128, 384], F32) for i in range(2)]
        rt = [P.sbuf(f"rt{i}", [128, 192], F32) for i in range(4)]
        qkr = [P.sbuf(f"qkr{i}", [128, 384], BF16) for i in range(2)]
        kd = [P.sbuf(f"kd{i}", [128, 2, 2, 64], BF16) for i in range(2)]
        wmask = P.sbuf("wmask", [128, 6, 512], BF16)
        pT = [P.sbuf(f"pT{i}", [128, 512], BF16) for i in range(4)]
        osb = [P.sbuf(f"osb{i}", [65, 512], F32) for i in range(2)]
        rec = [P.sbuf(f"rec{i}", [64, 512], F32) for i in range(2)]
        yb = [P.sbuf(f"yb{i}", [128, 512], BF16) for i in range(2)]
        P.scope_begin()
        psP = [P.psum(f"psP{i}", [128, 512], F32) for i in range(2)]
        psTt = P.psum("psTt", [128, 4, 128], BF16)
        if mixer == 0:
            P.dma("pool", wmask[:, :, :], wmask_d.raw(0, 128, 0, [[512, 6], [1, 512]]))
        for ti in range(NT):
            P.memset(va.v(ti)[:, ti, :, 64:65], 1.0, eng="pool")
        P.memset(kTA[64:128, :, :], 0.0)
        P.memset(kTB[0:64, :, :], 0.0, eng="pool")
        P.barrier()
        pb0 = l * PB_N
        tiles = [(bi, t0, nt, n) for bi, (t0, nt) in enumerate(BLOCKS) for n in range(nt)]

        def proj(idx):
            bi, t0, nt, n = tiles[idx]
            W = nt * 128
            h_ = hb[bi % 2]
            if n == 0:
                P.dma("sp", h_[:, :, 0:W], hTs.v(bi).ap(hT_view(t0 * 128, t0 * 128 + W)))
            ps = psP[(t0 + n) % 2]
            for kc in range(8):
                P.mm(ps[:, :], h_[:, kc, n * 128:(n + 1) * 128], wq[:, kc, :], start=(kc == 0), stop=(kc == 7))

        def post(idx):
            bi, t0, nt, n = tiles[idx]
            ti = t0 + n
            col0 = ti * 128
            ps = psP[ti % 2]
            q_ = qk[ti % 2]
            if mixer == 1:
                P.act(sq[:, :], ps[:, 0:384], AF.Square)
                P.reduce(st6[:, 0:6], sq.raw(0, 128, 0, [[64, 6], [1, 64]]), ALU.add)
                P.act(st6[:, 6:12], st6[:, 0:6], AF.Ln, scale=1.0 / 64, bias=EPS)
                P.act(st6[:, 6:12], st6[:, 6:12], AF.Exp, scale=-0.5)
                P.tt(q_.raw(0, 128, 0, [[64, 6], [1, 64]]), ps.raw(0, 128, 0, [[64, 6], [1, 64]]),
                     st6.raw(0, 128, 6, [[1, 6], [0, 64]]), ALU.mult)
                P.tt(q_.raw(0, 128, 0, [[64, 4], [1, 64]]), q_.raw(0, 128, 0, [[64, 4], [1, 64]]),
                     pbc.raw(0, 128, pb0 + PB_GQ, [[0, 4], [1, 64]]), ALU.mult)
                P.tt(q_.raw(0, 128, 256, [[64, 2], [1, 64]]), q_.raw(0, 128, 256, [[64, 2], [1, 64]]),
                     pbc.raw(0, 128, pb0 + PB_GK, [[0, 2], [1, 64]]), ALU.mult)
            else:
                P.copy(q_[:, :], ps[:, 0:384], eng="act")
            hd = [[64, 6], [32, 2], [1, 16]]
            x1 = q_.raw(0, 128, 0, hd)
            x2 = q_.raw(0, 128, 16, hd)
            cs = rope.raw(0, 128, ti * 64, [[0, 6], [16, 2], [1, 16]])
            sn = rope.raw(0, 128, ti * 64 + 32, [[0, 6], [16, 2], [1, 16]])
            r_ = qkr[ti % 2]
            o1 = r_.raw(0, 128, 0, hd)
            o2 = r_.raw(0, 128, 16, hd)
            fl = [[32, 6], [16, 2], [1, 16]]
            ta, tb_, tc, td = (rt[i].raw(0, 128, 0, fl) for i in range(4))
            P.tt(ta, x1, cs, ALU.mult)
            P.tt(tb_, x2, sn, ALU.mult)
            P.tt(tc, x2, cs, ALU.mult)
            P.tt(td, x1, sn, ALU.mult)
            P.tt(o1, ta, tb_, ALU.subtract)
            P.tt(o2, tc, td, ALU.add)
            kd_ = kd[ti % 2]
            P.copy(kd_[:, :, :, :], r_.raw(0, 128, 256, [[64, 2], [0, 2], [1, 64]]), eng="act")
            P.tr(psTt[:, 0, :], r_[:, 0:128], ident_b[:, :])
            P.tr(psTt[:, 1, :], r_[:, 128:256], ident_b[:, :])
            P.tr(psTt[:, 2, :], kd_.raw(0, 128, 0, [[1, 128]]), ident_b[:, :])
            P.tr(psTt[:, 3, :], kd_.raw(0, 128, 128, [[1, 128]]), ident_b[:, :])
            P.copy(qT.v(ti)[:, :, col0:col0 + 128], psTt[:, 0:2, :], eng="act")
            P.copy(kTA.v(ti)[0:64, :, col0:col0 + 128], psTt[0:64, 2:4, :], eng="act")
            P.copy(kTB.v(ti)[64:128, :, col0:col0 + 128], psTt[64:128, 2:4, :])
            P.copy(va.v(ti)[:, ti, :, 0:64], ps.raw(0, 128, 384, [[64, 2], [1, 64]]))

        proj(0)
        for idx in range(len(tiles)):
            if idx + 1 < len(tiles):
                proj(idx + 1)
            post(idx)
        P.scope_end()
        psS = [P.psum(f"psS{i}", [128, 512], F32) for i in range(4)]
        psO = [P.psum(f"psO{i}", [128, 512], F32) for i in range(2)]
        psD = P.psum("psD", [128, 512], F32)
        unit = 0
        scnt = [0]
        pending = [None]
        for bi, (t0, nt) in enumerate(BLOCKS):
            if bi == 0 and not need_ctx:
                continue
            W = nt * 128
            qc0 = t0 * 128
            if bi == 0:
                keys = [(0, None), (1, None)]
            elif mixer == 1:
                keys = [(kt, None) for kt in range(NT)]
            else:
                keys = [(0, None), (1, None)]
                for kt in range(max(2, t0 - 1), min(NT - 1, t0 + 4) + 1):
                    keys.append((kt, kt - t0 + 1))
            for h in range(4):
                g = h // 2
                r0 = (h % 2) * 64
                po = psO[unit % 2]
                nk = len(keys)
                rq = [qT.v(t0 + n)[r0:r0 + 64, g, qc0:qc0 + W] for n in range(nt)]
                sidx = {}

                def emit_S(ki, g=g, r0=r0, W=W, qc0=qc0, keys=keys, rq=rq, sidx=sidx):
                    kt, mk = keys[ki]
                    si = scnt[0]
                    scnt[0] += 1
                    sidx[ki] = si
                    ps = psS[si % len(psS)]
                    kTx = kTA if r0 == 0 else kTB
                    P.op("pe", lambda: nc.tensor.matmul(
                        ps.t[:, 0:W], kTx.t[:, g, kt * 128:(kt + 1) * 128], qT.t[:, g, qc0:qc0 + W],
                        start=True, stop=(mk is None)), [ps[:, 0:W]], [kTx.v(kt)[:, g, 0:1]] + rq)
                    if mk is not None:
                        P.mm(ps[:, 0:W], ident_b[:, :], wmask[:, mk, 0:W], start=False, stop=True)

                def emit_rest(ki, g=g, W=W, keys=keys, po=po, nk=nk, sidx=sidx):
                    kt, mk = keys[ki]
                    si = sidx[ki]
                    ps = psS[si % len(psS)]
                    p_ = pT[si % len(pT)]
                    P.act(p_[:, 0:W], ps[:, 0:W], AF.Exp, scale=0.125)
                    P.op("pe", lambda: nc.tensor.matmul(
                        po.t[0:65, 0:W], va.t[:, kt, g, :], p_.t[:, 0:W], start=(ki == 0), stop=(ki == nk - 1)),
                        [po[0:65, 0:W]], [va.v(kt)[:, kt, g, :], p_[:, 0:W]])

                def finalize(unit=unit, h=h, g=g, r0=r0, W=W, qc0=qc0, po=po, bi=bi):
                    o_ = osb[unit % 2]
                    rc = rec[unit % 2]
                    P.copy(o_[0:65, 0:W], po[0:65, 0:W])
                    P.mm(psD[:, 0:W], sel65[0:65, :], o_[0:65, 0:W])
                    if mixer == 0:
                        P.ts(rc[0:64, 0:W], psD[0:64, 0:W], esink[0:64, l * 4 + h:l * 4 + h + 1], ALU.add)
                        P.recip(rc[0:64, 0:W], rc[0:64, 0:W])
                    else:
                        P.recip(rc[0:64, 0:W], psD[0:64, 0:W])
                    y_ = yb[(unit // 2) % 2]
                    P.tt(y_[r0:r0 + 64, 0:W], o_[0:64, 0:W], rc[0:64, 0:W], ALU.mult)
                    if h % 2 == 1:
                        row0 = mixer * 256 + g * 128
                        P.dma("sp", yTs.v((bi, mixer * 2 + g)).ap(yTs.t.ap()[row0:row0 + 128, qc0:qc0 + W]), y_[:, 0:W])

                LA = 2
                for ki in range(min(LA, nk)):
                    emit_S(ki)
                for ki in range(nk):
                    if ki + LA < nk:
                        emit_S(ki + LA)
                    emit_rest(ki)
                    if ki == 1 and pending[0] is not None:
                        pending[0]()
                        pending[0] = None
                if pending[0] is not None:
                    pending[0]()
                pending[0] = finalize
                unit += 1
        if pending[0] is not None:
            pending[0]()
        P.scope_end()

    def phase_mlstm(l, need_ctx):
        P.scope_begin()
        wC = P.sbuf("wC", [128, 8, 1040], BF16)
        wcast(wC, w_in, l, 1024, 2064)
        qT = P.sbuf("qTc", [128, 2, T], BF16)
        kT = P.sbuf("kTc", [128, 2, T], BF16)
        ktok = P.sbuf("ktok", [128, NT, 256], BF16)
        va = P.sbuf("vac", [128, NT, 4, 65], BF16)
        sigo = P.sbuf("sigo", [128, 2, T], BF16)
        hF = P.sbuf("hF", [128, 2, T], BF16)
        ig = P.sbuf("ig", [128, NT, 8], F32)
        lf = P.sbuf("lf", [128, NT, 8], F32)
        mst = P.sbuf("mst", [128, 8], F32)
        Cf = P.sbuf("Cf", [128, 8, 65], F32)
        Cb = P.sbuf("Cb", [128, 8, 65], BF16)
        P.scope_begin()
        hb = [P.sbuf(f"hbc{i}", [128, 8, 512], BF16) for i in range(2)]
        qkb = [P.sbuf(f"qkb{i}", [128, 512], BF16) for i in range(2)]
        gpre = P.sbuf("gpre", [128, 4, 16], F32)
        ge = P.sbuf("ge", [128, 4, 8], F32)
        psA = [P.psum(f"psA{i}", [128, 512], F32) for i in range(2)]
        psTt = P.psum("psTtc", [128, 4, 128], BF16)
        for ti in range(NT):
            P.memset(va.v(ti)[:, ti, :, 64:65], 1.0, eng="pool")
        pb0 = l * PB_N
        for bi, (t0, nt) in enumerate(BLOCKS):
            W = nt * 128
            h_ = hb[bi % 2]
            P.dma("sp", h_[:, :, 0:W], hTs.v(bi).ap(hT_view(t0 * 128, t0 * 128 + W)))
            for n in range(nt):
                ti = t0 + n
                col0 = ti * 128
                p1, p2 = psA[0], psA[1]
                for kc in range(8):
                    P.mm(p1[:, :], h_[:, kc, n * 128:(n + 1) * 128], wC[:, kc, 0:512], start=(kc == 0), stop=(kc == 7))
                for kc in range(8):
                    P.mm(p2[:, 0:256], h_[:, kc, n * 128:(n + 1) * 128], wC[:, kc, 512:768], start=(kc == 0),
                         stop=(kc == 7))
                for kc in range(8):
                    P.mm(p2[:, 256:272], h_[:, kc, n * 128:(n + 1) * 128], wC[:, kc, 1024:1040], start=(kc == 0),
                         stop=(kc == 7))
                qb = qkb[ti % 2]
                P.copy(qb[:, :], p1[:, :], eng="act")
                P.copy(ktok.v(ti)[:, ti, :], qb[:, 256:512], eng="pool")
                for i in range(4):
                    P.tr(psTt[:, i, :], qb[:, i * 128:(i + 1) * 128], ident_b[:, :])
                P.ts(qT.v(ti)[:, :, col0:col0 + 128], psTt[:, 0:2, :], 0.125, ALU.mult)
                P.copy(kT.v(ti)[:, :, col0:col0 + 128], psTt[:, 2:4, :])
                P.copy(va.v(ti)[:, ti, :, 0:64], p2.raw(0, 128, 0, [[64, 4], [1, 64]]))
                P.tt(gpre[:, n, :], p2[:, 256:272], pbc[:, pb0 + PB_GB:pb0 + PB_GB + 16], ALU.add)
            if DBG.get("c_skipg", 0):
                continue
            P.copy(ig.v(bi).ap(ig.t[:, t0:t0 + nt, :].rearrange("p n (d h) -> p n d h", d=2)),
                   gpre.raw(0, 128, 0, [[16, nt], [8, 2], [1, 4]]))
            P.act(ge.raw(0, 128, 0, [[8, nt], [4, 2], [1, 4]]), gpre.raw(0, 128, 4, [[16, nt], [8, 2], [1, 4]]),
                  AF.Exp, scale=-1.0)
            P.act(ge[:, 0:nt, :], ge[:, 0:nt, :], AF.Ln, bias=1.0)
            P.ts(lf.v(bi)[:, t0:t0 + nt, :], ge[:, 0:nt, :], -1.0, ALU.mult)
            for pr in range(0 if DBG.get("c_skipo", 0) else 2):
                po = psA[pr]
                for kc in range(8):
                    P.mm(po[:, 0:W], wC[:, kc, 768 + pr * 128:768 + (pr + 1) * 128], h_[:, kc, 0:W], start=(kc == 0), stop=(kc == 7))
                P.act(sigo.v(bi)[:, pr, t0 * 128:t0 * 128 + W], po[:, 0:W], AF.Sigmoid)
        P.scope_end()
        ab = [P.sbuf(f"ab{i}", [128, 128], F32) for i in range(2)]
        abT = P.sbuf("abT", [8, 128], F32)
        rhsE = P.sbuf("rhsE", [128, 8, 128], F32)
        Mbc = [P.sbuf(f"Mbc{i}", [128, 4, 128], F32) for i in range(2)]
        f32t = lambda nm: [P.sbuf(f"{nm}{i}", [128, 128], F32) for i in range(4)]
        bf16t = lambda nm: [P.sbuf(f"{nm}{i}", [128, 128], BF16) for i in range(4)]
        wT, dp, nd, dm, hs, sqh, rs = (f32t(n_) for n_ in ("wT", "dp", "nd", "dm", "hs", "sqh", "rs"))
        ST, qd, qz, kw = (bf16t(n_) for n_ in ("ST", "qd", "qz", "kw"))
        wk = [P.sbuf(f"wk{i}", [128, 2], F32) for i in range(4)]
        tmp4 = [P.sbuf(f"tmp4{i}", [128, 4, 128], F32) for i in range(2)]
        bend = [P.sbuf(f"bend{i}", [128, 4], F32) for i in range(2)]
        e14 = [P.sbuf(f"e14{i}", [64, 4, 128], F32) for i in range(2)]
        yb = [P.sbuf(f"ybc{i}", [128, 128], BF16) for i in range(4)]
        ps_bc = P.psum("ps_bc", [128, 1024], F32)
        psHd = [P.psum(f"psHd{i}", [128, 512], F32) for i in range(4)]
        psX = P.psum("psX", [128, 512], F32)
        for t_ in ab + [rhsE] + qd + qz + kw + sqh:
            P.memset(t_.v(None).ap(t_.t.ap()), 0.0)
        P.barrier()
        gml0 = l * PC_N + PC_GML
        it = 0
        for d in range(DBG.get("c_dirs", 2)):
            P.barrier()
            P.memset(mst[:, :], 0.0)
            P.memset(Cf[:, :, :], 0.0)
            P.memset(Cb[:, :, :], 0.0)
            P.barrier()
            order = list(range(NT)) if d == 0 else [1, 0] + list(range(NT - 1, 1, -1))
            last = 127 if d == 0 else 0
            def prologue(c, par, d=d):
                bi = 0 if c < 2 else 1 + (c - 2) // 4
                ab_ = ab[par]
                psBv = psX[:, 0:4]
                P.mm(psBv, tri[:, d, :], lf.v(bi)[:, c, d * 4:(d + 1) * 4])
                P.tt(ab_[:, 0:4], ig.v(bi)[:, c, d * 4:(d + 1) * 4], psBv, ALU.subtract)
                P.copy(ab_[:, 4:8], psBv, eng="act")
                P.tr(psX[:, 128:256], ab_[:, :], ident_f[:, :])
                P.copy(abT[:, :], psX[0:8, 128:256], eng="act")
                P.tt(rhsE[0:8, :, :], abT.raw(0, 8, 0, [[0, 8], [1, 128]]), eye8x[:, :, :], ALU.mult)
                P.mm(ps_bc[:, 0:512], ones_f[:, :], rhsE.raw(0, 128, 0, [[1, 512]]))
                P.mm(ps_bc[:, 512:1024], ones_f[:, :], rhsE.raw(0, 128, 512, [[1, 512]]))

            clist = order[:DBG.get("c_chunks", NT)]
            if clist:
                prologue(clist[0], it % 2)
            for ci_, c in enumerate(clist):
                bi = 0 if c < 2 else 1 + (c - 2) // 4
                cols = slice(c * 128, (c + 1) * 128)
                need_out = need_ctx or c >= 2
                ab_ = ab[it % 2]
                Mb = Mbc[it % 2]
                par = it % 2
                it += 1
                for h in range(4):
                    j = d * 4 + h
                    if d == 0:
                        o_ap = Mb[:, h, :]
                        a_ap = ps_bc[:, h * 128:(h + 1) * 128]
                    else:
                        o_ap = Mb.raw(0, 128, h * 128 + 127, [[-1, 128]])
                        a_ap = ps_bc.raw(0, 128, h * 128 + 127, [[-1, 128]])
                    P.scan(o_ap, zeros_f[:, :], a_ap, mst.v(j)[:, j:j + 1], ALU.add, ALU.max)
                P.copy(bend[par][:, :], ps_bc.raw(0, 128, 512 + last, [[128, 4]]))
                if need_out:
                    P.tt(tmp4[par][:, :, :], negm.raw(0, 128, d * 128, [[0, 4], [1, 128]]), Mb[:, :, :], ALU.subtract)
                    P.tt(e14[par][0:64, :, :], ps_bc.raw(0, 64, 512, [[128, 4], [1, 128]]), Mb[0:64, :, :], ALU.add)
                    P.act(e14[par][0:64, :, :], e14[par][0:64, :, :], AF.Exp, scale=-1.0)
                if ci_ + 1 < len(clist):
                    prologue(clist[ci_ + 1], 1 - par)

                def head_gen(h, d=d, c=c, bi=bi, cols=cols, need_out=need_out, ab_=ab_, Mb=Mb, par=par, last=last):
                    j = d * 4 + h
                    pair = h // 2
                    r0 = (h % 2) * 64
                    rr = slice(r0, r0 + 64)
                    Mend = Mb[:, h, last:last + 1]
                    if need_out:
                        P.copy(qz[h][rr, :], qT.v(c)[rr, pair, cols], eng="act")
                        P.act(dp[h][rr, :], Mb[rr, h, :], AF.Exp, scale=-1.0, bias=mst.v(j)[rr, j:j + 1])
                        yield
                        P.act(wT[h][:, :], tmp4[par][:, h, :], AF.Exp, bias=ab_[:, h:h + 1])
                        P.mm(psHd[h][:, 0:128], kT.v(c)[:, pair, cols], qz[h][:, :])
                        P.tt(qd[h][rr, :], qT.v(c)[rr, pair, cols], dp[h][rr, :], ALU.mult)
                        yield
                        P.tt(ST[h][:, :], psHd[h][:, 0:128], wT[h][:, :], ALU.mult)
                        yield
                        P.mm(psHd[h][0:65, 128:256], va.v(c)[:, c, h, :], ST[h][:, :], start=True, stop=False)
                        P.mm(psHd[h][0:65, 128:256], Cb.v(j)[:, j, :], qd[h][:, :], start=False, stop=True)
                    P.act(wk[h][:, 0:1], Mend, AF.Exp, scale=-1.0, bias=ab_[:, h:h + 1])
                    P.act(wk[h][:, 1:2], Mend, AF.Exp, scale=-1.0, bias=mst.v(j)[:, j:j + 1])
                    yield
                    P.tt(mst.v(j)[:, j:j + 1], bend[par][:, h:h + 1], Mend, ALU.add)
                    P.ts(kw[h][:, rr], ktok.v(c)[:, c, h * 64:(h + 1) * 64], wk[h][:, 0:1], ALU.mult)
                    yield
                    P.mm(psHd[h][:, 256:321], kw[h][:, :], va.v(c)[:, c, h, :])
                    yield
                    P.stt(Cf.v(j)[rr, j, :], Cf.v(j)[rr, j, :], wk[h][rr, 1:2], psHd[h][rr, 256:321], ALU.mult, ALU.add)
                    yield
                    P.copy(Cb.v(j)[rr, j, :], Cf.v(j)[rr, j, :], eng="act")
                    if need_out:
                        P.copy(nd[h][0:65, :], psHd[h][0:65, 128:256], eng="act")
                        yield
                        P.mm(psHd[h][:, 384:512], sel65[0:65, :], nd[h][0:65, :])
                        yield
                        P.act(dm[h][0:64, :], psHd[h][0:64, 384:512], AF.Abs)
                        yield
                        P.tt(dm[h][0:64, :], dm[h][0:64, :], e14[par][0:64, h, :], ALU.max)
                        yield
                        P.recip(dm[h][0:64, :], dm[h][0:64, :])
                        yield
                        if d == 0:
                            P.tt(hF.v(c)[rr, pair, cols], nd[h][0:64, :], dm[h][0:64, :], ALU.mult)
                        else:
                            hs_ = hs[h]
                            P.tt(hs_[rr, :], nd[h][0:64, :], dm[h][0:64, :], ALU.mult)
                            yield
                            P.tt(hs_[rr, :], hs_[rr, :], hF.v(c)[rr, pair, cols], ALU.add)
                            yield
                            P.act(sqh[h][rr, :], hs_[rr, :], AF.Square)
                            yield
                            P.mm(psHd[h][:, 384:512], onesblk[:, :], sqh[h][:, :])
                            yield
                            P.act(rs[h][rr, :], psHd[h][rr, 384:512], AF.Ln, scale=1.0 / 64, bias=EPS)
                            yield
                            P.act(rs[h][rr, :], rs[h][rr, :], AF.Exp, scale=-0.5)
                            yield
                            P.tt(hs_[rr, :], hs_[rr, :], rs[h][rr, :], ALU.mult)
                            yield
                            y_ = yb[par * 2 + pair]
                            P.stt(y_[rr, :], hs_[rr, :], pcol[rr, gml0 + pair:gml0 + pair + 1],
                                  sigo.v(bi)[rr, pair, cols], ALU.mult, ALU.mult)
                            if h % 2 == 1:
                                row0 = 512 + pair * 128
                                P.dma("sp", yTs.v((c, 4 + pair)).ap(yTs.t.ap()[row0:row0 + 128, cols]), y_[:, :])

                gens = [head_gen(h) for h in range(4)]
                while gens:
                    nxt = []
                    for g_ in gens:
                        try:
                            next(g_)
                            nxt.append(g_)
                        except StopIteration:
                            pass
                    gens = nxt
        P.scope_end()

    def phase_ssd(l, need_ctx):
        P.scope_begin()
        wD = P.sbuf("wD", [128, 8, 1032], BF16)
        wcast(wD, w_in, l, 2064, 3096)
        xT = P.sbuf("xTd", [128, 2, T], BF16)
        BT = P.sbuf("BTd", [128, 2, T], BF16)
        CT = P.sbuf("CTd", [128, 2, T], BF16)
        szT = P.sbuf("szT", [128, 2, T], BF16)
        yF = P.sbuf("yF", [128, 2, T], BF16)
        dtv = P.sbuf("dtv", [128, NT, 8], F32)
        dA = P.sbuf("dA", [128, NT, 8], F32)
        P.scope_begin()
        hb = [P.sbuf(f"hbd{i}", [128, 8, 514], BF16) for i in range(2)]
        pre = [P.sbuf(f"pre{i}", [128, 514], F32) for i in range(2)]
        acc = [P.sbuf(f"acc{i}", [128, 512], F32) for i in range(2)]
        dtx = P.sbuf("dtx", [128, 4, 8], F32)
        dty = P.sbuf("dty", [128, 4, 8], F32)
        psA = [P.psum(f"psAd{i}", [128, 512], F32) for i in range(2)]
        psH2 = P.psum("psH2", [128, 512], F32)
        pc0 = l * PC_N
        pb0 = l * PB_N
        dests = [xT, xT, BT, BT, CT, CT]
        colof = [0, 128, 512, 640, 768, 896]
        for bi, (t0, nt) in enumerate(BLOCKS[:DBG.get("d_blocks", 9)]):
            W = nt * 128
            h_ = hb[bi % 2]
            has_l = bi >= 2
            has_r = 1 <= bi < len(BLOCKS) - 1
            c_lo = t0 * 128 - (1 if has_l else 0)
            c_hi = t0 * 128 + W + (1 if has_r else 0)
            d_lo = 0 if has_l else 1
            P.dma("sp", h_[:, :, d_lo:d_lo + (c_hi - c_lo)], hTs.v(bi).ap(hT_view(c_lo, c_hi)))
            if not has_l:
                P.memset(h_[:, :, 0:1], 0.0, eng="pool")
            if not has_r:
                P.memset(h_[:, :, W + 1:W + 2], 0.0, eng="pool")
            n_main = min(512, W + 2)
            for cc in range(DBG.get("d_ncc", 6)):
                ps = psA[cc % 2]
                pr_ = pre[cc % 2]
                wc = colof[cc]
                for kc in range(8):
                    P.mm(ps[:, 0:n_main], wD[:, kc, wc:wc + 128], h_[:, kc, 0:n_main], start=(kc == 0), stop=(kc == 7))
                P.copy(pr_[:, 0:n_main], ps[:, 0:n_main], eng="act")
                if W + 2 > 512:
                    for kc in range(8):
                        P.mm(psH2[:, 0:2], wD[:, kc, wc:wc + 128], h_[:, kc, 512:514], start=(kc == 0), stop=(kc == 7))
                    P.copy(pr_[:, 512:514], psH2[:, 0:2], eng="act")
                a_ = acc[cc % 2]
                cw = lambda jj: pcol[:, pc0 + PC_CW + jj * 6 + cc:pc0 + PC_CW + jj * 6 + cc + 1]
                P.ts(a_[:, 0:W], pr_[:, 0:W], cw(0), ALU.mult)
                P.stt(a_[:, 0:W], pr_[:, 1:W + 1], cw(1), a_[:, 0:W], ALU.mult, ALU.add)
                P.stt(a_[:, 0:W], pr_[:, 2:W + 2], cw(2), a_[:, 0:W], ALU.mult, ALU.add)
                P.act(dests[cc].v(bi)[:, cc % 2, t0 * 128:t0 * 128 + W], a_[:, 0:W], AF.Silu,
                      bias=pcol[:, pc0 + PC_CB + cc:pc0 + PC_CB + cc + 1])
            for zc in range(DBG.get("d_nz", 2)):
                ps = psA[zc]
                for kc in range(8):
                    P.mm(ps[:, 0:W], wD[:, kc, 256 + zc * 128:256 + (zc + 1) * 128], h_[:, kc, 1:W + 1],
                         start=(kc == 0), stop=(kc == 7))
                P.act(szT.v(bi)[:, zc, t0 * 128:t0 * 128 + W], ps[:, 0:W], AF.Silu)
            if DBG.get("d_skipdt", 0) == 1:
                continue
            for n in range(nt):
                for kc in range(8):
                    P.mm(psH2[:, 16 + n * 8:24 + n * 8], h_[:, kc, 1 + n * 128:1 + (n + 1) * 128], wD[:, kc, 1024:1032],
                         start=(kc == 0), stop=(kc == 7))
            if DBG.get("d_skipdt", 0) == 2:
                continue
            dps = psH2.raw(0, 128, 16, [[8, nt], [1, 8]])
            dtops = [
                lambda: P.tt(dtx[:, 0:nt, :], dps, pbc.raw(0, 128, pb0 + PB_DTB, [[0, nt], [1, 8]]), ALU.add),
                lambda: P.act(dty[:, 0:nt, :], dtx[:, 0:nt, :], AF.Abs),
                lambda: P.act(dty[:, 0:nt, :], dty[:, 0:nt, :], AF.Exp, scale=-1.0),
                lambda: P.act(dty[:, 0:nt, :], dty[:, 0:nt, :], AF.Ln, bias=1.0),
                lambda: P.ts(dtx[:, 0:nt, :], dtx[:, 0:nt, :], 0.0, ALU.max),
                lambda: P.tt(dtv.v(bi)[:, t0:t0 + nt, :], dtx[:, 0:nt, :], dty[:, 0:nt, :], ALU.add),
                lambda: [P.tt(dA.v(bi)[:, t0 + n_, :], dtv.v(bi)[:, t0 + n_, :], abc[:, l * 8:(l + 1) * 8], ALU.mult)
                         for n_ in range(nt)],
            ]
            for f_ in dtops[:DBG.get("d_dtops", 99)]:
                f_()
        P.scope_end()
        hf = P.sbuf("hstf", [128, 8, 64], F32)
        hz = P.sbuf("hstz", [128, 8, 128], BF16)

        def mk(dd):
            B = {}
            B["cum"] = [P.sbuf(f"cum{dd}{i}", [128, 128], F32) for i in range(2)]
            B["cumT"] = P.sbuf(f"cumT{dd}", [8, 128], F32)
            B["ncum"] = [P.sbuf(f"ncum{dd}{i}", [128, 4], F32) for i in range(2)]
            B["cbc"] = [P.sbuf(f"cbc{dd}{i}", [128, 512], F32) for i in range(2)]
            B["xtB"] = [P.sbuf(f"xtB{dd}{i}", [128, 4, 128], BF16) for i in range(2)]
            B["rhsE"] = P.sbuf(f"rhsEd{dd}", [128, 4, 128], F32)
            B["ecum"] = [P.sbuf(f"ecum{dd}{i}", [128, 4, 128], BF16) for i in range(2)]
            B["cend"] = [P.sbuf(f"cend{dd}{i}", [128, 8], F32) for i in range(2)]
            B["tmpall"] = [P.sbuf(f"tmpall{dd}{i}", [128, 4, 128], F32) for i in range(2)]
            B["dec"] = [P.sbuf(f"dec{dd}{i}", [128, 128], F32) for i in range(4)]
            B["ST"] = [P.sbuf(f"STd{dd}{i}", [128, 128], BF16) for i in range(4)]
            B["xdtz"] = [P.sbuf(f"xdtz{dd}{i}", [128, 128], BF16) for i in range(4)]
            B["Cd"] = [P.sbuf(f"Cd{dd}{i}", [128, 128], BF16) for i in range(4)]
            B["wend"] = [P.sbuf(f"wend{dd}{i}", [128, 1], F32) for i in range(4)]
            B["Bw"] = [P.sbuf(f"Bw{dd}{i}", [128, 128], BF16) for i in range(4)]
            B["yz"] = [P.sbuf(f"yz{dd}{i}", [128, 2, 128], F32) for i in range(2)]
            B["sqy"] = P.sbuf(f"sqy{dd}", [128, 2, 128], F32)
            B["rs"] = P.sbuf(f"rsd{dd}", [128, 128], F32)
            B["yb"] = [P.sbuf(f"ybd{dd}{i}", [128, 2, 128], BF16) for i in range(2)]
            B["psTt"] = P.psum(f"psTtd{dd}", [128, 4, 128], BF16)
            B["ps_bc"] = P.psum(f"ps_bcd{dd}", [128, 512], F32)
            B["psX"] = P.psum(f"psXd{dd}", [128, 512], F32)
            for t_ in B["cum"] + [B["rhsE"]] + B["xdtz"]:
                P.memset(t_.v(None).ap(t_.t.ap()), 0.0)
            return B

        ndir = DBG.get("d_dirs", 2)
        DB = [mk(dd) for dd in range(ndir)]
        psY = [P.psum(f"psYd{i}", [128, 512], F32) for i in range(2)]
        P.memset(hf[:, :, :], 0.0)
        P.memset(hz[:, :, :], 0.0)
        P.barrier()
        orders = [list(range(NT)), [1, 0] + list(range(NT - 1, 1, -1))]
        steps = [{c: i for i, c in enumerate(o)} for o in orders]

        def dir_gen(d):
            B = DB[d]
            last = 127 if d == 0 else 0
            psX, ps_bc, psTt = B["psX"], B["ps_bc"], B["psTt"]
            clist = orders[d][:DBG.get('d_chunks', NT)]

            def prologue(c, k1):
                bi = 0 if c < 2 else 1 + (c - 2) // 4
                cols = slice(c * 128, (c + 1) * 128)
                need_out = need_ctx or c >= 2
                cum_ = B["cum"][k1]
                P.mm(psX[:, 0:4], tri[:, d, :], dA.v(bi)[:, c, d * 4:(d + 1) * 4])
                P.copy(cum_[:, 0:4], psX[:, 0:4], eng="act")
                P.ts(B["ncum"][k1][:, :], psX[:, 0:4], -1.0, ALU.mult)
                P.tr(psX[:, 128:256], cum_[:, :], ident_f[:, :])
                P.copy(B["cumT"][0:4, :], psX[0:4, 128:256], eng="act")
                P.tt(B["rhsE"][0:4, :, :], B["cumT"].raw(0, 4, 0, [[0, 4], [1, 128]]), eye8x[0:4, 0:4, :], ALU.mult)
                P.mm(ps_bc[:, :], ones_f[:, :], B["rhsE"].raw(0, 128, 0, [[1, 512]]))
                ce = B["cend"][k1]
                P.copy(B["cbc"][k1][:, :], ps_bc[:, :], eng="act")
                P.copy(ce[:, 0:4], ps_bc.raw(0, 128, last, [[128, 4]]))
                P.act(ce[:, 4:8], ce[:, 0:4], AF.Exp)
                P.tr(psTt[:, 0, :], xT.v(bi)[:, 0, cols], ident_b[:, :])
                P.tr(psTt[:, 1, :], xT.v(bi)[:, 1, cols], ident_b[:, :])
                P.tr(psTt[:, 2, :], BT.v(bi)[:, 0, cols], ident_b[:, :])
                P.tr(psTt[:, 3, :], BT.v(bi)[:, 1, cols], ident_b[:, :])
                P.copy(B["xtB"][k1][:, :, :], psTt[:, :, :], eng="act")
                if need_out:
                    P.act(B["ecum"][k1][:, :, :], B["cbc"][k1].raw(0, 128, 0, [[128, 4], [1, 128]]), AF.Exp)
                    P.tt(B["tmpall"][k1][:, :, :], negm.raw(0, 128, d * 128, [[0, 4], [1, 128]]),
                         B["cbc"][k1].raw(0, 128, 0, [[128, 4], [1, 128]]), ALU.add)

            if clist:
                prologue(clist[0], 0)
            for ci_, c in enumerate(clist):
                bi = 0 if c < 2 else 1 + (c - 2) // 4
                cols = slice(c * 128, (c + 1) * 128)
                need_out = need_ctx or c >= 2
                k1 = ci_ % 2
                second = (ndir == 2) and steps[1 - d][c] < ci_
                if ndir == 1:
                    second = False
                cum_ = B["cum"][k1]
                ce = B["cend"][k1]
                if ci_ + 1 < len(clist):
                    prologue(clist[ci_ + 1], 1 - k1)
                yz_ = B["yz"][k1]

                def pair_gen(p):
                    bank = psY[p]
                    yo = d * 256
                    gcol = slice(256 + p * 128, 384 + p * 128)
                    heads = [(2 * p + q, d * 4 + 2 * p + q, slice(q * 64, q * 64 + 64)) for q in range(2)]
                    xdtz, wend, Bw, dec, ST, Cd = B["xdtz"], B["wend"], B["Bw"], B["dec"], B["ST"], B["Cd"]
                    if need_out:
                        P.mm(psX[:, gcol], BT.v(bi)[:, p, cols], CT.v(bi)[:, p, cols])
                    for h, j, rr in heads:
                        P.ts(xdtz[h][:, rr], B["xtB"][k1][:, p, rr], dtv.v(bi)[:, c, j:j + 1], ALU.mult)
                        P.act(wend[h][:, :], cum_[:, h:h + 1], AF.Exp, scale=-1.0, bias=ce[:, h:h + 1])
                    yield
                    for h, j, rr in heads:
                        if need_out:
                            P.tt(Cd[h][:, :], CT.v(bi)[:, p, cols], B["ecum"][k1][:, h, :], ALU.mult, eng="pool")
                        P.ts(Bw[h][:, :], B["xtB"][k1][:, 2 + p, :], wend[h][:, 0:1], ALU.mult)
                    yield
                    if need_out:
                        for h, j, rr in heads:
                            P.act(dec[h][:, :], B["tmpall"][k1][:, h, :], AF.Exp, bias=B["ncum"][k1][:, h:h + 1])
                        yield
                        for h, j, rr in heads:
                            P.tt(ST[h][:, :], psX[:, gcol], dec[h][:, :], ALU.mult)
                        yield
                        (hA, jA, _), (hB, jB, _) = heads
                        yv = bank[:, yo:yo + 128]
                        P.mm(yv, xdtz[hA][:, :], ST[hA][:, :], start=True, stop=False)
                        P.mm(yv, hz.v(jA)[:, jA, :], Cd[hA][:, :], start=False, stop=False)
                        P.mm(yv, xdtz[hB][:, :], ST[hB][:, :], start=False, stop=False)
                        P.mm(yv, hz.v(jB)[:, jB, :], Cd[hB][:, :], start=False, stop=True)
                        yield
                        if not second:
                            P.copy(yF.v(c)[:, p, cols], yv, eng="act")
                        else:
                            P.tt(yz_[:, p, :], yv, yF.v(c)[:, p, cols], ALU.add)
                    for q, (h, j, rr) in enumerate(heads):
                        P.mm(bank[:, yo + 128 + q * 64:yo + 192 + q * 64], Bw[h][:, :], xdtz[h][:, rr])
                    yield
                    for q, (h, j, rr) in enumerate(heads):
                        P.stt(hf.v(j)[:, j, :], hf.v(j)[:, j, :], ce[:, 4 + h:5 + h],
                              bank[:, yo + 128 + q * 64:yo + 192 + q * 64], ALU.mult, ALU.add)
                    yield
                    for h, j, rr in heads:
                        P.copy(hz.v(j)[:, j, rr], hf.v(j)[:, j, :], eng="act")
                    if need_out and second:
                        P.stt(yz_[:, p, :], xT.v(bi)[:, p, cols], pcol[:, pc0 + PC_DSK + p:pc0 + PC_DSK + p + 1],
                              yz_[:, p, :], ALU.mult, ALU.add)
                        yield
                        P.tt(yz_[:, p, :], yz_[:, p, :], szT.v(bi)[:, p, cols], ALU.mult)

                gens = [pair_gen(p) for p in range(2)]
                while gens:
                    nxt = []
                    for g_ in gens:
                        try:
                            next(g_)
                            nxt.append(g_)
                        except StopIteration:
                            pass
                    gens = nxt
                    yield
                if need_out and second:
                    sqy, rs = B["sqy"], B["rs"]
                    P.act(sqy[:, :, :], yz_[:, :, :], AF.Square)
                    pSS = psX[:, 384:512]
                    P.mm(pSS, ones_f[:, :], sqy[:, 0, :], start=True, stop=False)
                    P.mm(pSS, ones_f[:, :], sqy[:, 1, :], start=False, stop=True)
                    P.act(rs[:, :], pSS, AF.Ln, scale=1.0 / 256, bias=EPS)
                    P.act(rs[:, :], rs[:, :], AF.Exp, scale=-0.5)
                    y_ = B["yb"][k1]
                    for pair in range(2):
                        P.stt(y_[:, pair, :], yz_[:, pair, :],
                              pcol[:, pc0 + PC_GSSM + pair:pc0 + PC_GSSM + pair + 1], rs[:, :], ALU.mult, ALU.mult)
                    P.dma("sp", yTs.v((c, 6)).ap(yTs.t.ap()[768:1024, cols].rearrange("(a p) t -> p a t", p=128)),
                          y_[:, :, :])
                yield

        dgens = [dir_gen(d) for d in range(ndir)]
        while dgens:
            nxt = []
            for g_ in dgens:
                try:
                    next(g_)
                    nxt.append(g_)
                except StopIteration:
                    pass
            dgens = nxt
        P.scope_end()

    def phase_oproj(l, need_ctx):
        P.scope_begin()
        wo = P.sbuf("wo", [128, 8, D], BF16)
        wcast(wo, w_out, l, 0, D)
        yTb = [P.sbuf(f"yTb{i}", [128, 8, 512], BF16) for i in range(2)]
        xb = [P.sbuf(f"xbo{i}", [128, 4, D], F32) for i in range(2)]
        junk = P.sbuf("junko", [128, D], BF16)
        xh = [P.sbuf(f"xho{i}", [128, D], F32) for i in range(2)]
        hb = [P.sbuf(f"hbo{i}", [128, 8, 512], BF16) for i in range(2)]
        ss = [P.sbuf(f"sso{i}", [128, 8], F32) for i in range(2)]
        t1 = [P.sbuf(f"t1o{i}", [128, 512], F32) for i in range(2)]
        psO = [P.psum(f"psOo{i}", [128, 512], F32) for i in range(2)]
        psT = [P.psum(f"psTo{i}", [128, D], F32) for i in range(2)]
        kk = [0]

        def oproj(bi):
            t0, nt = BLOCKS[bi]
            W = nt * 128
            ci = 1 if bi == 0 else 0
            X = xb[bi % 2]
            yb_ = yTb[bi % 2]
            P.dma("sp", X[:, 0:nt, :], x_src(l, bi))
            P.dma("sp", yb_[:, :, 0:W],
                  yTs.v(("o", bi)).ap(yTs.t.ap().rearrange("(kc p) t -> p kc t", p=128)[:, :, t0 * 128:t0 * 128 + W]))
            for n in range(nt):
                for half in range(2):
                    ps = psO[kk[0] % 2]
                    t_ = t1[kk[0] % 2]
                    kk[0] += 1
                    for fc in range(8):
                        P.mm(ps[:, :], yb_[:, fc, n * 128:(n + 1) * 128], wo[:, fc, half * 512:(half + 1) * 512],
                             start=(fc == 0), stop=(fc == 7))
                    P.tt(t_[:, :], ps[:, :], gtb[:, 0, ci, half * 512:(half + 1) * 512], ALU.mult)
                    P.tt(X[:, n, half * 512:(half + 1) * 512], X[:, n, half * 512:(half + 1) * 512], t_[:, :], ALU.add,
                         eng="pool")
            P.dma("sp", Xs.v(bi).ap(Xs.t.ap()[t0 * 128:(t0 + nt) * 128, :].rearrange("(n p) d -> p n d", p=128)),
                  X[:, 0:nt, :])

        def onorm(bi):
            t0, nt = BLOCKS[bi]
            W = nt * 128
            ci = 1 if bi == 0 else 0
            norm_block(xb[bi % 2], nt, ci, 2, hb[bi % 2], ss[bi % 2], xh, psT, junk)
            P.dma("sp", hTs.v(bi).ap(hT_view(t0 * 128, t0 * 128 + W)), hb[bi % 2][:, :, 0:W])

        blist = [bi for bi in range(len(BLOCKS)) if not (bi == 0 and not need_ctx)]
        oproj(blist[0])
        for i_, bi in enumerate(blist):
            if i_ + 1 < len(blist):
                oproj(blist[i_ + 1])
            onorm(bi)
        P.scope_end()

    def load_ffn_w(l, fh):
        w1 = P.sbuf("w1", [128, 8, 2048], BF16)
        w2 = P.sbuf("w2", [128, 16, D], BF16)
        wcast(w1, w_ff1, l, fh * 2048, (fh + 1) * 2048)
        src2 = w_ff2.t.ap()[l].rearrange("(fc p) n -> p fc n", p=128)
        for fc0 in range(0, 16, 4):
            for hh in range(2):
                P.dma("pool", w2[:, fc0:fc0 + 4, hh * 512:(hh + 1) * 512],
                      w_ff2.v(l).ap(src2[:, fh * 16 + fc0:fh * 16 + fc0 + 4, hh * 512:(hh + 1) * 512]))
        return w1, w2

    def phase_oproj_ffn0(l, need_ctx):
        P.scope_begin()
        w = load_ffn_w(l, 0)
        phase_oproj(l, need_ctx)
        phase_ffn(l, 0, need_ctx, False, w)
        P.scope_end()

    def phase_ffn(l, fh, need_ctx, final, w=None):
        P.scope_begin()
        w1, w2 = w if w is not None else load_ffn_w(l, fh)
        hb = [P.sbuf(f"hbf{i}", [128, 8, 512], BF16) for i in range(2)]
        xb = [P.sbuf(f"xbf{i}", [128, 4, D], F32) for i in range(2)]
        aT = P.sbuf("aT", [128, 16, 512], BF16)
        r = [P.sbuf(f"r{i}", [128, 512], F32) for i in range(2)]
        t1 = [P.sbuf(f"t1f{i}", [128, 512], F32) for i in range(2)]
        gf = P.sbuf("gf", [128, D], F32)
        junk = P.sbuf("junkf", [128, D], BF16)
        ssf = P.sbuf("ssf", [128, 8], F32)
        ps1 = [P.psum(f"ps1{i}", [128, 512], F32) for i in range(3)]
        ps2 = [P.psum(f"ps2{i}", [128, 512], F32) for i in range(3)]
        if final:
            P.dma("sp", gf[:, :], gfin_d[:, :])
        k1 = 0
        k2 = 0
        for bi, (t0, nt) in enumerate(BLOCKS):
            if bi == 0 and not need_ctx:
                continue
            W = nt * 128
            ci = 1 if bi == 0 else 0
            h_ = hb[bi % 2]
            X = xb[bi % 2]
            P.dma("sp", h_[:, :, 0:W], hTs.v(bi).ap(hT_view(t0 * 128, t0 * 128 + W)))
            xdr = Xs.v(bi).ap(Xs.t.ap()[t0 * 128:(t0 + nt) * 128, :].rearrange("(n p) d -> p n d", p=128))
            P.dma("sp", X[:, 0:nt, :], xdr)
            for fc in range(16):
                ps = ps1[k1 % 3]
                r_ = r[k1 % 2]
                k1 += 1
                for kc in range(8):
                    P.mm(ps[:, 0:W], w1[:, kc, fc * 128:(fc + 1) * 128], h_[:, kc, 0:W], start=(kc == 0), stop=(kc == 7))
                P.act(r_[:, 0:W], ps[:, 0:W], AF.Relu)
                P.tt(aT[:, fc, 0:W], r_[:, 0:W], r_[:, 0:W], ALU.mult)
            for n in range(nt):
                for half in range(2):
                    ps = ps2[k2 % 3]
                    t_ = t1[k2 % 2]
                    k2 += 1
                    for fc in range(16):
                        P.mm(ps[:, :], aT[:, fc, n * 128:(n + 1) * 128], w2[:, fc, half * 512:(half + 1) * 512],
                             start=(fc == 0), stop=(fc == 15))
                    P.tt(t_[:, :], ps[:, :], gtb[:, 1, ci, half * 512:(half + 1) * 512], ALU.mult)
                    P.tt(X[:, n, half * 512:(half + 1) * 512], X[:, n, half * 512:(half + 1) * 512], t_[:, :], ALU.add,
                         eng="pool")
            if final:
                for n in range(nt):
                    P.act(junk[:, :], X[:, n, :], AF.Square, accum_out=ssf[:, n:n + 1])
                P.act(ssf[:, 4:4 + nt], ssf[:, 0:nt], AF.Ln, scale=1.0 / D, bias=EPS)
                P.act(ssf[:, 4:4 + nt], ssf[:, 4:4 + nt], AF.Exp, scale=-0.5)
                for n in range(nt):
                    P.stt(X[:, n, :], X[:, n, :], ssf[:, 4 + n:5 + n], gf[:, :], ALU.mult, ALU.mult)
                l0 = (t0 - 2) * 128
                P.dma("sp", out_d.v(bi).ap(out_d.t.ap()[l0:l0 + W, :].rearrange("(n p) d -> p n d", p=128)),
                      X[:, 0:nt, :])
            else:
                P.dma("sp", xdr, X[:, 0:nt, :])
        P.scope_end()

    P.barrier()
    done = False
    for l in range(nlayers):
        need_ctx = l < L - 1
        steps = [("M", lambda: phase_mod(l)), ("N", lambda: phase_norm(l)),
                 ("A", lambda: phase_attn(l, 0, need_ctx)), ("B", lambda: phase_attn(l, 1, need_ctx)),
                 ("C", lambda: phase_mlstm(l, need_ctx)), ("D", lambda: phase_ssd(l, need_ctx)),
                 ("F0", lambda: phase_oproj_ffn0(l, need_ctx)),
                 ("F1", lambda: phase_ffn(l, 1, need_ctx, l == L - 1))]
        for name, fn in steps:
            fn()
            if stop == (l, name):
                done = True
                break
        if done:
            break
    P.finish()
    return nc, P


def _consts():
    s = np.arange(128)[:, None]
    t = np.arange(128)[None, :]
    negm = np.zeros((128, 2, 128), np.float32)
    negm[:, 0, :] = np.where(s <= t, 0.0, -1e9)
    negm[:, 1, :] = np.where(s >= t, 0.0, -1e9)
    tri = (negm == 0).astype(np.float32)
    sel = np.zeros((128, 128), np.float32)
    sel[64, :] = 1.0
    oblk = np.zeros((128, 128), np.float32)
    oblk[:64, :64] = 1.0
    oblk[64:, 64:] = 1.0
    eye8x = np.zeros((8, 8, 128), np.float32)
    for k in range(8):
        eye8x[k, k, :] = 1.0
    rows = NLAT // 64
    row = np.repeat(np.arange(rows, dtype=np.float32), 64)
    col = np.tile(np.arange(64, dtype=np.float32), rows)
    inv = (10000.0 ** (-np.arange(0, 32, 2, dtype=np.float32) / 32)).astype(np.float32)
    ang = np.concatenate([row[:, None] * inv, col[:, None] * inv], axis=-1).astype(np.float32)
    cos = np.concatenate([np.ones((NCTX, 32), np.float32), np.cos(ang).astype(np.float32)], 0)
    sin = np.concatenate([np.zeros((NCTX, 32), np.float32), np.sin(ang).astype(np.float32)], 0)
    rope = np.concatenate([cos, sin], -1).reshape(NT, 128, 64).transpose(1, 0, 2).reshape(128, NT * 64)
    wm = np.zeros((128, 6, 512), np.float32)
    q = np.arange(512)[None, :]
    for r in range(-1, 5):
        kpos = r * 128 + np.arange(128)[:, None]
        wm[:, r + 1, :] = np.where(np.abs(q - kpos) <= 128, 0.0, NEG)
    return dict(ident_f=np.eye(128, dtype=np.float32), negm=negm.reshape(128, 256), tri=tri.reshape(128, 256),
                sel65=sel, onesblk=oblk, eye8x=eye8x.reshape(8, 1024), rope=np.ascontiguousarray(rope),
                wmask=wm.reshape(128, 6 * 512))


def _colform(v):
    return np.ascontiguousarray(np.asarray(v, np.float32).reshape(-1, 128).T)


def _prep_shared(inp):
    f = lambda a: np.ascontiguousarray(np.asarray(a, np.float32))
    pcol = np.zeros((128, L * PC_N), np.float32)
    pbc = np.zeros((128, L * PB_N), np.float32)
    for l in range(L):
        o = l * PC_N
        pcol[:, o + PC_G1:o + PC_G1 + 8] = _colform(inp["g_norm1"][l])
        pcol[:, o + PC_G2:o + PC_G2 + 8] = _colform(inp["g_norm2"][l])
        for j in range(3):
            pcol[:, o + PC_CW + j * 6:o + PC_CW + j * 6 + 6] = _colform(inp["conv_w"][l][j])
        pcol[:, o + PC_CB:o + PC_CB + 6] = _colform(inp["conv_b"][l])
        pcol[:, o + PC_GML:o + PC_GML + 2] = _colform(inp["g_mlstm"][l])
        pcol[:, o + PC_GSSM:o + PC_GSSM + 2] = _colform(inp["g_ssm"][l])
        pcol[:, o + PC_DSK:o + PC_DSK + 2] = _colform(np.repeat(np.asarray(inp["d_skip"][l], np.float32), 64))
        o = l * PB_N
        pbc[:, o + PB_GQ:o + PB_GQ + 64] = np.asarray(inp["g_q_b"][l], np.float32)[None, :]
        pbc[:, o + PB_GK:o + PB_GK + 64] = np.asarray(inp["g_k_b"][l], np.float32)[None, :]
        pbc[:, o + PB_SINK:o + PB_SINK + 4] = np.asarray(inp["sink_a"][l], np.float32)[None, :]
        bi_ = np.asarray(inp["b_igate"][l], np.float32)
        bf_ = np.asarray(inp["b_fgate"][l], np.float32)
        pbc[:, o + PB_GB:o + PB_GB + 16] = np.concatenate([bi_[0], bf_[0], bi_[1], bf_[1]])[None, :]
        pbc[:, o + PB_ALOG:o + PB_ALOG + 8] = np.asarray(inp["a_log"][l], np.float32).reshape(-1)[None, :]
        pbc[:, o + PB_DTB:o + PB_DTB + 8] = np.asarray(inp["dt_bias"][l], np.float32).reshape(-1)[None, :]
    b_ada = np.asarray(inp["b_ada"], np.float32)
    bada_col = np.concatenate([_colform(b_ada[l]) for l in range(L)], axis=1)
    bada_gt = np.concatenate([np.concatenate([b_ada[l, 2 * D:3 * D], b_ada[l, 5 * D:6 * D]]) for l in range(L)])
    bada_gt = np.ascontiguousarray(np.broadcast_to(bada_gt[None, :], (128, L * 2 * D)))
    gfin = np.ascontiguousarray(np.broadcast_to(np.asarray(inp["g_final"], np.float32)[None, :], (128, D)))
    sh = dict(w_ada=f(inp["w_ada"]), bada_col=np.ascontiguousarray(bada_col), bada_gt=bada_gt, w_in=f(inp["w_in"]),
              w_out=f(inp["w_out"]), w_ff1=f(inp["w_ff1"]), w_ff2=f(inp["w_ff2"]), pcol=pcol, pbc=pbc, gfin=gfin)
    sh.update(_consts())
    return sh


def make_in_maps(inp, cores):
    sh = _prep_shared(inp)
    maps = []
    for b in cores:
        m = dict(sh)
        m["xin"] = np.ascontiguousarray(np.concatenate([np.asarray(inp["ctx"][b], np.float32),
                                                        np.asarray(inp["x"][b], np.float32)], 0))
        m["ccol"] = np.ascontiguousarray(np.concatenate([_colform(inp["c"][b]), _colform(inp["c_ctx"])], 1))
        maps.append(m)
    return maps


def kernel(**inputs):
    nc, _ = build_program()
    maps = make_in_maps(inputs, list(range(8)))
    res = run_bass_kernel_spmd(nc, maps, core_ids=list(range(8)))
    return np.stack([np.asarray(r["out"], np.float32) for r in res.results], 0)
```

```python
import numpy as np
import ml_dtypes
import concourse.bass as bass
import concourse.mybir as mybir
from concourse.bass_utils import run_bass_kernel_spmd

F32 = mybir.dt.float32
BF16 = mybir.dt.bfloat16
AF = mybir.ActivationFunctionType
ALU = mybir.AluOpType
AX = mybir.AxisListType

EPOCH = 12000
STRICT = False


class Reg:
    __slots__ = ("w", "r", "psum")

    def __init__(self, psum=False):
        self.w = None
        self.r = {}
        self.psum = psum


class View:
    __slots__ = ("reg", "ap")

    def __init__(self, reg, ap):
        self.reg = reg
        self.ap = ap


class _Keyed:
    def __init__(self, tile, key):
        self.tile = tile
        self.key = key

    def _reg(self):
        t = self.tile
        reg = t.regs.get(self.key)
        if reg is None:
            reg = t.regs[self.key] = Reg(t if t.is_psum else None)
        return reg

    def __getitem__(self, idx):
        return View(self._reg(), self.tile.t[idx])

    def ap(self, ap):
        return View(self._reg(), ap)


class Tile:
    def __init__(self, t, is_psum=False):
        self.t = t
        self.is_psum = is_psum
        self.bank_readers = {}
        self.regs = {}
        self.F = int(np.prod(t.shape[1:]))

    def v(self, key):
        return _Keyed(self, key)

    def __getitem__(self, idx):
        return _Keyed(self, None)[idx]

    def raw(self, p0, npart, off, dims, key=None):
        ap = bass.AP(self.t, p0 * self.F + off, [[self.F, npart]] + [list(d) for d in dims])
        return _Keyed(self, key).ap(ap)


class Prog:
    def __init__(self, nc):
        self.nc = nc
        self.eng = {"pe": nc.tensor, "act": nc.scalar, "dve": nc.vector, "pool": nc.gpsimd, "sp": nc.sync}
        self.seq = {e: 0 for e in self.eng}
        self.sems = {e: [] for e in self.eng}
        self.seen = {e: {} for e in self.eng}
        self.dma_pool = {}
        self._cms = []
        self._scopes = []
        self.ninst = 0
        for q, n in (("sp", 16), ("pool", 8), ("act", 4)):
            sl = []
            for i in range(n):
                sl.append([self._sem(f"d_{q}_{i}"), 0])
            self.dma_pool[q] = [sl, 0]

    def _sem(self, name):
        cm = self.nc.semaphore(name)
        s = cm.__enter__()
        self._cms.append(cm)
        return s

    def _alloc(self, cm, is_psum=False):
        t = cm.__enter__()
        if self._scopes:
            self._scopes[-1].append(cm)
        else:
            self._cms.append(cm)
        return Tile(t, is_psum)

    def sbuf(self, name, shape, dtype):
        return self._alloc(self.nc.sbuf_tensor(name + f"_{self.ninst}", list(shape), dtype))

    def psum(self, name, shape, dtype=F32):
        return self._alloc(self.nc.psum_tensor(name + f"_{self.ninst}", list(shape), dtype), True)

    def dram(self, name, shape, dtype, kind="Internal"):
        t = self.nc.dram_tensor(name, list(shape), dtype, kind=kind)
        return Tile(t)

    def scope_begin(self):
        self._scopes.append([])

    def scope_end(self):
        self.barrier()
        cms = self._scopes.pop()
        for cm in reversed(cms):
            cm.__exit__(None, None, None)

    def _esem(self, e, seq):
        ep = (seq - 1) // EPOCH
        while len(self.sems[e]) <= ep:
            self.sems[e].append(self._sem(f"s_{e}_{len(self.sems[e])}"))
        return self.sems[e][ep], (seq - 1) % EPOCH + 1

    def _need(self, e, dep):
        if dep[0] == "e":
            _, e2, seq = dep
            key = ("e", e2)
            if self.seen[e].get(key, 0) >= seq:
                return None
            self.seen[e][key] = seq
            return self._esem(e2, seq)
        _, q, si, val = dep
        key = ("d", q, si)
        if self.seen[e].get(key, 0) >= val:
            return None
        self.seen[e][key] = val
        return (self.dma_pool[q][0][si][0], val)

    def _wait(self, e, dep):
        n = self._need(e, dep)
        if n is not None:
            self.eng[e].wait_ge(n[0], n[1])
            self.ninst += 1

    def _deps(self, e, outs, ins):
        deps = []
        for v in ins:
            if v.reg.w is not None:
                deps.append(v.reg.w)
            if v.reg.psum is not None:
                for e2, val in v.reg.psum.bank_readers.items():
                    if e2 != e:
                        deps.append(("e", e2, val))
        for v in outs:
            w = v.reg.w
            if w is not None:
                if not (w[0] == "e" and w[1] == e and (e == "pe" or not STRICT)):
                    deps.append(w)
            for k, val in v.reg.r.items():
                if k[0] == "e":
                    if k[1] == e and not STRICT:
                        continue
                    deps.append(("e", k[1], val))
                else:
                    deps.append(("d", k[1], k[2], val))
        best = {}
        for d in deps:
            k = d[:2] if d[0] == "e" else d[:3]
            if k not in best or d[-1] > best[k][-1]:
                best[k] = d
        needs = []
        for d in best.values():
            n = self._need(e, d)
            if n is not None:
                needs.append(n)
        return needs

    def op(self, e, fn, outs, ins, embed=True):
        self.nops = getattr(self, 'nops', 0) + 1
        if self.nops > DBG.get('maxops', 10 ** 9):
            return None
        needs = self._deps(e, outs, ins)
        emb = None
        if embed and e != "pe" and needs:
            emb = needs.pop()
        for sem, val in needs:
            self.eng[e].wait_ge(sem, val)
            self.ninst += 1
        inst = fn()
        if emb is not None:
            inst._wait_ge(emb[0], emb[1])
        self.seq[e] += 1
        seq = self.seq[e]
        sem, val = self._esem(e, seq)
        inst.then_inc(sem, 1)
        self.ninst += 1
        me = ("e", e, seq)
        for v in ins:
            v.reg.r[("e", e)] = seq
            if v.reg.psum is not None:
                v.reg.psum.bank_readers[e] = seq
        for v in outs:
            v.reg.w = me
            v.reg.r = {}
        return inst

    def dma(self, q, out, in_, **kw):
        e = q
        if getattr(self, 'nops', 0) > DBG.get('maxops', 10 ** 9):
            return None
        for sem_, val_ in self._deps(e, [out], [in_]):
            self.eng[e].wait_ge(sem_, val_)
            self.ninst += 1
        pool = self.dma_pool[q]
        si = pool[1] % len(pool[0])
        pool[1] += 1
        slot = pool[0][si]
        if slot[1] > 0:
            self._wait(e, ("d", q, si, slot[1]))
        slot[1] += 16
        inst = self.eng[e].dma_start(out=out.ap, in_=in_.ap, **kw)
        inst.then_inc(slot[0], 16)
        self.ninst += 1
        in_.reg.r[("d", q, si)] = slot[1]
        out.reg.w = ("d", q, si, slot[1])
        out.reg.r = {}
        return inst

    def barrier(self):
        for e in self.eng:
            for q, (sl, _) in self.dma_pool.items():
                for si, (sem, val) in enumerate(sl):
                    if val > 0:
                        self._wait(e, ("d", q, si, val))
            for e2 in self.eng:
                if self.seq[e2] > 0:
                    self._wait(e, ("e", e2, self.seq[e2]))

    def finish(self):
        self.barrier()

    def mm(self, out, lhsT, rhs, start=True, stop=True):
        return self.op("pe", lambda: self.nc.tensor.matmul(out.ap, lhsT.ap, rhs.ap, start=start, stop=stop),
                       [out], [lhsT, rhs])

    def tr(self, out, in_, ident):
        return self.op("pe", lambda: self.nc.tensor.transpose(out.ap, in_.ap, ident.ap), [out], [in_, ident])

    def act(self, out, in_, func, bias=None, scale=1.0, accum_out=None):
        ins = [in_]
        kw = {}
        if bias is not None:
            if isinstance(bias, View):
                ins.append(bias)
                kw["bias"] = bias.ap
            else:
                kw["bias"] = bias
        if isinstance(scale, View):
            ins.append(scale)
            kw["scale"] = scale.ap
        else:
            kw["scale"] = scale
        outs = [out]
        if accum_out is not None:
            outs.append(accum_out)
            kw["accum_out"] = accum_out.ap
        return self.op("act", lambda: self.nc.scalar.activation(out=out.ap, in_=in_.ap, func=func, **kw), outs, ins,
                       embed=(accum_out is None))

    def tt(self, out, in0, in1, op, eng="dve"):
        E = self.eng[eng]
        return self.op(eng, lambda: E.tensor_tensor(out=out.ap, in0=in0.ap, in1=in1.ap, op=op), [out], [in0, in1])

    def ts(self, out, in0, s1, op0, s2=None, op1=None, eng="dve"):
        E = self.eng[eng]
        ins = [in0]
        a1, a2 = s1, s2
        if isinstance(s1, View):
            ins.append(s1)
            a1 = s1.ap
        if isinstance(s2, View):
            ins.append(s2)
            a2 = s2.ap
        kw = {}
        if op1 is not None:
            kw["op1"] = op1
        return self.op(eng, lambda: E.tensor_scalar(out=out.ap, in0=in0.ap, scalar1=a1, scalar2=a2, op0=op0, **kw),
                       [out], ins)

    def stt(self, out, in0, s, in1, op0, op1):
        E = self.nc.vector
        ins = [in0, in1]
        a = s
        if isinstance(s, View):
            ins.append(s)
            a = s.ap
        return self.op("dve", lambda: E.scalar_tensor_tensor(out=out.ap, in0=in0.ap, scalar=a, in1=in1.ap,
                                                              op0=op0, op1=op1), [out], ins)

    def copy(self, out, in_, eng="dve"):
        if eng == "act":
            return self.op("act", lambda: self.nc.scalar.copy(out=out.ap, in_=in_.ap), [out], [in_])
        E = self.eng[eng]
        return self.op(eng, lambda: E.tensor_copy(out=out.ap, in_=in_.ap), [out], [in_])

    def memset(self, out, val, eng="dve"):
        E = self.eng[eng]
        return self.op(eng, lambda: E.memset(out.ap, val), [out], [])

    def recip(self, out, in_):
        return self.op("dve", lambda: self.nc.vector.reciprocal(out=out.ap, in_=in_.ap), [out], [in_])

    def reduce(self, out, in_, op, axis=AX.X):
        return self.op("dve", lambda: self.nc.vector.tensor_reduce(out=out.ap, in_=in_.ap, axis=axis, op=op),
                       [out], [in_])

    def scan(self, out, d0, d1, initial, op0, op1):
        ins = [d0, d1]
        a = initial
        if isinstance(initial, View):
            ins.append(initial)
            a = initial.ap
        return self.op("dve", lambda: self.nc.vector.tensor_tensor_scan(out=out.ap, data0=d0.ap, data1=d1.ap,
                                                                        initial=a, op0=op0, op1=op1), [out], ins)


L = 2
D = 1024
NCTX = 256
NLAT = 4096
T = NCTX + NLAT
NT = T // 128
NIN = 3096
DFF = 4096
EPS = 1e-6
BLOCKS = [(0, 2)] + [(2 + 4 * j, 4) for j in range(8)]
NEG = -30000.0
DBG = {}

PC_G1, PC_G2, PC_CW, PC_CB, PC_GML, PC_GSSM, PC_DSK, PC_N = 0, 8, 16, 34, 40, 42, 44, 46
PB_GQ, PB_GK, PB_SINK, PB_GB, PB_ALOG, PB_DTB, PB_N = 0, 64, 128, 132, 148, 156, 164


def build_program(nlayers=L, stop=None, dbg=False):
    nc = bass.Bass("TRN2", target_bir_lowering=False)
    P = Prog(nc)
    EI = "ExternalInput"
    xin = P.dram("xin", [T, D], F32, EI)
    ccol_d = P.dram("ccol", [128, 16], F32, EI)
    w_ada = P.dram("w_ada", [L, D, 6 * D], F32, EI)
    badac_d = P.dram("bada_col", [128, L * 48], F32, EI)
    badag_d = P.dram("bada_gt", [128, L * 2 * D], F32, EI)
    w_in = P.dram("w_in", [L, D, NIN], F32, EI)
    w_out = P.dram("w_out", [L, D, D], F32, EI)
    w_ff1 = P.dram("w_ff1", [L, D, DFF], F32, EI)
    w_ff2 = P.dram("w_ff2", [L, DFF, D], F32, EI)
    pcol_d = P.dram("pcol", [128, L * PC_N], F32, EI)
    pbc_d = P.dram("pbc", [128, L * PB_N], F32, EI)
    gfin_d = P.dram("gfin", [128, D], F32, EI)
    identf_d = P.dram("ident_f", [128, 128], F32, EI)
    negm_d = P.dram("negm", [128, 2 * 128], F32, EI)
    tri_d = P.dram("tri", [128, 2 * 128], F32, EI)
    sel_d = P.dram("sel65", [128, 128], F32, EI)
    oblk_d = P.dram("onesblk", [128, 128], F32, EI)
    eye_d = P.dram("eye8x", [8, 8 * 128], F32, EI)
    rope_d = P.dram("rope", [128, NT * 64], F32, EI)
    wmask_d = P.dram("wmask", [128, 6 * 512], F32, EI)
    out_d = P.dram("out", [NLAT, D], F32, "ExternalOutput")
    okind = "ExternalOutput" if dbg else "Internal"
    Xs = P.dram("Xs", [T, D], F32, okind)
    hTs = P.dram("hTs", [D, T], BF16, okind)
    yTs = P.dram("yTs", [D, T], BF16, okind)

    ident_f = P.sbuf("ident_f", [128, 128], F32)
    ident_b = P.sbuf("ident_b", [128, 128], BF16)
    ones_f = P.sbuf("ones_f", [128, 128], F32)
    zeros_f = P.sbuf("zeros_f", [128, 128], F32)
    negm = P.sbuf("negm", [128, 2, 128], F32)
    tri = P.sbuf("tri", [128, 2, 128], F32)
    sel65 = P.sbuf("sel65", [128, 128], F32)
    onesblk = P.sbuf("onesblk", [128, 128], F32)
    eye8x = P.sbuf("eye8x", [8, 8, 128], F32)
    pcol = P.sbuf("pcol", [128, L * PC_N], F32)
    pbc = P.sbuf("pbc", [128, L * PB_N], F32)
    ccol = P.sbuf("ccol", [128, 16], F32)
    badac = P.sbuf("badac", [128, L * 48], F32)
    modv = P.sbuf("modv", [128, 4, 8, 2], F32)
    gtb = P.sbuf("gtb", [128, 2, 2, D], F32)
    esink = P.sbuf("esink", [128, L * 4], F32)
    abc = P.sbuf("abc", [128, L * 8], F32)

    P.dma("sp", ident_f[:, :], identf_d[:, :])
    P.dma("pool", ident_b[:, :], identf_d[:, :])
    P.dma("sp", negm[:, :, :], negm_d.raw(0, 128, 0, [[128, 2], [1, 128]]))
    P.dma("sp", tri[:, :, :], tri_d.raw(0, 128, 0, [[128, 2], [1, 128]]))
    P.dma("sp", sel65[:, :], sel_d[:, :])
    P.dma("sp", onesblk[:, :], oblk_d[:, :])
    P.dma("sp", eye8x[:, :, :], eye_d.raw(0, 8, 0, [[128, 8], [1, 128]]))
    P.dma("sp", pcol[:, :], pcol_d[:, :])
    P.dma("sp", pbc[:, :], pbc_d[:, :])
    P.dma("sp", ccol[:, :], ccol_d[:, :])
    P.dma("sp", badac[:, :], badac_d[:, :])
    P.memset(ones_f[:, :], 1.0)
    P.memset(zeros_f[:, :], 0.0)
    for l in range(L):
        P.act(esink[:, l * 4:(l + 1) * 4], pbc[:, l * PB_N + PB_SINK:l * PB_N + PB_SINK + 4], AF.Exp)
        P.act(abc[:, l * 8:(l + 1) * 8], pbc[:, l * PB_N + PB_ALOG:l * PB_N + PB_ALOG + 8], AF.Exp)
        P.ts(abc[:, l * 8:(l + 1) * 8], abc[:, l * 8:(l + 1) * 8], -1.0, ALU.mult)

    def dview(tile, key, ap):
        return tile.v(key).ap(ap)

    def hT_view(c0, c1):
        return hTs.t.ap().rearrange("(kc p) t -> p kc t", p=128)[:, :, c0:c1]

    def wcast(dst, wt, l, c0, c1, rows=D):
        src = wt.t.ap()[l].rearrange("(kc p) n -> p kc n", p=128)
        npc = (c1 - c0 + 511) // 512
        step = (c1 - c0 + npc - 1) // npc
        for a in range(c0, c1, step):
            b = min(c1, a + step)
            P.dma("pool", dst.v(None).ap(dst.t[:, :, a - c0:b - c0]), wt.v(l).ap(src[:, :, a:b]))

    def phase_mod(l):
        P.scope_begin()
        wA = [P.sbuf(f"wA{i}", [128, 8, 512], BF16) for i in range(2)]
        sc = P.sbuf("sc", [128, 16], F32)
        scb = P.sbuf("scb", [128, 16], BF16)
        screp = P.sbuf("screp", [128, 16, 128], BF16)
        bgt = P.sbuf("bgt", [128, 2, D], F32)
        mcol = P.sbuf("mcol", [128, 48, 2], F32)
        psC = P.psum("psC", [128, 96], F32)
        psB = [P.psum(f"psBm{i}", [128, 512], F32) for i in range(2)]
        P.dma("sp", bgt[:, :, :], badag_d.raw(0, 128, l * 2 * D, [[D, 2], [1, D]]))
        P.act(sc[:, :], ccol[:, :], AF.Silu)
        P.copy(scb[:, :], sc[:, :])
        P.copy(screp[:, :, :], sc.raw(0, 128, 0, [[1, 16], [0, 128]]))
        src = w_ada.t.ap()[l].rearrange("(kc p) n -> p kc n", p=128)
        nb = 0
        for pc in range(12):
            w = wA[pc % 2]
            P.dma("pool", w[:, :, :], w_ada.v(l).ap(src[:, :, pc * 512:(pc + 1) * 512]))
            which = pc // 2
            if which in (2, 5):
                for wi in range(2):
                    ps = psB[nb % 2]
                    nb += 1
                    for kc in range(8):
                        P.mm(ps[:, :], screp[:, wi * 8 + kc, :], w[:, kc, :], start=(kc == 0), stop=(kc == 7))
                    gi = 0 if which == 2 else 1
                    half = pc % 2
                    P.tt(gtb[:, gi, wi, half * 512:(half + 1) * 512], ps[:, :], bgt[:, gi, half * 512:(half + 1) * 512],
                         ALU.add)
            else:
                for cc in range(4):
                    ch = pc * 4 + cc
                    for kc in range(8):
                        P.mm(psC[:, ch * 2:ch * 2 + 2], w[:, kc, cc * 128:(cc + 1) * 128],
                             scb.raw(0, 128, kc, [[8, 2]]), start=(kc == 0), stop=(kc == 7))
        for which in (0, 1, 3, 4):
            c0 = which * 8
            P.tt(mcol[:, c0:c0 + 8, :], psC.raw(0, 128, c0 * 2, [[2, 8], [1, 2]]),
                 badac.raw(0, 128, l * 48 + c0, [[1, 8], [0, 2]]), ALU.add)
        for k, (gcol, scw, shw) in enumerate(((PC_G1, 1, 0), (PC_G2, 4, 3))):
            gap = pcol.raw(0, 128, l * PC_N + gcol, [[1, 8], [0, 2]])
            P.stt(modv[:, 2 * k, :, :], mcol[:, scw * 8:scw * 8 + 8, :], 1.0, gap, ALU.add, ALU.mult)
            P.copy(modv[:, 2 * k + 1, :, :], mcol[:, shw * 8:shw * 8 + 8, :])
        P.scope_end()

    def norm_block(Xtile, nt, ci, mi, hb, ss, xh, psT, junk):
        for n in range(nt):
            P.act(junk[:, :], Xtile[:, n, :], AF.Square, accum_out=ss[:, n:n + 1])
        P.act(ss[:, 4:4 + nt], ss[:, 0:nt], AF.Ln, scale=1.0 / D, bias=EPS)
        P.act(ss[:, 4:4 + nt], ss[:, 4:4 + nt], AF.Exp, scale=-0.5)
        def evac(n):
            pst = psT[n % 2]
            for kc in range(8):
                P.ts(hb[:, kc, n * 128:(n + 1) * 128], pst[:, kc * 128:(kc + 1) * 128],
                     modv[:, mi, kc, ci:ci + 1], ALU.mult, modv[:, mi + 1, kc, ci:ci + 1], ALU.add)

        for n in range(nt):
            x_ = xh[n % 2]
            P.ts(x_[:, :], Xtile[:, n, :], ss[:, 4 + n:5 + n], ALU.mult)
            pst = psT[n % 2]
            for kc in range(8):
                P.tr(pst[:, kc * 128:(kc + 1) * 128], x_[:, kc * 128:(kc + 1) * 128], ident_f[:, :])
            if n > 0:
                evac(n - 1)
        evac(nt - 1)

    def x_src(l, bi):
        t0, nt = BLOCKS[bi]
        src = xin if l == 0 else Xs
        return src.v(bi).ap(src.t.ap()[t0 * 128:(t0 + nt) * 128, :].rearrange("(n p) d -> p n d", p=128))

    def phase_norm(l):
        P.scope_begin()
        xb = [P.sbuf(f"xb{i}", [128, 4, D], F32) for i in range(2)]
        junk = P.sbuf("junk", [128, D], BF16)
        xh = [P.sbuf(f"xh{i}", [128, D], F32) for i in range(2)]
        hb = [P.sbuf(f"hb{i}", [128, 8, 512], BF16) for i in range(2)]
        ss = [P.sbuf(f"ss{i}", [128, 8], F32) for i in range(2)]
        psT = [P.psum(f"psTn{i}", [128, D], F32) for i in range(2)]
        for bi, (t0, nt) in enumerate(BLOCKS):
            W = nt * 128
            X = xb[bi % 2]
            P.dma("sp", X[:, 0:nt, :], x_src(l, bi))
            norm_block(X, nt, 1 if bi == 0 else 0, 0, hb[bi % 2], ss[bi % 2], xh, psT, junk)
            P.dma("sp", hTs.v(bi).ap(hT_view(t0 * 128, t0 * 128 + W)), hb[bi % 2][:, :, 0:W])
        P.scope_end()

    def phase_attn(l, mixer, need_ctx):
        P.scope_begin()
        cbase = 0 if mixer == 0 else 512
        wq = P.sbuf("wq", [128, 8, 512], BF16)
        rope = P.sbuf("rope", [128, NT, 64], F32)
        P.dma("sp", rope[:, :, :], rope_d.raw(0, 128, 0, [[64, NT], [1, 64]]))
        wcast(wq, w_in, l, cbase, cbase + 512)
        qT = P.sbuf("qT", [128, 2, T], BF16)
        kTA = P.sbuf("kTA", [128, 2, T], BF16)
        kTB = P.sbuf("kTB", [128, 2, T], BF16)
        va = P.sbuf("va", [128, NT, 2, 65], BF16)
        hb = [P.sbuf(f"hba{i}", [128, 8, 512], BF16) for i in range(2)]
        sq = P.sbuf("sq", [128, 384], F32)
        st6 = P.sbuf("st6", [128, 12], F32)
        qk = [P.sbuf(f"qk{i}", [128, 384], F32) for i in range(2)]
        rt = [P.sbuf(f"rt{i}", [128, 192], F32) for i in range(4)]
        qkr = [P.sbuf(f"qkr{i}", [128, 384], BF16) for i in range(2)]
        kd = [P.sbuf(f"kd{i}", [128, 2, 2, 64], BF16) for i in range(2)]
        wmask = P.sbuf("wmask", [128, 6, 512], BF16)
        pT = [P.sbuf(f"pT{i}", [128, 512], BF16) for i in range(4)]
        osb = [P.sbuf(f"osb{i}", [65, 512], F32) for i in range(2)]
        rec = [P.sbuf(f"rec{i}", [64, 512], F32) for i in range(2)]
        yb = [P.sbuf(f"yb{i}", [128, 512], BF16) for i in range(2)]
        P.scope_begin()
        psP = [P.psum(f"psP{i}", [128, 512], F32) for i in range(2)]
        psTt = P.psum("psTt", [128, 4, 128], BF16)
        if mixer == 0:
            P.dma("pool", wmask[:, :, :], wmask_d.raw(0, 128, 0, [[512, 6], [1, 512]]))
        for ti in range(NT):
            P.memset(va.v(ti)[:, ti, :, 64:65], 1.0, eng="pool")
        P.memset(kTA[64:128, :, :], 0.0)
        P.memset(kTB[0:64, :, :], 0.0, eng="pool")
        P.barrier()
        pb0 = l * PB_N
        tiles = [(bi, t0, nt, n) for bi, (t0, nt) in enumerate(BLOCKS) for n in range(nt)]

        def proj(idx):
            bi, t0, nt, n = tiles[idx]
            W = nt * 128
            h_ = hb[bi % 2]
            if n == 0:
                P.dma("sp", h_[:, :, 0:W], hTs.v(bi).ap(hT_view(t0 * 128, t0 * 128 + W)))
            ps = psP[(t0 + n) % 2]
            for kc in range(8):
                P.mm(ps[:, :], h_[:, kc, n * 128:(n + 1) * 128], wq[:, kc, :], start=(kc == 0), stop=(kc == 7))

        def post(idx):
            bi, t0, nt, n = tiles[idx]
            ti = t0 + n
            col0 = ti * 128
            ps = psP[ti % 2]
            q_ = qk[ti % 2]
            if mixer == 1:
                P.act(sq[:, :], ps[:, 0:384], AF.Square)
                P.reduce(st6[:, 0:6], sq.raw(0, 128, 0, [[64, 6], [1, 64]]), ALU.add)
                P.act(st6[:, 6:12], st6[:, 0:6], AF.Ln, scale=1.0 / 64, bias=EPS)
                P.act(st6[:, 6:12], st6[:, 6:12], AF.Exp, scale=-0.5)
                P.tt(q_.raw(0, 128, 0, [[64, 6], [1, 64]]), ps.raw(0, 128, 0, [[64, 6], [1, 64]]),
                     st6.raw(0, 128, 6, [[1, 6], [0, 64]]), ALU.mult)
                P.tt(q_.raw(0, 128, 0, [[64, 4], [1, 64]]), q_.raw(0, 128, 0, [[64, 4], [1, 64]]),
                     pbc.raw(0, 128, pb0 + PB_GQ, [[0, 4], [1, 64]]), ALU.mult)
                P.tt(q_.raw(0, 128, 256, [[64, 2], [1, 64]]), q_.raw(0, 128, 256, [[64, 2], [1, 64]]),
                     pbc.raw(0, 128, pb0 + PB_GK, [[0, 2], [1, 64]]), ALU.mult)
            else:
                P.copy(q_[:, :], ps[:, 0:384], eng="act")
            hd = [[64, 6], [32, 2], [1, 16]]
            x1 = q_.raw(0, 128, 0, hd)
            x2 = q_.raw(0, 128, 16, hd)
            cs = rope.raw(0, 128, ti * 64, [[0, 6], [16, 2], [1, 16]])
            sn = rope.raw(0, 128, ti * 64 + 32, [[0, 6], [16, 2], [1, 16]])
            r_ = qkr[ti % 2]
            o1 = r_.raw(0, 128, 0, hd)
            o2 = r_.raw(0, 128, 16, hd)
            fl = [[32, 6], [16, 2], [1, 16]]
            ta, tb_, tc, td = (rt[i].raw(0, 128, 0, fl) for i in range(4))
            P.tt(ta, x1, cs, ALU.mult)
            P.tt(tb_, x2, sn, ALU.mult)
            P.tt(tc, x2, cs, ALU.mult)
            P.tt(td, x1, sn, ALU.mult)
            P.tt(o1, ta, tb_, ALU.subtract)
            P.tt(o2, tc, td, ALU.add)
            kd_ = kd[ti % 2]
            P.copy(kd_[:, :, :, :], r_.raw(0, 128, 256, [[64, 2], [0, 2], [1, 64]]), eng="act")
            P.tr(psTt[:, 0, :], r_[:, 0:128], ident_b[:, :])
            P.tr(psTt[:, 1, :], r_[:, 128:256], ident_b[:, :])
            P.tr(psTt[:, 2, :], kd_.raw(0, 128, 0, [[1, 128]]), ident_b[:, :])
            P.tr(psTt[:, 3, :], kd_.raw(0, 128, 128, [[1, 128]]), ident_b[:, :])
            P.copy(qT.v(ti)[:, :, col0:col0 + 128], psTt[:, 0:2, :], eng="act")
            P.copy(kTA.v(ti)[0:64, :, col0:col0 + 128], psTt[0:64, 2:4, :], eng="act")
            P.copy(kTB.v(ti)[64:128, :, col0:col0 + 128], psTt[64:128, 2:4, :])
            P.copy(va.v(ti)[:, ti, :, 0:64], ps.raw(0, 128, 384, [[64, 2], [1, 64]]))

        proj(0)
        for idx in range(len(tiles)):
            if idx + 1 < len(tiles):
                proj(idx + 1)
            post(idx)
        P.scope_end()
        psS = [P.psum(f"psS{i}", [128, 512], F32) for i in range(4)]
        psO = [P.psum(f"psO{i}", [128, 512], F32) for i in range(2)]
        psD = P.psum("psD", [128, 512], F32)
        unit = 0
        scnt = [0]
        pending = [None]
        for bi, (t0, nt) in enumerate(BLOCKS):
            if bi == 0 and not need_ctx:
                continue
            W = nt * 128
            qc0 = t0 * 128
            if bi == 0:
                keys = [(0, None), (1, None)]
            elif mixer == 1:
                keys = [(kt, None) for kt in range(NT)]
            else:
                keys = [(0, None), (1, None)]
                for kt in range(max(2, t0 - 1), min(NT - 1, t0 + 4) + 1):
                    keys.append((kt, kt - t0 + 1))
            for h in range(4):
                g = h // 2
                r0 = (h % 2) * 64
                po = psO[unit % 2]
                nk = len(keys)
                rq = [qT.v(t0 + n)[r0:r0 + 64, g, qc0:qc0 + W] for n in range(nt)]
                sidx = {}

                def emit_S(ki, g=g, r0=r0, W=W, qc0=qc0, keys=keys, rq=rq, sidx=sidx):
                    kt, mk = keys[ki]
                    si = scnt[0]
                    scnt[0] += 1
                    sidx[ki] = si
                    ps = psS[si % len(psS)]
                    kTx = kTA if r0 == 0 else kTB
                    P.op("pe", lambda: nc.tensor.matmul(
                        ps.t[:, 0:W], kTx.t[:, g, kt * 128:(kt + 1) * 128], qT.t[:, g, qc0:qc0 + W],
                        start=True, stop=(mk is None)), [ps[:, 0:W]], [kTx.v(kt)[:, g, 0:1]] + rq)
                    if mk is not None:
                        P.mm(ps[:, 0:W], ident_b[:, :], wmask[:, mk, 0:W], start=False, stop=True)

                def emit_rest(ki, g=g, W=W, keys=keys, po=po, nk=nk, sidx=sidx):
                    kt, mk = keys[ki]
                    si = sidx[ki]
                    ps = psS[si % len(psS)]
                    p_ = pT[si % len(pT)]
                    P.act(p_[:, 0:W], ps[:, 0:W], AF.Exp, scale=0.125)
                    P.op("pe", lambda: nc.tensor.matmul(
                        po.t[0:65, 0:W], va.t[:, kt, g, :], p_.t[:, 0:W], start=(ki == 0), stop=(ki == nk - 1)),
                        [po[0:65, 0:W]], [va.v(kt)[:, kt, g, :], p_[:, 0:W]])

                def finalize(unit=unit, h=h, g=g, r0=r0, W=W, qc0=qc0, po=po, bi=bi):
                    o_ = osb[unit % 2]
                    rc = rec[unit % 2]
                    P.copy(o_[0:65, 0:W], po[0:65, 0:W])
                    P.mm(psD[:, 0:W], sel65[0:65, :], o_[0:65, 0:W])
                    if mixer == 0:
                        P.ts(rc[0:64, 0:W], psD[0:64, 0:W], esink[0:64, l * 4 + h:l * 4 + h + 1], ALU.add)
                        P.recip(rc[0:64, 0:W], rc[0:64, 0:W])
                    else:
                        P.recip(rc[0:64, 0:W], psD[0:64, 0:W])
                    y_ = yb[(unit // 2) % 2]
                    P.tt(y_[r0:r0 + 64, 0:W], o_[0:64, 0:W], rc[0:64, 0:W], ALU.mult)
                    if h % 2 == 1:
                        row0 = mixer * 256 + g * 128
                        P.dma("sp", yTs.v((bi, mixer * 2 + g)).ap(yTs.t.ap()[row0:row0 + 128, qc0:qc0 + W]), y_[:, 0:W])

                LA = 2
                for ki in range(min(LA, nk)):
                    emit_S(ki)
                for ki in range(nk):
                    if ki + LA < nk:
                        emit_S(ki + LA)
                    emit_rest(ki)
                    if ki == 1 and pending[0] is not None:
                        pending[0]()
                        pending[0] = None
                if pending[0] is not None:
                    pending[0]()
                pending[0] = finalize
                unit += 1
        if pending[0] is not None:
            pending[0]()
        P.scope_end()

    def phase_mlstm(l, need_ctx):
        P.scope_begin()
        wC = P.sbuf("wC", [128, 8, 1040], BF16)
        wcast(wC, w_in, l, 1024, 2064)
        qT = P.sbuf("qTc", [128, 2, T], BF16)
        kT = P.sbuf("kTc", [128, 2, T], BF16)
        ktok = P.sbuf("ktok", [128, NT, 256], BF16)
        va = P.sbuf("vac", [128, NT, 4, 65], BF16)
        sigo = P.sbuf("sigo", [128, 2, T], BF16)
        hF = P.sbuf("hF", [128, 2, T], BF16)
        ig = P.sbuf("ig", [128, NT, 8], F32)
        lf = P.sbuf("lf", [128, NT, 8], F32)
        mst = P.sbuf("mst", [128, 8], F32)
        Cf = P.sbuf("Cf", [128, 8, 65], F32)
        Cb = P.sbuf("Cb", [128, 8, 65], BF16)
        P.scope_begin()
        hb = [P.sbuf(f"hbc{i}", [128, 8, 512], BF16) for i in range(2)]
        qkb = [P.sbuf(f"qkb{i}", [128, 512], BF16) for i in range(2)]
        gpre = P.sbuf("gpre", [128, 4, 16], F32)
        ge = P.sbuf("ge", [128, 4, 8], F32)
        psA = [P.psum(f"psA{i}", [128, 512], F32) for i in range(2)]
        psTt = P.psum("psTtc", [128, 4, 128], BF16)
        for ti in range(NT):
            P.memset(va.v(ti)[:, ti, :, 64:65], 1.0, eng="pool")
        pb0 = l * PB_N
        for bi, (t0, nt) in enumerate(BLOCKS):
            W = nt * 128
            h_ = hb[bi % 2]
            P.dma("sp", h_[:, :, 0:W], hTs.v(bi).ap(hT_view(t0 * 128, t0 * 128 + W)))
            for n in range(nt):
                ti = t0 + n
                col0 = ti * 128
                p1, p2 = psA[0], psA[1]
                for kc in range(8):
                    P.mm(p1[:, :], h_[:, kc, n * 128:(n + 1) * 128], wC[:, kc, 0:512], start=(kc == 0), stop=(kc == 7))
                for kc in range(8):
                    P.mm(p2[:, 0:256], h_[:, kc, n * 128:(n + 1) * 128], wC[:, kc, 512:768], start=(kc == 0),
                         stop=(kc == 7))
                for kc in range(8):
                    P.mm(p2[:, 256:272], h_[:, kc, n * 128:(n + 1) * 128], wC[:, kc, 1024:1040], start=(kc == 0),
                         stop=(kc == 7))
                qb = qkb[ti % 2]
                P.copy(qb[:, :], p1[:, :], eng="act")
                P.copy(ktok.v(ti)[:, ti, :], qb[:, 256:512], eng="pool")
                for i in range(4):
                    P.tr(psTt[:, i, :], qb[:, i * 128:(i + 1) * 128], ident_b[:, :])
                P.ts(qT.v(ti)[:, :, col0:col0 + 128], psTt[:, 0:2, :], 0.125, ALU.mult)
                P.copy(kT.v(ti)[:, :, col0:col0 + 128], psTt[:, 2:4, :])
                P.copy(va.v(ti)[:, ti, :, 0:64], p2.raw(0, 128, 0, [[64, 4], [1, 64]]))
                P.tt(gpre[:, n, :], p2[:, 256:272], pbc[:, pb0 + PB_GB:pb0 + PB_GB + 16], ALU.add)
            if DBG.get("c_skipg", 0):
                continue
            P.copy(ig.v(bi).ap(ig.t[:, t0:t0 + nt, :].rearrange("p n (d h) -> p n d h", d=2)),
                   gpre.raw(0, 128, 0, [[16, nt], [8, 2], [1, 4]]))
            P.act(ge.raw(0, 128, 0, [[8, nt], [4, 2], [1, 4]]), gpre.raw(0, 128, 4, [[16, nt], [8, 2], [1, 4]]),
                  AF.Exp, scale=-1.0)
            P.act(ge[:, 0:nt, :], ge[:, 0:nt, :], AF.Ln, bias=1.0)
            P.ts(lf.v(bi)[:, t0:t0 + nt, :], ge[:, 0:nt, :], -1.0, ALU.mult)
            for pr in range(0 if DBG.get("c_skipo", 0) else 2):
                po = psA[pr]
                for kc in range(8):
                    P.mm(po[:, 0:W], wC[:, kc, 768 + pr * 128:768 + (pr + 1) * 128], h_[:, kc, 0:W], start=(kc == 0), stop=(kc == 7))
                P.act(sigo.v(bi)[:, pr, t0 * 128:t0 * 128 + W], po[:, 0:W], AF.Sigmoid)
        P.scope_end()
        ab = [P.sbuf(f"ab{i}", [128, 128], F32) for i in range(2)]
        abT = P.sbuf("abT", [8, 128], F32)
        rhsE = P.sbuf("rhsE", [128, 8, 128], F32)
        Mbc = [P.sbuf(f"Mbc{i}", [128, 4, 128], F32) for i in range(2)]
        f32t = lambda nm: [P.sbuf(f"{nm}{i}", [128, 128], F32) for i in range(4)]
        bf16t = lambda nm: [P.sbuf(f"{nm}{i}", [128, 128], BF16) for i in range(4)]
        wT, dp, nd, dm, hs, sqh, rs = (f32t(n_) for n_ in ("wT", "dp", "nd", "dm", "hs", "sqh", "rs"))
        ST, qd, qz, kw = (bf16t(n_) for n_ in ("ST", "qd", "qz", "kw"))
        wk = [P.sbuf(f"wk{i}", [128, 2], F32) for i in range(4)]
        tmp4 = [P.sbuf(f"tmp4{i}", [128, 4, 128], F32) for i in range(2)]
        bend = [P.sbuf(f"bend{i}", [128, 4], F32) for i in range(2)]
        e14 = [P.sbuf(f"e14{i}", [64, 4, 128], F32) for i in range(2)]
        yb = [P.sbuf(f"ybc{i}", [128, 128], BF16) for i in range(4)]
        ps_bc = P.psum("ps_bc", [128, 1024], F32)
        psHd = [P.psum(f"psHd{i}", [128, 512], F32) for i in range(4)]
        psX = P.psum("psX", [128, 512], F32)
        for t_ in ab + [rhsE] + qd + qz + kw + sqh:
            P.memset(t_.v(None).ap(t_.t.ap()), 0.0)
        P.barrier()
        gml0 = l * PC_N + PC_GML
        it = 0
        for d in range(DBG.get("c_dirs", 2)):
            P.barrier()
            P.memset(mst[:, :], 0.0)
            P.memset(Cf[:, :, :], 0.0)
            P.memset(Cb[:, :, :], 0.0)
            P.barrier()
            order = list(range(NT)) if d == 0 else [1, 0] + list(range(NT - 1, 1, -1))
            last = 127 if d == 0 else 0
            def prologue(c, par, d=d):
                bi = 0 if c < 2 else 1 + (c - 2) // 4
                ab_ = ab[par]
                psBv = psX[:, 0:4]
                P.mm(psBv, tri[:, d, :], lf.v(bi)[:, c, d * 4:(d + 1) * 4])
                P.tt(ab_[:, 0:4], ig.v(bi)[:, c, d * 4:(d + 1) * 4], psBv, ALU.subtract)
                P.copy(ab_[:, 4:8], psBv, eng="act")
                P.tr(psX[:, 128:256], ab_[:, :], ident_f[:, :])
                P.copy(abT[:, :], psX[0:8, 128:256], eng="act")
                P.tt(rhsE[0:8, :, :], abT.raw(0, 8, 0, [[0, 8], [1, 128]]), eye8x[:, :, :], ALU.mult)
                P.mm(ps_bc[:, 0:512], ones_f[:, :], rhsE.raw(0, 128, 0, [[1, 512]]))
                P.mm(ps_bc[:, 512:1024], ones_f[:, :], rhsE.raw(0, 128, 512, [[1, 512]]))

            clist = order[:DBG.get("c_chunks", NT)]
            if clist:
                prologue(clist[0], it % 2)
            for ci_, c in enumerate(clist):
                bi = 0 if c < 2 else 1 + (c - 2) // 4
                cols = slice(c * 128, (c + 1) * 128)
                need_out = need_ctx or c >= 2
                ab_ = ab[it % 2]
                Mb = Mbc[it % 2]
                par = it % 2
                it += 1
                for h in range(4):
                    j = d * 4 + h
                    if d == 0:
                        o_ap = Mb[:, h, :]
                        a_ap = ps_bc[:, h * 128:(h + 1) * 128]
                    else:
                        o_ap = Mb.raw(0, 128, h * 128 + 127, [[-1, 128]])
                        a_ap = ps_bc.raw(0, 128, h * 128 + 127, [[-1, 128]])
                    P.scan(o_ap, zeros_f[:, :], a_ap, mst.v(j)[:, j:j + 1], ALU.add, ALU.max)
                P.copy(bend[par][:, :], ps_bc.raw(0, 128, 512 + last, [[128, 4]]))
                if need_out:
                    P.tt(tmp4[par][:, :, :], negm.raw(0, 128, d * 128, [[0, 4], [1, 128]]), Mb[:, :, :], ALU.subtract)
                    P.tt(e14[par][0:64, :, :], ps_bc.raw(0, 64, 512, [[128, 4], [1, 128]]), Mb[0:64, :, :], ALU.add)
                    P.act(e14[par][0:64, :, :], e14[par][0:64, :, :], AF.Exp, scale=-1.0)
                if ci_ + 1 < len(clist):
                    prologue(clist[ci_ + 1], 1 - par)

                def head_gen(h, d=d, c=c, bi=bi, cols=cols, need_out=need_out, ab_=ab_, Mb=Mb, par=par, last=last):
                    j = d * 4 + h
                    pair = h // 2
                    r0 = (h % 2) * 64
                    rr = slice(r0, r0 + 64)
                    Mend = Mb[:, h, last:last + 1]
                    if need_out:
                        P.copy(qz[h][rr, :], qT.v(c)[rr, pair, cols], eng="act")
                        P.act(dp[h][rr, :], Mb[rr, h, :], AF.Exp, scale=-1.0, bias=mst.v(j)[rr, j:j + 1])
                        yield
                        P.act(wT[h][:, :], tmp4[par][:, h, :], AF.Exp, bias=ab_[:, h:h + 1])
                        P.mm(psHd[h][:, 0:128], kT.v(c)[:, pair, cols], qz[h][:, :])
                        P.tt(qd[h][rr, :], qT.v(c)[rr, pair, cols], dp[h][rr, :], ALU.mult)
                        yield
                        P.tt(ST[h][:, :], psHd[h][:, 0:128], wT[h][:, :], ALU.mult)
                        yield
                        P.mm(psHd[h][0:65, 128:256], va.v(c)[:, c, h, :], ST[h][:, :], start=True, stop=False)
                        P.mm(psHd[h][0:65, 128:256], Cb.v(j)[:, j, :], qd[h][:, :], start=False, stop=True)
                    P.act(wk[h][:, 0:1], Mend, AF.Exp, scale=-1.0, bias=ab_[:, h:h + 1])
                    P.act(wk[h][:, 1:2], Mend, AF.Exp, scale=-1.0, bias=mst.v(j)[:, j:j + 1])
                    yield
                    P.tt(mst.v(j)[:, j:j + 1], bend[par][:, h:h + 1], Mend, ALU.add)
                    P.ts(kw[h][:, rr], ktok.v(c)[:, c, h * 64:(h + 1) * 64], wk[h][:, 0:1], ALU.mult)
                    yield
                    P.mm(psHd[h][:, 256:321], kw[h][:, :], va.v(c)[:, c, h, :])
                    yield
                    P.stt(Cf.v(j)[rr, j, :], Cf.v(j)[rr, j, :], wk[h][rr, 1:2], psHd[h][rr, 256:321], ALU.mult, ALU.add)
                    yield
                    P.copy(Cb.v(j)[rr, j, :], Cf.v(j)[rr, j, :], eng="act")
                    if need_out:
                        P.copy(nd[h][0:65, :], psHd[h][0:65, 128:256], eng="act")
                        yield
                        P.mm(psHd[h][:, 384:512], sel65[0:65, :], nd[h][0:65, :])
                        yield
                        P.act(dm[h][0:64, :], psHd[h][0:64, 384:512], AF.Abs)
                        yield
                        P.tt(dm[h][0:64, :], dm[h][0:64, :], e14[par][0:64, h, :], ALU.max)
                        yield
                        P.recip(dm[h][0:64, :], dm[h][0:64, :])
                        yield
                        if d == 0:
                            P.tt(hF.v(c)[rr, pair, cols], nd[h][0:64, :], dm[h][0:64, :], ALU.mult)
                        else:
                            hs_ = hs[h]
                            P.tt(hs_[rr, :], nd[h][0:64, :], dm[h][0:64, :], ALU.mult)
                            yield
                            P.tt(hs_[rr, :], hs_[rr, :], hF.v(c)[rr, pair, cols], ALU.add)
                            yield
                            P.act(sqh[h][rr, :], hs_[rr, :], AF.Square)
                            yield
                            P.mm(psHd[h][:, 384:512], onesblk[:, :], sqh[h][:, :])
                            yield
                            P.act(rs[h][rr, :], psHd[h][rr, 384:512], AF.Ln, scale=1.0 / 64, bias=EPS)
                            yield
                            P.act(rs[h][rr, :], rs[h][rr, :], AF.Exp, scale=-0.5)
                            yield
                            P.tt(hs_[rr, :], hs_[rr, :], rs[h][rr, :], ALU.mult)
                            yield
                            y_ = yb[par * 2 + pair]
                            P.stt(y_[rr, :], hs_[rr, :], pcol[rr, gml0 + pair:gml0 + pair + 1],
                                  sigo.v(bi)[rr, pair, cols], ALU.mult, ALU.mult)
                            if h % 2 == 1:
                                row0 = 512 + pair * 128
                                P.dma("sp", yTs.v((c, 4 + pair)).ap(yTs.t.ap()[row0:row0 + 128, cols]), y_[:, :])

                gens = [head_gen(h) for h in range(4)]
                while gens:
                    nxt = []
                    for g_ in gens:
                        try:
                            next(g_)
                            nxt.append(g_)
                        except StopIteration:
                            pass
                    gens = nxt
        P.scope_end()

    def phase_ssd(l, need_ctx):
        P.scope_begin()
        wD = P.sbuf("wD", [128, 8, 1032], BF16)
        wcast(wD, w_in, l, 2064, 3096)
        xT = P.sbuf("xTd", [128, 2, T], BF16)
        BT = P.sbuf("BTd", [128, 2, T], BF16)
        CT = P.sbuf("CTd", [128, 2, T], BF16)
        szT = P.sbuf("szT", [128, 2, T], BF16)
        yF = P.sbuf("yF", [128, 2, T], BF16)
        dtv = P.sbuf("dtv", [128, NT, 8], F32)
        dA = P.sbuf("dA", [128, NT, 8], F32)
        P.scope_begin()
        hb = [P.sbuf(f"hbd{i}", [128, 8, 514], BF16) for i in range(2)]
        pre = [P.sbuf(f"pre{i}", [128, 514], F32) for i in range(2)]
        acc = [P.sbuf(f"acc{i}", [128, 512], F32) for i in range(2)]
        dtx = P.sbuf("dtx", [128, 4, 8], F32)
        dty = P.sbuf("dty", [128, 4, 8], F32)
        psA = [P.psum(f"psAd{i}", [128, 512], F32) for i in range(2)]
        psH2 = P.psum("psH2", [128, 512], F32)
        pc0 = l * PC_N
        pb0 = l * PB_N
        dests = [xT, xT, BT, BT, CT, CT]
        colof = [0, 128, 512, 640, 768, 896]
        for bi, (t0, nt) in enumerate(BLOCKS[:DBG.get("d_blocks", 9)]):
            W = nt * 128
            h_ = hb[bi % 2]
            has_l = bi >= 2
            has_r = 1 <= bi < len(BLOCKS) - 1
            c_lo = t0 * 128 - (1 if has_l else 0)
            c_hi = t0 * 128 + W + (1 if has_r else 0)
            d_lo = 0 if has_l else 1
            P.dma("sp", h_[:, :, d_lo:d_lo + (c_hi - c_lo)], hTs.v(bi).ap(hT_view(c_lo, c_hi)))
            if not has_l:
                P.memset(h_[:, :, 0:1], 0.0, eng="pool")
            if not has_r:
                P.memset(h_[:, :, W + 1:W + 2], 0.0, eng="pool")
            n_main = min(512, W + 2)
            for cc in range(DBG.get("d_ncc", 6)):
                ps = psA[cc % 2]
                pr_ = pre[cc % 2]
                wc = colof[cc]
                for kc in range(8):
                    P.mm(ps[:, 0:n_main], wD[:, kc, wc:wc + 128], h_[:, kc, 0:n_main], start=(kc == 0), stop=(kc == 7))
                P.copy(pr_[:, 0:n_main], ps[:, 0:n_main], eng="act")
                if W + 2 > 512:
                    for kc in range(8):
                        P.mm(psH2[:, 0:2], wD[:, kc, wc:wc + 128], h_[:, kc, 512:514], start=(kc == 0), stop=(kc == 7))
                    P.copy(pr_[:, 512:514], psH2[:, 0:2], eng="act")
                a_ = acc[cc % 2]
                cw = lambda jj: pcol[:, pc0 + PC_CW + jj * 6 + cc:pc0 + PC_CW + jj * 6 + cc + 1]
                P.ts(a_[:, 0:W], pr_[:, 0:W], cw(0), ALU.mult)
                P.stt(a_[:, 0:W], pr_[:, 1:W + 1], cw(1), a_[:, 0:W], ALU.mult, ALU.add)
                P.stt(a_[:, 0:W], pr_[:, 2:W + 2], cw(2), a_[:, 0:W], ALU.mult, ALU.add)
                P.act(dests[cc].v(bi)[:, cc % 2, t0 * 128:t0 * 128 + W], a_[:, 0:W], AF.Silu,
                      bias=pcol[:, pc0 + PC_CB + cc:pc0 + PC_CB + cc + 1])
            for zc in range(DBG.get("d_nz", 2)):
                ps = psA[zc]
                for kc in range(8):
                    P.mm(ps[:, 0:W], wD[:, kc, 256 + zc * 128:256 + (zc + 1) * 128], h_[:, kc, 1:W + 1],
                         start=(kc == 0), stop=(kc == 7))
                P.act(szT.v(bi)[:, zc, t0 * 128:t0 * 128 + W], ps[:, 0:W], AF.Silu)
            if DBG.get("d_skipdt", 0) == 1:
                continue
            for n in range(nt):
                for kc in range(8):
                    P.mm(psH2[:, 16 + n * 8:24 + n * 8], h_[:, kc, 1 + n * 128:1 + (n + 1) * 128], wD[:, kc, 1024:1032],
                         start=(kc == 0), stop=(kc == 7))
            if DBG.get("d_skipdt", 0) == 2:
                continue
            dps = psH2.raw(0, 128, 16, [[8, nt], [1, 8]])
            dtops = [
                lambda: P.tt(dtx[:, 0:nt, :], dps, pbc.raw(0, 128, pb0 + PB_DTB, [[0, nt], [1, 8]]), ALU.add),
                lambda: P.act(dty[:, 0:nt, :], dtx[:, 0:nt, :], AF.Abs),
                lambda: P.act(dty[:, 0:nt, :], dty[:, 0:nt, :], AF.Exp, scale=-1.0),
                lambda: P.act(dty[:, 0:nt, :], dty[:, 0:nt, :], AF.Ln, bias=1.0),
                lambda: P.ts(dtx[:, 0:nt, :], dtx[:, 0:nt, :], 0.0, ALU.max),
                lambda: P.tt(dtv.v(bi)[:, t0:t0 + nt, :], dtx[:, 0:nt, :], dty[:, 0:nt, :], ALU.add),
                lambda: [P.tt(dA.v(bi)[:, t0 + n_, :], dtv.v(bi)[:, t0 + n_, :], abc[:, l * 8:(l + 1) * 8], ALU.mult)
                         for n_ in range(nt)],
            ]
            for f_ in dtops[:DBG.get("d_dtops", 99)]:
                f_()
        P.scope_end()
        hf = P.sbuf("hstf", [128, 8, 64], F32)
        hz = P.sbuf("hstz", [128, 8, 128], BF16)

        def mk(dd):
            B = {}
            B["cum"] = [P.sbuf(f"cum{dd}{i}", [128, 128], F32) for i in range(2)]
            B["cumT"] = P.sbuf(f"cumT{dd}", [8, 128], F32)
            B["ncum"] = [P.sbuf(f"ncum{dd}{i}", [128, 4], F32) for i in range(2)]
            B["cbc"] = [P.sbuf(f"cbc{dd}{i}", [128, 512], F32) for i in range(2)]
            B["xtB"] = [P.sbuf(f"xtB{dd}{i}", [128, 4, 128], BF16) for i in range(2)]
            B["rhsE"] = P.sbuf(f"rhsEd{dd}", [128, 4, 128], F32)
            B["ecum"] = [P.sbuf(f"ecum{dd}{i}", [128, 4, 128], BF16) for i in range(2)]
            B["cend"] = [P.sbuf(f"cend{dd}{i}", [128, 8], F32) for i in range(2)]
            B["tmpall"] = [P.sbuf(f"tmpall{dd}{i}", [128, 4, 128], F32) for i in range(2)]
            B["dec"] = [P.sbuf(f"dec{dd}{i}", [128, 128], F32) for i in range(4)]
            B["ST"] = [P.sbuf(f"STd{dd}{i}", [128, 128], BF16) for i in range(4)]
            B["xdtz"] = [P.sbuf(f"xdtz{dd}{i}", [128, 128], BF16) for i in range(4)]
            B["Cd"] = [P.sbuf(f"Cd{dd}{i}", [128, 128], BF16) for i in range(4)]
            B["wend"] = [P.sbuf(f"wend{dd}{i}", [128, 1], F32) for i in range(4)]
            B["Bw"] = [P.sbuf(f"Bw{dd}{i}", [128, 128], BF16) for i in range(4)]
            B["yz"] = [P.sbuf(f"yz{dd}{i}", [128, 2, 128], F32) for i in range(2)]
            B["sqy"] = P.sbuf(f"sqy{dd}", [128, 2, 128], F32)
            B["rs"] = P.sbuf(f"rsd{dd}", [128, 128], F32)
            B["yb"] = [P.sbuf(f"ybd{dd}{i}", [128, 2, 128], BF16) for i in range(2)]
            B["psTt"] = P.psum(f"psTtd{dd}", [128, 4, 128], BF16)
            B["ps_bc"] = P.psum(f"ps_bcd{dd}", [128, 512], F32)
            B["psX"] = P.psum(f"psXd{dd}", [128, 512], F32)
            for t_ in B["cum"] + [B["rhsE"]] + B["xdtz"]:
                P.memset(t_.v(None).ap(t_.t.ap()), 0.0)
            return B

        ndir = DBG.get("d_dirs", 2)
        DB = [mk(dd) for dd in range(ndir)]
        psY = [P.psum(f"psYd{i}", [128, 512], F32) for i in range(2)]
        P.memset(hf[:, :, :], 0.0)
        P.memset(hz[:, :, :], 0.0)
        P.barrier()
        orders = [list(range(NT)), [1, 0] + list(range(NT - 1, 1, -1))]
        steps = [{c: i for i, c in enumerate(o)} for o in orders]

        def dir_gen(d):
            B = DB[d]
            last = 127 if d == 0 else 0
            psX, ps_bc, psTt = B["psX"], B["ps_bc"], B["psTt"]
            clist = orders[d][:DBG.get('d_chunks', NT)]

            def prologue(c, k1):
                bi = 0 if c < 2 else 1 + (c - 2) // 4
                cols = slice(c * 128, (c + 1) * 128)
                need_out = need_ctx or c >= 2
                cum_ = B["cum"][k1]
                P.mm(psX[:, 0:4], tri[:, d, :], dA.v(bi)[:, c, d * 4:(d + 1) * 4])
                P.copy(cum_[:, 0:4], psX[:, 0:4], eng="act")
                P.ts(B["ncum"][k1][:, :], psX[:, 0:4], -1.0, ALU.mult)
                P.tr(psX[:, 128:256], cum_[:, :], ident_f[:, :])
                P.copy(B["cumT"][0:4, :], psX[0:4, 128:256], eng="act")
                P.tt(B["rhsE"][0:4, :, :], B["cumT"].raw(0, 4, 0, [[0, 4], [1, 128]]), eye8x[0:4, 0:4, :], ALU.mult)
                P.mm(ps_bc[:, :], ones_f[:, :], B["rhsE"].raw(0, 128, 0, [[1, 512]]))
                ce = B["cend"][k1]
                P.copy(B["cbc"][k1][:, :], ps_bc[:, :], eng="act")
                P.copy(ce[:, 0:4], ps_bc.raw(0, 128, last, [[128, 4]]))
                P.act(ce[:, 4:8], ce[:, 0:4], AF.Exp)
                P.tr(psTt[:, 0, :], xT.v(bi)[:, 0, cols], ident_b[:, :])
                P.tr(psTt[:, 1, :], xT.v(bi)[:, 1, cols], ident_b[:, :])
                P.tr(psTt[:, 2, :], BT.v(bi)[:, 0, cols], ident_b[:, :])
                P.tr(psTt[:, 3, :], BT.v(bi)[:, 1, cols], ident_b[:, :])
                P.copy(B["xtB"][k1][:, :, :], psTt[:, :, :], eng="act")
                if need_out:
                    P.act(B["ecum"][k1][:, :, :], B["cbc"][k1].raw(0, 128, 0, [[128, 4], [1, 128]]), AF.Exp)
                    P.tt(B["tmpall"][k1][:, :, :], negm.raw(0, 128, d * 128, [[0, 4], [1, 128]]),
                         B["cbc"][k1].raw(0, 128, 0, [[128, 4], [1, 128]]), ALU.add)

            if clist:
                prologue(clist[0], 0)
            for ci_, c in enumerate(clist):
                bi = 0 if c < 2 else 1 + (c - 2) // 4
                cols = slice(c * 128, (c + 1) * 128)
                need_out = need_ctx or c >= 2
                k1 = ci_ % 2
                second = (ndir == 2) and steps[1 - d][c] < ci_
                if ndir == 1:
                    second = False
                cum_ = B["cum"][k1]
                ce = B["cend"][k1]
                if ci_ + 1 < len(clist):
                    prologue(clist[ci_ + 1], 1 - k1)
                yz_ = B["yz"][k1]

                def pair_gen(p):
                    bank = psY[p]
                    yo = d * 256
                    gcol = slice(256 + p * 128, 384 + p * 128)
                    heads = [(2 * p + q, d * 4 + 2 * p + q, slice(q * 64, q * 64 + 64)) for q in range(2)]
                    xdtz, wend, Bw, dec, ST, Cd = B["xdtz"], B["wend"], B["Bw"], B["dec"], B["ST"], B["Cd"]
                    if need_out:
                        P.mm(psX[:, gcol], BT.v(bi)[:, p, cols], CT.v(bi)[:, p, cols])
                    for h, j, rr in heads:
                        P.ts(xdtz[h][:, rr], B["xtB"][k1][:, p, rr], dtv.v(bi)[:, c, j:j + 1], ALU.mult)
                        P.act(wend[h][:, :], cum_[:, h:h + 1], AF.Exp, scale=-1.0, bias=ce[:, h:h + 1])
                    yield
                    for h, j, rr in heads:
                        if need_out:
                            P.tt(Cd[h][:, :], CT.v(bi)[:, p, cols], B["ecum"][k1][:, h, :], ALU.mult, eng="pool")
                        P.ts(Bw[h][:, :], B["xtB"][k1][:, 2 + p, :], wend[h][:, 0:1], ALU.mult)
                    yield
                    if need_out:
                        for h, j, rr in heads:
                            P.act(dec[h][:, :], B["tmpall"][k1][:, h, :], AF.Exp, bias=B["ncum"][k1][:, h:h + 1])
                        yield
                        for h, j, rr in heads:
                            P.tt(ST[h][:, :], psX[:, gcol], dec[h][:, :], ALU.mult)
                        yield
                        (hA, jA, _), (hB, jB, _) = heads
                        yv = bank[:, yo:yo + 128]
                        P.mm(yv, xdtz[hA][:, :], ST[hA][:, :], start=True, stop=False)
                        P.mm(yv, hz.v(jA)[:, jA, :], Cd[hA][:, :], start=False, stop=False)
                        P.mm(yv, xdtz[hB][:, :], ST[hB][:, :], start=False, stop=False)
                        P.mm(yv, hz.v(jB)[:, jB, :], Cd[hB][:, :], start=False, stop=True)
                        yield
                        if not second:
                            P.copy(yF.v(c)[:, p, cols], yv, eng="act")
                        else:
                            P.tt(yz_[:, p, :], yv, yF.v(c)[:, p, cols], ALU.add)
                    for q, (h, j, rr) in enumerate(heads):
                        P.mm(bank[:, yo + 128 + q * 64:yo + 192 + q * 64], Bw[h][:, :], xdtz[h][:, rr])
                    yield
                    for q, (h, j, rr) in enumerate(heads):
                        P.stt(hf.v(j)[:, j, :], hf.v(j)[:, j, :], ce[:, 4 + h:5 + h],
                              bank[:, yo + 128 + q * 64:yo + 192 + q * 64], ALU.mult, ALU.add)
                    yield
                    for h, j, rr in heads:
                        P.copy(hz.v(j)[:, j, rr], hf.v(j)[:, j, :], eng="act")
                    if need_out and second:
                        P.stt(yz_[:, p, :], xT.v(bi)[:, p, cols], pcol[:, pc0 + PC_DSK + p:pc0 + PC_DSK + p + 1],
                              yz_[:, p, :], ALU.mult, ALU.add)
                        yield
                        P.tt(yz_[:, p, :], yz_[:, p, :], szT.v(bi)[:, p, cols], ALU.mult)

                gens = [pair_gen(p) for p in range(2)]
                while gens:
                    nxt = []
                    for g_ in gens:
                        try:
                            next(g_)
                            nxt.append(g_)
                        except StopIteration:
                            pass
                    gens = nxt
                    yield
                if need_out and second:
                    sqy, rs = B["sqy"], B["rs"]
                    P.act(sqy[:, :, :], yz_[:, :, :], AF.Square)
                    pSS = psX[:, 384:512]
                    P.mm(pSS, ones_f[:, :], sqy[:, 0, :], start=True, stop=False)
                    P.mm(pSS, ones_f[:, :], sqy[:, 1, :], start=False, stop=True)
                    P.act(rs[:, :], pSS, AF.Ln, scale=1.0 / 256, bias=EPS)
                    P.act(rs[:, :], rs[:, :], AF.Exp, scale=-0.5)
                    y_ = B["yb"][k1]
                    for pair in range(2):
                        P.stt(y_[:, pair, :], yz_[:, pair, :],
                              pcol[:, pc0 + PC_GSSM + pair:pc0 + PC_GSSM + pair + 1], rs[:, :], ALU.mult, ALU.mult)
                    P.dma("sp", yTs.v((c, 6)).ap(yTs.t.ap()[768:1024, cols].rearrange("(a p) t -> p a t", p=128)),
                          y_[:, :, :])
                yield

        dgens = [dir_gen(d) for d in range(ndir)]
        while dgens:
            nxt = []
            for g_ in dgens:
                try:
                    next(g_)
                    nxt.append(g_)
                except StopIteration:
                    pass
            dgens = nxt
        P.scope_end()

    def phase_oproj(l, need_ctx):
        P.scope_begin()
        wo = P.sbuf("wo", [128, 8, D], BF16)
        wcast(wo, w_out, l, 0, D)
        yTb = [P.sbuf(f"yTb{i}", [128, 8, 512], BF16) for i in range(2)]
        xb = [P.sbuf(f"xbo{i}", [128, 4, D], F32) for i in range(2)]
        junk = P.sbuf("junko", [128, D], BF16)
        xh = [P.sbuf(f"xho{i}", [128, D], F32) for i in range(2)]
        hb = [P.sbuf(f"hbo{i}", [128, 8, 512], BF16) for i in range(2)]
        ss = [P.sbuf(f"sso{i}", [128, 8], F32) for i in range(2)]
        t1 = [P.sbuf(f"t1o{i}", [128, 512], F32) for i in range(2)]
        psO = [P.psum(f"psOo{i}", [128, 512], F32) for i in range(2)]
        psT = [P.psum(f"psTo{i}", [128, D], F32) for i in range(2)]
        kk = [0]

        def oproj(bi):
            t0, nt = BLOCKS[bi]
            W = nt * 128
            ci = 1 if bi == 0 else 0
            X = xb[bi % 2]
            yb_ = yTb[bi % 2]
            P.dma("sp", X[:, 0:nt, :], x_src(l, bi))
            P.dma("sp", yb_[:, :, 0:W],
                  yTs.v(("o", bi)).ap(yTs.t.ap().rearrange("(kc p) t -> p kc t", p=128)[:, :, t0 * 128:t0 * 128 + W]))
            for n in range(nt):
                for half in range(2):
                    ps = psO[kk[0] % 2]
                    t_ = t1[kk[0] % 2]
                    kk[0] += 1
                    for fc in range(8):
                        P.mm(ps[:, :], yb_[:, fc, n * 128:(n + 1) * 128], wo[:, fc, half * 512:(half + 1) * 512],
                             start=(fc == 0), stop=(fc == 7))
                    P.tt(t_[:, :], ps[:, :], gtb[:, 0, ci, half * 512:(half + 1) * 512], ALU.mult)
                    P.tt(X[:, n, half * 512:(half + 1) * 512], X[:, n, half * 512:(half + 1) * 512], t_[:, :], ALU.add,
                         eng="pool")
            P.dma("sp", Xs.v(bi).ap(Xs.t.ap()[t0 * 128:(t0 + nt) * 128, :].rearrange("(n p) d -> p n d", p=128)),
                  X[:, 0:nt, :])

        def onorm(bi):
            t0, nt = BLOCKS[bi]
            W = nt * 128
            ci = 1 if bi == 0 else 0
            norm_block(xb[bi % 2], nt, ci, 2, hb[bi % 2], ss[bi % 2], xh, psT, junk)
            P.dma("sp", hTs.v(bi).ap(hT_view(t0 * 128, t0 * 128 + W)), hb[bi % 2][:, :, 0:W])

        blist = [bi for bi in range(len(BLOCKS)) if not (bi == 0 and not need_ctx)]
        oproj(blist[0])
        for i_, bi in enumerate(blist):
            if i_ + 1 < len(blist):
                oproj(blist[i_ + 1])
            onorm(bi)
        P.scope_end()

    def phase_ffn(l, fh, need_ctx, final):
        P.scope_begin()
        w1 = P.sbuf("w1", [128, 8, 2048], BF16)
        w2 = P.sbuf("w2", [128, 16, D], BF16)
        wcast(w1, w_ff1, l, fh * 2048, (fh + 1) * 2048)
        src2 = w_ff2.t.ap()[l].rearrange("(fc p) n -> p fc n", p=128)
        for fc0 in range(0, 16, 4):
            for hh in range(2):
                P.dma("pool", w2[:, fc0:fc0 + 4, hh * 512:(hh + 1) * 512],
                      w_ff2.v(l).ap(src2[:, fh * 16 + fc0:fh * 16 + fc0 + 4, hh * 512:(hh + 1) * 512]))
        hb = [P.sbuf(f"hbf{i}", [128, 8, 512], BF16) for i in range(2)]
        xb = [P.sbuf(f"xbf{i}", [128, 4, D], F32) for i in range(2)]
        aT = P.sbuf("aT", [128, 16, 512], BF16)
        r = [P.sbuf(f"r{i}", [128, 512], F32) for i in range(2)]
        t1 = [P.sbuf(f"t1f{i}", [128, 512], F32) for i in range(2)]
        gf = P.sbuf("gf", [128, D], F32)
        junk = P.sbuf("junkf", [128, D], BF16)
        ssf = P.sbuf("ssf", [128, 8], F32)
        ps1 = [P.psum(f"ps1{i}", [128, 512], F32) for i in range(3)]
        ps2 = [P.psum(f"ps2{i}", [128, 512], F32) for i in range(3)]
        if final:
            P.dma("sp", gf[:, :], gfin_d[:, :])
        k1 = 0
        k2 = 0
        for bi, (t0, nt) in enumerate(BLOCKS):
            if bi == 0 and not need_ctx:
                continue
            W = nt * 128
            ci = 1 if bi == 0 else 0
            h_ = hb[bi % 2]
            X = xb[bi % 2]
            P.dma("sp", h_[:, :, 0:W], hTs.v(bi).ap(hT_view(t0 * 128, t0 * 128 + W)))
            xdr = Xs.v(bi).ap(Xs.t.ap()[t0 * 128:(t0 + nt) * 128, :].rearrange("(n p) d -> p n d", p=128))
            P.dma("sp", X[:, 0:nt, :], xdr)
            for fc in range(16):
                ps = ps1[k1 % 3]
                r_ = r[k1 % 2]
                k1 += 1
                for kc in range(8):
                    P.mm(ps[:, 0:W], w1[:, kc, fc * 128:(fc + 1) * 128], h_[:, kc, 0:W], start=(kc == 0), stop=(kc == 7))
                P.act(r_[:, 0:W], ps[:, 0:W], AF.Relu)
                P.tt(aT[:, fc, 0:W], r_[:, 0:W], r_[:, 0:W], ALU.mult)
            for n in range(nt):
                for half in range(2):
                    ps = ps2[k2 % 3]
                    t_ = t1[k2 % 2]
                    k2 += 1
                    for fc in range(16):
                        P.mm(ps[:, :], aT[:, fc, n * 128:(n + 1) * 128], w2[:, fc, half * 512:(half + 1) * 512],
                             start=(fc == 0), stop=(fc == 15))
                    P.tt(t_[:, :], ps[:, :], gtb[:, 1, ci, half * 512:(half + 1) * 512], ALU.mult)
                    P.tt(X[:, n, half * 512:(half + 1) * 512], X[:, n, half * 512:(half + 1) * 512], t_[:, :], ALU.add,
                         eng="pool")
            if final:
                for n in range(nt):
                    P.act(junk[:, :], X[:, n, :], AF.Square, accum_out=ssf[:, n:n + 1])
                P.act(ssf[:, 4:4 + nt], ssf[:, 0:nt], AF.Ln, scale=1.0 / D, bias=EPS)
                P.act(ssf[:, 4:4 + nt], ssf[:, 4:4 + nt], AF.Exp, scale=-0.5)
                for n in range(nt):
                    P.stt(X[:, n, :], X[:, n, :], ssf[:, 4 + n:5 + n], gf[:, :], ALU.mult, ALU.mult)
                l0 = (t0 - 2) * 128
                P.dma("sp", out_d.v(bi).ap(out_d.t.ap()[l0:l0 + W, :].rearrange("(n p) d -> p n d", p=128)),
                      X[:, 0:nt, :])
            else:
                P.dma("sp", xdr, X[:, 0:nt, :])
        P.scope_end()

    P.barrier()
    done = False
    for l in range(nlayers):
        need_ctx = l < L - 1
        steps = [("M", lambda: phase_mod(l)), ("N", lambda: phase_norm(l)),
                 ("A", lambda: phase_attn(l, 0, need_ctx)), ("B", lambda: phase_attn(l, 1, need_ctx)),
                 ("C", lambda: phase_mlstm(l, need_ctx)), ("D", lambda: phase_ssd(l, need_ctx)),
                 ("O", lambda: phase_oproj(l, need_ctx)),
                 ("F0", lambda: phase_ffn(l, 0, need_ctx, False)),
                 ("F1", lambda: phase_ffn(l, 1, need_ctx, l == L - 1))]
        for name, fn in steps:
            fn()
            if stop == (l, name):
                done = True
                break
        if done:
            break
    P.finish()
    return nc, P


def _consts():
    s = np.arange(128)[:, None]
    t = np.arange(128)[None, :]
    negm = np.zeros((128, 2, 128), np.float32)
    negm[:, 0, :] = np.where(s <= t, 0.0, -1e9)
    negm[:, 1, :] = np.where(s >= t, 0.0, -1e9)
    tri = (negm == 0).astype(np.float32)
    sel = np.zeros((128, 128), np.float32)
    sel[64, :] = 1.0
    oblk = np.zeros((128, 128), np.float32)
    oblk[:64, :64] = 1.0
    oblk[64:, 64:] = 1.0
    eye8x = np.zeros((8, 8, 128), np.float32)
    for k in range(8):
        eye8x[k, k, :] = 1.0
    rows = NLAT // 64
    row = np.repeat(np.arange(rows, dtype=np.float32), 64)
    col = np.tile(np.arange(64, dtype=np.float32), rows)
    inv = (10000.0 ** (-np.arange(0, 32, 2, dtype=np.float32) / 32)).astype(np.float32)
    ang = np.concatenate([row[:, None] * inv, col[:, None] * inv], axis=-1).astype(np.float32)
    cos = np.concatenate([np.ones((NCTX, 32), np.float32), np.cos(ang).astype(np.float32)], 0)
    sin = np.concatenate([np.zeros((NCTX, 32), np.float32), np.sin(ang).astype(np.float32)], 0)
    rope = np.concatenate([cos, sin], -1).reshape(NT, 128, 64).transpose(1, 0, 2).reshape(128, NT * 64)
    wm = np.zeros((128, 6, 512), np.float32)
    q = np.arange(512)[None, :]
    for r in range(-1, 5):
        kpos = r * 128 + np.arange(128)[:, None]
        wm[:, r + 1, :] = np.where(np.abs(q - kpos) <= 128, 0.0, NEG)
    return dict(ident_f=np.eye(128, dtype=np.float32), negm=negm.reshape(128, 256), tri=tri.reshape(128, 256),
                sel65=sel, onesblk=oblk, eye8x=eye8x.reshape(8, 1024), rope=np.ascontiguousarray(rope),
                wmask=wm.reshape(128, 6 * 512))


def _colform(v):
    return np.ascontiguousarray(np.asarray(v, np.float32).reshape(-1, 128).T)


def _prep_shared(inp):
    f = lambda a: np.ascontiguousarray(np.asarray(a, np.float32))
    pcol = np.zeros((128, L * PC_N), np.float32)
    pbc = np.zeros((128, L * PB_N), np.float32)
    for l in range(L):
        o = l * PC_N
        pcol[:, o + PC_G1:o + PC_G1 + 8] = _colform(inp["g_norm1"][l])
        pcol[:, o + PC_G2:o + PC_G2 + 8] = _colform(inp["g_norm2"][l])
        for j in range(3):
            pcol[:, o + PC_CW + j * 6:o + PC_CW + j * 6 + 6] = _colform(inp["conv_w"][l][j])
        pcol[:, o + PC_CB:o + PC_CB + 6] = _colform(inp["conv_b"][l])
        pcol[:, o + PC_GML:o + PC_GML + 2] = _colform(inp["g_mlstm"][l])
        pcol[:, o + PC_GSSM:o + PC_GSSM + 2] = _colform(inp["g_ssm"][l])
        pcol[:, o + PC_DSK:o + PC_DSK + 2] = _colform(np.repeat(np.asarray(inp["d_skip"][l], np.float32), 64))
        o = l * PB_N
        pbc[:, o + PB_GQ:o + PB_GQ + 64] = np.asarray(inp["g_q_b"][l], np.float32)[None, :]
        pbc[:, o + PB_GK:o + PB_GK + 64] = np.asarray(inp["g_k_b"][l], np.float32)[None, :]
        pbc[:, o + PB_SINK:o + PB_SINK + 4] = np.asarray(inp["sink_a"][l], np.float32)[None, :]
        bi_ = np.asarray(inp["b_igate"][l], np.float32)
        bf_ = np.asarray(inp["b_fgate"][l], np.float32)
        pbc[:, o + PB_GB:o + PB_GB + 16] = np.concatenate([bi_[0], bf_[0], bi_[1], bf_[1]])[None, :]
        pbc[:, o + PB_ALOG:o + PB_ALOG + 8] = np.asarray(inp["a_log"][l], np.float32).reshape(-1)[None, :]
        pbc[:, o + PB_DTB:o + PB_DTB + 8] = np.asarray(inp["dt_bias"][l], np.float32).reshape(-1)[None, :]
    b_ada = np.asarray(inp["b_ada"], np.float32)
    bada_col = np.concatenate([_colform(b_ada[l]) for l in range(L)], axis=1)
    bada_gt = np.concatenate([np.concatenate([b_ada[l, 2 * D:3 * D], b_ada[l, 5 * D:6 * D]]) for l in range(L)])
    bada_gt = np.ascontiguousarray(np.broadcast_to(bada_gt[None, :], (128, L * 2 * D)))
    gfin = np.ascontiguousarray(np.broadcast_to(np.asarray(inp["g_final"], np.float32)[None, :], (128, D)))
    sh = dict(w_ada=f(inp["w_ada"]), bada_col=np.ascontiguousarray(bada_col), bada_gt=bada_gt, w_in=f(inp["w_in"]),
              w_out=f(inp["w_out"]), w_ff1=f(inp["w_ff1"]), w_ff2=f(inp["w_ff2"]), pcol=pcol, pbc=pbc, gfin=gfin)
    sh.update(_consts())
    return sh


def make_in_maps(inp, cores):
    sh = _prep_shared(inp)
    maps = []
    for b in cores:
        m = dict(sh)
        m["xin"] = np.ascontiguousarray(np.concatenate([np.asarray(inp["ctx"][b], np.float32),
                                                        np.asarray(inp["x"][b], np.float32)], 0))
        m["ccol"] = np.ascontiguousarray(np.concatenate([_colform(inp["c"][b]), _colform(inp["c_ctx"])], 1))
        maps.append(m)
    return maps


def kernel(**inputs):
    nc, _ = build_program()
    maps = make_in_maps(inputs, list(range(8)))
    res = run_bass_kernel_spmd(nc, maps, core_ids=list(range(8)))
    return np.stack([np.asarray(r["out"], np.float32) for r in res.results], 0)
```

```python
import numpy as np
import ml_dtypes
import concourse.bass as bass
import concourse.mybir as mybir
from concourse.bass_utils import run_bass_kernel_spmd

F32 = mybir.dt.float32
BF16 = mybir.dt.bfloat16
AF = mybir.ActivationFunctionType
ALU = mybir.AluOpType
AX = mybir.AxisListType

EPOCH = 12000
STRICT = False


class Reg:
    __slots__ = ("w", "r", "psum")

    def __init__(self, psum=False):
        self.w = None
        self.r = {}
        self.psum = psum


class View:
    __slots__ = ("reg", "ap")

    def __init__(self, reg, ap):
        self.reg = reg
        self.ap = ap


class _Keyed:
    def __init__(self, tile, key):
        self.tile = tile
        self.key = key

    def _reg(self):
        t = self.tile
        reg = t.regs.get(self.key)
        if reg is None:
            reg = t.regs[self.key] = Reg(t if t.is_psum else None)
        return reg

    def __getitem__(self, idx):
        return View(self._reg(), self.tile.t[idx])

    def ap(self, ap):
        return View(self._reg(), ap)


class Tile:
    def __init__(self, t, is_psum=False):
        self.t = t
        self.is_psum = is_psum
        self.bank_readers = {}
        self.regs = {}
        self.F = int(np.prod(t.shape[1:]))

    def v(self, key):
        return _Keyed(self, key)

    def __getitem__(self, idx):
        return _Keyed(self, None)[idx]

    def raw(self, p0, npart, off, dims, key=None):
        ap = bass.AP(self.t, p0 * self.F + off, [[self.F, npart]] + [list(d) for d in dims])
        return _Keyed(self, key).ap(ap)


class Prog:
    def __init__(self, nc):
        self.nc = nc
        self.eng = {"pe": nc.tensor, "act": nc.scalar, "dve": nc.vector, "pool": nc.gpsimd, "sp": nc.sync}
        self.seq = {e: 0 for e in self.eng}
        self.sems = {e: [] for e in self.eng}
        self.seen = {e: {} for e in self.eng}
        self.dma_pool = {}
        self._cms = []
        self._scopes = []
        self.ninst = 0
        for q, n in (("sp", 16), ("pool", 8), ("act", 4)):
            sl = []
            for i in range(n):
                sl.append([self._sem(f"d_{q}_{i}"), 0])
            self.dma_pool[q] = [sl, 0]

    def _sem(self, name):
        cm = self.nc.semaphore(name)
        s = cm.__enter__()
        self._cms.append(cm)
        return s

    def _alloc(self, cm, is_psum=False):
        t = cm.__enter__()
        if self._scopes:
            self._scopes[-1].append(cm)
        else:
            self._cms.append(cm)
        return Tile(t, is_psum)

    def sbuf(self, name, shape, dtype):
        return self._alloc(self.nc.sbuf_tensor(name + f"_{self.ninst}", list(shape), dtype))

    def psum(self, name, shape, dtype=F32):
        return self._alloc(self.nc.psum_tensor(name + f"_{self.ninst}", list(shape), dtype), True)

    def dram(self, name, shape, dtype, kind="Internal"):
        t = self.nc.dram_tensor(name, list(shape), dtype, kind=kind)
        return Tile(t)

    def scope_begin(self):
        self._scopes.append([])

    def scope_end(self):
        self.barrier()
        cms = self._scopes.pop()
        for cm in reversed(cms):
            cm.__exit__(None, None, None)

    def _esem(self, e, seq):
        ep = (seq - 1) // EPOCH
        while len(self.sems[e]) <= ep:
            self.sems[e].append(self._sem(f"s_{e}_{len(self.sems[e])}"))
        return self.sems[e][ep], (seq - 1) % EPOCH + 1

    def _need(self, e, dep):
        if dep[0] == "e":
            _, e2, seq = dep
            key = ("e", e2)
            if self.seen[e].get(key, 0) >= seq:
                return None
            self.seen[e][key] = seq
            return self._esem(e2, seq)
        _, q, si, val = dep
        key = ("d", q, si)
        if self.seen[e].get(key, 0) >= val:
            return None
        self.seen[e][key] = val
        return (self.dma_pool[q][0][si][0], val)

    def _wait(self, e, dep):
        n = self._need(e, dep)
        if n is not None:
            self.eng[e].wait_ge(n[0], n[1])
            self.ninst += 1

    def _deps(self, e, outs, ins):
        deps = []
        for v in ins:
            if v.reg.w is not None:
                deps.append(v.reg.w)
            if v.reg.psum is not None:
                for e2, val in v.reg.psum.bank_readers.items():
                    if e2 != e:
                        deps.append(("e", e2, val))
        for v in outs:
            w = v.reg.w
            if w is not None:
                if not (w[0] == "e" and w[1] == e and (e == "pe" or not STRICT)):
                    deps.append(w)
            for k, val in v.reg.r.items():
                if k[0] == "e":
                    if k[1] == e and not STRICT:
                        continue
                    deps.append(("e", k[1], val))
                else:
                    deps.append(("d", k[1], k[2], val))
        best = {}
        for d in deps:
            k = d[:2] if d[0] == "e" else d[:3]
            if k not in best or d[-1] > best[k][-1]:
                best[k] = d
        needs = []
        for d in best.values():
            n = self._need(e, d)
            if n is not None:
                needs.append(n)
        return needs

    def op(self, e, fn, outs, ins, embed=True):
        self.nops = getattr(self, 'nops', 0) + 1
        if self.nops > DBG.get('maxops', 10 ** 9):
            return None
        needs = self._deps(e, outs, ins)
        emb = None
        if embed and e != "pe" and needs:
            emb = needs.pop()
        for sem, val in needs:
            self.eng[e].wait_ge(sem, val)
            self.ninst += 1
        inst = fn()
        if emb is not None:
            inst._wait_ge(emb[0], emb[1])
        self.seq[e] += 1
        seq = self.seq[e]
        sem, val = self._esem(e, seq)
        inst.then_inc(sem, 1)
        self.ninst += 1
        me = ("e", e, seq)
        for v in ins:
            v.reg.r[("e", e)] = seq
            if v.reg.psum is not None:
                v.reg.psum.bank_readers[e] = seq
        for v in outs:
            v.reg.w = me
            v.reg.r = {}
        return inst

    def dma(self, q, out, in_, **kw):
        e = q
        if getattr(self, 'nops', 0) > DBG.get('maxops', 10 ** 9):
            return None
        for sem_, val_ in self._deps(e, [out], [in_]):
            self.eng[e].wait_ge(sem_, val_)
            self.ninst += 1
        pool = self.dma_pool[q]
        si = pool[1] % len(pool[0])
        pool[1] += 1
        slot = pool[0][si]
        if slot[1] > 0:
            self._wait(e, ("d", q, si, slot[1]))
        slot[1] += 16
        inst = self.eng[e].dma_start(out=out.ap, in_=in_.ap, **kw)
        inst.then_inc(slot[0], 16)
        self.ninst += 1
        in_.reg.r[("d", q, si)] = slot[1]
        out.reg.w = ("d", q, si, slot[1])
        out.reg.r = {}
        return inst

    def barrier(self):
        for e in self.eng:
            for q, (sl, _) in self.dma_pool.items():
                for si, (sem, val) in enumerate(sl):
                    if val > 0:
                        self._wait(e, ("d", q, si, val))
            for e2 in self.eng:
                if self.seq[e2] > 0:
                    self._wait(e, ("e", e2, self.seq[e2]))

    def finish(self):
        self.barrier()

    def mm(self, out, lhsT, rhs, start=True, stop=True):
        return self.op("pe", lambda: self.nc.tensor.matmul(out.ap, lhsT.ap, rhs.ap, start=start, stop=stop),
                       [out], [lhsT, rhs])

    def tr(self, out, in_, ident):
        return self.op("pe", lambda: self.nc.tensor.transpose(out.ap, in_.ap, ident.ap), [out], [in_, ident])

    def act(self, out, in_, func, bias=None, scale=1.0, accum_out=None):
        ins = [in_]
        kw = {}
        if bias is not None:
            if isinstance(bias, View):
                ins.append(bias)
                kw["bias"] = bias.ap
            else:
                kw["bias"] = bias
        if isinstance(scale, View):
            ins.append(scale)
            kw["scale"] = scale.ap
        else:
            kw["scale"] = scale
        outs = [out]
        if accum_out is not None:
            outs.append(accum_out)
            kw["accum_out"] = accum_out.ap
        return self.op("act", lambda: self.nc.scalar.activation(out=out.ap, in_=in_.ap, func=func, **kw), outs, ins,
                       embed=(accum_out is None))

    def tt(self, out, in0, in1, op, eng="dve"):
        E = self.eng[eng]
        return self.op(eng, lambda: E.tensor_tensor(out=out.ap, in0=in0.ap, in1=in1.ap, op=op), [out], [in0, in1])

    def ts(self, out, in0, s1, op0, s2=None, op1=None, eng="dve"):
        E = self.eng[eng]
        ins = [in0]
        a1, a2 = s1, s2
        if isinstance(s1, View):
            ins.append(s1)
            a1 = s1.ap
        if isinstance(s2, View):
            ins.append(s2)
            a2 = s2.ap
        kw = {}
        if op1 is not None:
            kw["op1"] = op1
        return self.op(eng, lambda: E.tensor_scalar(out=out.ap, in0=in0.ap, scalar1=a1, scalar2=a2, op0=op0, **kw),
                       [out], ins)

    def stt(self, out, in0, s, in1, op0, op1):
        E = self.nc.vector
        ins = [in0, in1]
        a = s
        if isinstance(s, View):
            ins.append(s)
            a = s.ap
        return self.op("dve", lambda: E.scalar_tensor_tensor(out=out.ap, in0=in0.ap, scalar=a, in1=in1.ap,
                                                              op0=op0, op1=op1), [out], ins)

    def copy(self, out, in_, eng="dve"):
        if eng == "act":
            return self.op("act", lambda: self.nc.scalar.copy(out=out.ap, in_=in_.ap), [out], [in_])
        E = self.eng[eng]
        return self.op(eng, lambda: E.tensor_copy(out=out.ap, in_=in_.ap), [out], [in_])

    def memset(self, out, val, eng="dve"):
        E = self.eng[eng]
        return self.op(eng, lambda: E.memset(out.ap, val), [out], [])

    def recip(self, out, in_):
        return self.op("dve", lambda: self.nc.vector.reciprocal(out=out.ap, in_=in_.ap), [out], [in_])

    def reduce(self, out, in_, op, axis=AX.X):
        return self.op("dve", lambda: self.nc.vector.tensor_reduce(out=out.ap, in_=in_.ap, axis=axis, op=op),
                       [out], [in_])

    def scan(self, out, d0, d1, initial, op0, op1):
        ins = [d0, d1]
        a = initial
        if isinstance(initial, View):
            ins.append(initial)
            a = initial.ap
        return self.op("dve", lambda: self.nc.vector.tensor_tensor_scan(out=out.ap, data0=d0.ap, data1=d1.ap,
                                                                        initial=a, op0=op0, op1=op1), [out], ins)


L = 2
D = 1024
NCTX = 256
NLAT = 4096
T = NCTX + NLAT
NT = T // 128
NIN = 3096
DFF = 4096
EPS = 1e-6
BLOCKS = [(0, 2)] + [(2 + 4 * j, 4) for j in range(8)]
NEG = -30000.0
DBG = {}

PC_G1, PC_G2, PC_CW, PC_CB, PC_GML, PC_GSSM, PC_DSK, PC_N = 0, 8, 16, 34, 40, 42, 44, 46
PB_GQ, PB_GK, PB_SINK, PB_GB, PB_ALOG, PB_DTB, PB_N = 0, 64, 128, 132, 148, 156, 164


def build_program(nlayers=L, stop=None, dbg=False):
    nc = bass.Bass("TRN2", target_bir_lowering=False)
    P = Prog(nc)
    EI = "ExternalInput"
    xin = P.dram("xin", [T, D], F32, EI)
    ccol_d = P.dram("ccol", [128, 16], F32, EI)
    w_ada = P.dram("w_ada", [L, D, 6 * D], F32, EI)
    badac_d = P.dram("bada_col", [128, L * 48], F32, EI)
    badag_d = P.dram("bada_gt", [128, L * 2 * D], F32, EI)
    w_in = P.dram("w_in", [L, D, NIN], F32, EI)
    w_out = P.dram("w_out", [L, D, D], F32, EI)
    w_ff1 = P.dram("w_ff1", [L, D, DFF], F32, EI)
    w_ff2 = P.dram("w_ff2", [L, DFF, D], F32, EI)
    pcol_d = P.dram("pcol", [128, L * PC_N], F32, EI)
    pbc_d = P.dram("pbc", [128, L * PB_N], F32, EI)
    gfin_d = P.dram("gfin", [128, D], F32, EI)
    identf_d = P.dram("ident_f", [128, 128], F32, EI)
    negm_d = P.dram("negm", [128, 2 * 128], F32, EI)
    tri_d = P.dram("tri", [128, 2 * 128], F32, EI)
    sel_d = P.dram("sel65", [128, 128], F32, EI)
    oblk_d = P.dram("onesblk", [128, 128], F32, EI)
    eye_d = P.dram("eye8x", [8, 8 * 128], F32, EI)
    rope_d = P.dram("rope", [128, NT * 64], F32, EI)
    wmask_d = P.dram("wmask", [128, 6 * 512], F32, EI)
    out_d = P.dram("out", [NLAT, D], F32, "ExternalOutput")
    okind = "ExternalOutput" if dbg else "Internal"
    Xs = P.dram("Xs", [T, D], F32, okind)
    hTs = P.dram("hTs", [D, T], BF16, okind)
    yTs = P.dram("yTs", [D, T], BF16, okind)

    ident_f = P.sbuf("ident_f", [128, 128], F32)
    ident_b = P.sbuf("ident_b", [128, 128], BF16)
    ones_f = P.sbuf("ones_f", [128, 128], F32)
    zeros_f = P.sbuf("zeros_f", [128, 128], F32)
    negm = P.sbuf("negm", [128, 2, 128], F32)
    tri = P.sbuf("tri", [128, 2, 128], F32)
    sel65 = P.sbuf("sel65", [128, 128], F32)
    onesblk = P.sbuf("onesblk", [128, 128], F32)
    eye8x = P.sbuf("eye8x", [8, 8, 128], F32)
    pcol = P.sbuf("pcol", [128, L * PC_N], F32)
    pbc = P.sbuf("pbc", [128, L * PB_N], F32)
    ccol = P.sbuf("ccol", [128, 16], F32)
    badac = P.sbuf("badac", [128, L * 48], F32)
    modv = P.sbuf("modv", [128, 4, 8, 2], F32)
    gtb = P.sbuf("gtb", [128, 2, 2, D], F32)
    esink = P.sbuf("esink", [128, L * 4], F32)
    abc = P.sbuf("abc", [128, L * 8], F32)

    P.dma("sp", ident_f[:, :], identf_d[:, :])
    P.dma("pool", ident_b[:, :], identf_d[:, :])
    P.dma("sp", negm[:, :, :], negm_d.raw(0, 128, 0, [[128, 2], [1, 128]]))
    P.dma("sp", tri[:, :, :], tri_d.raw(0, 128, 0, [[128, 2], [1, 128]]))
    P.dma("sp", sel65[:, :], sel_d[:, :])
    P.dma("sp", onesblk[:, :], oblk_d[:, :])
    P.dma("sp", eye8x[:, :, :], eye_d.raw(0, 8, 0, [[128, 8], [1, 128]]))
    P.dma("sp", pcol[:, :], pcol_d[:, :])
    P.dma("sp", pbc[:, :], pbc_d[:, :])
    P.dma("sp", ccol[:, :], ccol_d[:, :])
    P.dma("sp", badac[:, :], badac_d[:, :])
    P.memset(ones_f[:, :], 1.0)
    P.memset(zeros_f[:, :], 0.0)
    for l in range(L):
        P.act(esink[:, l * 4:(l + 1) * 4], pbc[:, l * PB_N + PB_SINK:l * PB_N + PB_SINK + 4], AF.Exp)
        P.act(abc[:, l * 8:(l + 1) * 8], pbc[:, l * PB_N + PB_ALOG:l * PB_N + PB_ALOG + 8], AF.Exp)
        P.ts(abc[:, l * 8:(l + 1) * 8], abc[:, l * 8:(l + 1) * 8], -1.0, ALU.mult)

    def dview(tile, key, ap):
        return tile.v(key).ap(ap)

    def hT_view(c0, c1):
        return hTs.t.ap().rearrange("(kc p) t -> p kc t", p=128)[:, :, c0:c1]

    def wcast(dst, wt, l, c0, c1, rows=D):
        src = wt.t.ap()[l].rearrange("(kc p) n -> p kc n", p=128)
        npc = (c1 - c0 + 511) // 512
        step = (c1 - c0 + npc - 1) // npc
        for a in range(c0, c1, step):
            b = min(c1, a + step)
            P.dma("pool", dst.v(None).ap(dst.t[:, :, a - c0:b - c0]), wt.v(l).ap(src[:, :, a:b]))

    def phase_mod(l):
        P.scope_begin()
        wA = [P.sbuf(f"wA{i}", [128, 8, 512], BF16) for i in range(2)]
        sc = P.sbuf("sc", [128, 16], F32)
        scb = P.sbuf("scb", [128, 16], BF16)
        screp = P.sbuf("screp", [128, 16, 128], BF16)
        bgt = P.sbuf("bgt", [128, 2, D], F32)
        mcol = P.sbuf("mcol", [128, 48, 2], F32)
        psC = P.psum("psC", [128, 96], F32)
        psB = [P.psum(f"psBm{i}", [128, 512], F32) for i in range(2)]
        P.dma("sp", bgt[:, :, :], badag_d.raw(0, 128, l * 2 * D, [[D, 2], [1, D]]))
        P.act(sc[:, :], ccol[:, :], AF.Silu)
        P.copy(scb[:, :], sc[:, :])
        P.copy(screp[:, :, :], sc.raw(0, 128, 0, [[1, 16], [0, 128]]))
        src = w_ada.t.ap()[l].rearrange("(kc p) n -> p kc n", p=128)
        nb = 0
        for pc in range(12):
            w = wA[pc % 2]
            P.dma("pool", w[:, :, :], w_ada.v(l).ap(src[:, :, pc * 512:(pc + 1) * 512]))
            which = pc // 2
            if which in (2, 5):
                for wi in range(2):
                    ps = psB[nb % 2]
                    nb += 1
                    for kc in range(8):
                        P.mm(ps[:, :], screp[:, wi * 8 + kc, :], w[:, kc, :], start=(kc == 0), stop=(kc == 7))
                    gi = 0 if which == 2 else 1
                    half = pc % 2
                    P.tt(gtb[:, gi, wi, half * 512:(half + 1) * 512], ps[:, :], bgt[:, gi, half * 512:(half + 1) * 512],
                         ALU.add)
            else:
                for cc in range(4):
                    ch = pc * 4 + cc
                    for kc in range(8):
                        P.mm(psC[:, ch * 2:ch * 2 + 2], w[:, kc, cc * 128:(cc + 1) * 128],
                             scb.raw(0, 128, kc, [[8, 2]]), start=(kc == 0), stop=(kc == 7))
        for which in (0, 1, 3, 4):
            c0 = which * 8
            P.tt(mcol[:, c0:c0 + 8, :], psC.raw(0, 128, c0 * 2, [[2, 8], [1, 2]]),
                 badac.raw(0, 128, l * 48 + c0, [[1, 8], [0, 2]]), ALU.add)
        for k, (gcol, scw, shw) in enumerate(((PC_G1, 1, 0), (PC_G2, 4, 3))):
            gap = pcol.raw(0, 128, l * PC_N + gcol, [[1, 8], [0, 2]])
            P.stt(modv[:, 2 * k, :, :], mcol[:, scw * 8:scw * 8 + 8, :], 1.0, gap, ALU.add, ALU.mult)
            P.copy(modv[:, 2 * k + 1, :, :], mcol[:, shw * 8:shw * 8 + 8, :])
        P.scope_end()

    def norm_block(Xtile, nt, ci, mi, hb, ss, xh, psT, junk):
        for n in range(nt):
            P.act(junk[:, :], Xtile[:, n, :], AF.Square, accum_out=ss[:, n:n + 1])
        P.act(ss[:, 4:4 + nt], ss[:, 0:nt], AF.Ln, scale=1.0 / D, bias=EPS)
        P.act(ss[:, 4:4 + nt], ss[:, 4:4 + nt], AF.Exp, scale=-0.5)
        def evac(n):
            pst = psT[n % 2]
            for kc in range(8):
                P.ts(hb[:, kc, n * 128:(n + 1) * 128], pst[:, kc * 128:(kc + 1) * 128],
                     modv[:, mi, kc, ci:ci + 1], ALU.mult, modv[:, mi + 1, kc, ci:ci + 1], ALU.add)

        for n in range(nt):
            x_ = xh[n % 2]
            P.ts(x_[:, :], Xtile[:, n, :], ss[:, 4 + n:5 + n], ALU.mult)
            pst = psT[n % 2]
            for kc in range(8):
                P.tr(pst[:, kc * 128:(kc + 1) * 128], x_[:, kc * 128:(kc + 1) * 128], ident_f[:, :])
            if n > 0:
                evac(n - 1)
        evac(nt - 1)

    def x_src(l, bi):
        t0, nt = BLOCKS[bi]
        src = xin if l == 0 else Xs
        return src.v(bi).ap(src.t.ap()[t0 * 128:(t0 + nt) * 128, :].rearrange("(n p) d -> p n d", p=128))

    def phase_norm(l):
        P.scope_begin()
        xb = [P.sbuf(f"xb{i}", [128, 4, D], F32) for i in range(2)]
        junk = P.sbuf("junk", [128, D], BF16)
        xh = [P.sbuf(f"xh{i}", [128, D], F32) for i in range(2)]
        hb = [P.sbuf(f"hb{i}", [128, 8, 512], BF16) for i in range(2)]
        ss = [P.sbuf(f"ss{i}", [128, 8], F32) for i in range(2)]
        psT = [P.psum(f"psTn{i}", [128, D], F32) for i in range(2)]
        P.dma("sp", xb[0][:, 0:BLOCKS[0][1], :], x_src(l, 0))
        for bi, (t0, nt) in enumerate(BLOCKS):
            W = nt * 128
            X = xb[bi % 2]
            if bi + 1 < len(BLOCKS):
                P.dma("sp", xb[(bi + 1) % 2][:, 0:BLOCKS[bi + 1][1], :], x_src(l, bi + 1))
            norm_block(X, nt, 1 if bi == 0 else 0, 0, hb[bi % 2], ss[bi % 2], xh, psT, junk)
            P.dma("sp", hTs.v(bi).ap(hT_view(t0 * 128, t0 * 128 + W)), hb[bi % 2][:, :, 0:W])
        P.scope_end()

    def phase_attn(l, mixer, need_ctx):
        P.scope_begin()
        cbase = 0 if mixer == 0 else 512
        wq = P.sbuf("wq", [128, 8, 512], BF16)
        rope = P.sbuf("rope", [128, NT, 64], F32)
        P.dma("sp", rope[:, :, :], rope_d.raw(0, 128, 0, [[64, NT], [1, 64]]))
        wcast(wq, w_in, l, cbase, cbase + 512)
        qT = P.sbuf("qT", [128, 2, T], BF16)
        kTA = P.sbuf("kTA", [128, 2, T], BF16)
        kTB = P.sbuf("kTB", [128, 2, T], BF16)
        va = P.sbuf("va", [128, NT, 2, 65], BF16)
        hb = [P.sbuf(f"hba{i}", [128, 8, 512], BF16) for i in range(2)]
        sq = P.sbuf("sq", [128, 384], F32)
        st6 = P.sbuf("st6", [128, 12], F32)
        qk = [P.sbuf(f"qk{i}", [128, 384], F32) for i in range(2)]
        rt = [P.sbuf(f"rt{i}", [128, 192], F32) for i in range(4)]
        qkr = [P.sbuf(f"qkr{i}", [128, 384], BF16) for i in range(2)]
        kd = [P.sbuf(f"kd{i}", [128, 2, 2, 64], BF16) for i in range(2)]
        wmask = P.sbuf("wmask", [128, 6, 512], BF16)
        pT = [P.sbuf(f"pT{i}", [128, 512], BF16) for i in range(4)]
        osb = [P.sbuf(f"osb{i}", [65, 512], F32) for i in range(2)]
        rec = [P.sbuf(f"rec{i}", [64, 512], F32) for i in range(2)]
        yb = [P.sbuf(f"yb{i}", [128, 512], BF16) for i in range(2)]
        P.scope_begin()
        psP = [P.psum(f"psP{i}", [128, 512], F32) for i in range(2)]
        psTt = P.psum("psTt", [128, 4, 128], BF16)
        if mixer == 0:
            P.dma("pool", wmask[:, :, :], wmask_d.raw(0, 128, 0, [[512, 6], [1, 512]]))
        for ti in range(NT):
            P.memset(va.v(ti)[:, ti, :, 64:65], 1.0, eng="pool")
        P.memset(kTA[64:128, :, :], 0.0)
        P.memset(kTB[0:64, :, :], 0.0, eng="pool")
        P.barrier()
        pb0 = l * PB_N
        tiles = [(bi, t0, nt, n) for bi, (t0, nt) in enumerate(BLOCKS) for n in range(nt)]

        def proj(idx):
            bi, t0, nt, n = tiles[idx]
            W = nt * 128
            h_ = hb[bi % 2]
            if n == 0:
                P.dma("sp", h_[:, :, 0:W], hTs.v(bi).ap(hT_view(t0 * 128, t0 * 128 + W)))
            ps = psP[(t0 + n) % 2]
            for kc in range(8):
                P.mm(ps[:, :], h_[:, kc, n * 128:(n + 1) * 128], wq[:, kc, :], start=(kc == 0), stop=(kc == 7))

        def post(idx):
            bi, t0, nt, n = tiles[idx]
            ti = t0 + n
            col0 = ti * 128
            ps = psP[ti % 2]
            q_ = qk[ti % 2]
            if mixer == 1:
                P.act(sq[:, :], ps[:, 0:384], AF.Square)
                P.reduce(st6[:, 0:6], sq.raw(0, 128, 0, [[64, 6], [1, 64]]), ALU.add)
                P.act(st6[:, 6:12], st6[:, 0:6], AF.Ln, scale=1.0 / 64, bias=EPS)
                P.act(st6[:, 6:12], st6[:, 6:12], AF.Exp, scale=-0.5)
                P.tt(q_.raw(0, 128, 0, [[64, 6], [1, 64]]), ps.raw(0, 128, 0, [[64, 6], [1, 64]]),
                     st6.raw(0, 128, 6, [[1, 6], [0, 64]]), ALU.mult)
                P.tt(q_.raw(0, 128, 0, [[64, 4], [1, 64]]), q_.raw(0, 128, 0, [[64, 4], [1, 64]]),
                     pbc.raw(0, 128, pb0 + PB_GQ, [[0, 4], [1, 64]]), ALU.mult)
                P.tt(q_.raw(0, 128, 256, [[64, 2], [1, 64]]), q_.raw(0, 128, 256, [[64, 2], [1, 64]]),
                     pbc.raw(0, 128, pb0 + PB_GK, [[0, 2], [1, 64]]), ALU.mult)
            else:
                P.copy(q_[:, :], ps[:, 0:384], eng="act")
            hd = [[64, 6], [32, 2], [1, 16]]
            x1 = q_.raw(0, 128, 0, hd)
            x2 = q_.raw(0, 128, 16, hd)
            cs = rope.raw(0, 128, ti * 64, [[0, 6], [16, 2], [1, 16]])
            sn = rope.raw(0, 128, ti * 64 + 32, [[0, 6], [16, 2], [1, 16]])
            r_ = qkr[ti % 2]
            o1 = r_.raw(0, 128, 0, hd)
            o2 = r_.raw(0, 128, 16, hd)
            fl = [[32, 6], [16, 2], [1, 16]]
            ta, tb_, tc, td = (rt[i].raw(0, 128, 0, fl) for i in range(4))
            P.tt(ta, x1, cs, ALU.mult)
            P.tt(tb_, x2, sn, ALU.mult)
            P.tt(tc, x2, cs, ALU.mult)
            P.tt(td, x1, sn, ALU.mult)
            P.tt(o1, ta, tb_, ALU.subtract)
            P.tt(o2, tc, td, ALU.add)
            kd_ = kd[ti % 2]
            P.copy(kd_[:, :, :, :], r_.raw(0, 128, 256, [[64, 2], [0, 2], [1, 64]]), eng="act")
            P.tr(psTt[:, 0, :], r_[:, 0:128], ident_b[:, :])
            P.tr(psTt[:, 1, :], r_[:, 128:256], ident_b[:, :])
            P.tr(psTt[:, 2, :], kd_.raw(0, 128, 0, [[1, 128]]), ident_b[:, :])
            P.tr(psTt[:, 3, :], kd_.raw(0, 128, 128, [[1, 128]]), ident_b[:, :])
            P.copy(qT.v(ti)[:, :, col0:col0 + 128], psTt[:, 0:2, :], eng="act")
            P.copy(kTA.v(ti)[0:64, :, col0:col0 + 128], psTt[0:64, 2:4, :], eng="act")
            P.copy(kTB.v(ti)[64:128, :, col0:col0 + 128], psTt[64:128, 2:4, :])
            P.copy(va.v(ti)[:, ti, :, 0:64], ps.raw(0, 128, 384, [[64, 2], [1, 64]]))

        proj(0)
        for idx in range(len(tiles)):
            if idx + 1 < len(tiles):
                proj(idx + 1)
            post(idx)
        P.scope_end()
        psS = [P.psum(f"psS{i}", [128, 512], F32) for i in range(4)]
        psO = [P.psum(f"psO{i}", [128, 512], F32) for i in range(2)]
        psD = P.psum("psD", [128, 512], F32)
        unit = 0
        scnt = [0]
        pending = [None]
        for bi, (t0, nt) in enumerate(BLOCKS):
            if bi == 0 and not need_ctx:
                continue
            W = nt * 128
            qc0 = t0 * 128
            if bi == 0:
                keys = [(0, None), (1, None)]
            elif mixer == 1:
                keys = [(kt, None) for kt in range(NT)]
            else:
                keys = [(0, None), (1, None)]
                for kt in range(max(2, t0 - 1), min(NT - 1, t0 + 4) + 1):
                    keys.append((kt, kt - t0 + 1))
            for h in range(4):
                g = h // 2
                r0 = (h % 2) * 64
                po = psO[unit % 2]
                nk = len(keys)
                rq = [qT.v(t0 + n)[r0:r0 + 64, g, qc0:qc0 + W] for n in range(nt)]
                sidx = {}

                def emit_S(ki, g=g, r0=r0, W=W, qc0=qc0, keys=keys, rq=rq, sidx=sidx):
                    kt, mk = keys[ki]
                    si = scnt[0]
                    scnt[0] += 1
                    sidx[ki] = si
                    ps = psS[si % len(psS)]
                    kTx = kTA if r0 == 0 else kTB
                    P.op("pe", lambda: nc.tensor.matmul(
                        ps.t[:, 0:W], kTx.t[:, g, kt * 128:(kt + 1) * 128], qT.t[:, g, qc0:qc0 + W],
                        start=True, stop=(mk is None)), [ps[:, 0:W]], [kTx.v(kt)[:, g, 0:1]] + rq)
                    if mk is not None:
                        P.mm(ps[:, 0:W], ident_b[:, :], wmask[:, mk, 0:W], start=False, stop=True)

                def emit_rest(ki, g=g, W=W, keys=keys, po=po, nk=nk, sidx=sidx):
                    kt, mk = keys[ki]
                    si = sidx[ki]
                    ps = psS[si % len(psS)]
                    p_ = pT[si % len(pT)]
                    P.act(p_[:, 0:W], ps[:, 0:W], AF.Exp, scale=0.125)
                    P.op("pe", lambda: nc.tensor.matmul(
                        po.t[0:65, 0:W], va.t[:, kt, g, :], p_.t[:, 0:W], start=(ki == 0), stop=(ki == nk - 1)),
                        [po[0:65, 0:W]], [va.v(kt)[:, kt, g, :], p_[:, 0:W]])

                def finalize(unit=unit, h=h, g=g, r0=r0, W=W, qc0=qc0, po=po, bi=bi):
                    o_ = osb[unit % 2]
                    rc = rec[unit % 2]
                    P.copy(o_[0:65, 0:W], po[0:65, 0:W])
                    P.mm(psD[:, 0:W], sel65[0:65, :], o_[0:65, 0:W])
                    if mixer == 0:
                        P.ts(rc[0:64, 0:W], psD[0:64, 0:W], esink[0:64, l * 4 + h:l * 4 + h + 1], ALU.add)
                        P.recip(rc[0:64, 0:W], rc[0:64, 0:W])
                    else:
                        P.recip(rc[0:64, 0:W], psD[0:64, 0:W])
                    y_ = yb[(unit // 2) % 2]
                    P.tt(y_[r0:r0 + 64, 0:W], o_[0:64, 0:W], rc[0:64, 0:W], ALU.mult)
                    if h % 2 == 1:
                        row0 = mixer * 256 + g * 128
                        P.dma("sp", yTs.v((bi, mixer * 2 + g)).ap(yTs.t.ap()[row0:row0 + 128, qc0:qc0 + W]), y_[:, 0:W])

                LA = 2
                for ki in range(min(LA, nk)):
                    emit_S(ki)
                for ki in range(nk):
                    if ki + LA < nk:
                        emit_S(ki + LA)
                    emit_rest(ki)
                    if ki == 1 and pending[0] is not None:
                        pending[0]()
                        pending[0] = None
                if pending[0] is not None:
                    pending[0]()
                pending[0] = finalize
                unit += 1
        if pending[0] is not None:
            pending[0]()
        P.scope_end()

    def phase_mlstm(l, need_ctx):
        P.scope_begin()
        wC = P.sbuf("wC", [128, 8, 1040], BF16)
        wcast(wC, w_in, l, 1024, 2064)
        qT = P.sbuf("qTc", [128, 2, T], BF16)
        kT = P.sbuf("kTc", [128, 2, T], BF16)
        ktok = P.sbuf("ktok", [128, NT, 256], BF16)
        va = P.sbuf("vac", [128, NT, 4, 65], BF16)
        sigo = P.sbuf("sigo", [128, 2, T], BF16)
        hF = P.sbuf("hF", [128, 2, T], BF16)
        ig = P.sbuf("ig", [128, NT, 8], F32)
        lf = P.sbuf("lf", [128, NT, 8], F32)
        mst = P.sbuf("mst", [128, 8], F32)
        Cf = P.sbuf("Cf", [128, 8, 65], F32)
        Cb = P.sbuf("Cb", [128, 8, 65], BF16)
        P.scope_begin()
        hb = [P.sbuf(f"hbc{i}", [128, 8, 512], BF16) for i in range(2)]
        qkb = [P.sbuf(f"qkb{i}", [128, 512], BF16) for i in range(2)]
        gpre = P.sbuf("gpre", [128, 4, 16], F32)
        ge = P.sbuf("ge", [128, 4, 8], F32)
        psA = [P.psum(f"psA{i}", [128, 512], F32) for i in range(2)]
        psTt = P.psum("psTtc", [128, 4, 128], BF16)
        for ti in range(NT):
            P.memset(va.v(ti)[:, ti, :, 64:65], 1.0, eng="pool")
        pb0 = l * PB_N
        for bi, (t0, nt) in enumerate(BLOCKS):
            W = nt * 128
            h_ = hb[bi % 2]
            P.dma("sp", h_[:, :, 0:W], hTs.v(bi).ap(hT_view(t0 * 128, t0 * 128 + W)))
            for n in range(nt):
                ti = t0 + n
                col0 = ti * 128
                p1, p2 = psA[0], psA[1]
                for kc in range(8):
                    P.mm(p1[:, :], h_[:, kc, n * 128:(n + 1) * 128], wC[:, kc, 0:512], start=(kc == 0), stop=(kc == 7))
                for kc in range(8):
                    P.mm(p2[:, 0:256], h_[:, kc, n * 128:(n + 1) * 128], wC[:, kc, 512:768], start=(kc == 0),
                         stop=(kc == 7))
                for kc in range(8):
                    P.mm(p2[:, 256:272], h_[:, kc, n * 128:(n + 1) * 128], wC[:, kc, 1024:1040], start=(kc == 0),
                         stop=(kc == 7))
                qb = qkb[ti % 2]
                P.copy(qb[:, :], p1[:, :], eng="act")
                P.copy(ktok.v(ti)[:, ti, :], qb[:, 256:512], eng="pool")
                for i in range(4):
                    P.tr(psTt[:, i, :], qb[:, i * 128:(i + 1) * 128], ident_b[:, :])
                P.ts(qT.v(ti)[:, :, col0:col0 + 128], psTt[:, 0:2, :], 0.125, ALU.mult)
                P.copy(kT.v(ti)[:, :, col0:col0 + 128], psTt[:, 2:4, :])
                P.copy(va.v(ti)[:, ti, :, 0:64], p2.raw(0, 128, 0, [[64, 4], [1, 64]]))
                P.tt(gpre[:, n, :], p2[:, 256:272], pbc[:, pb0 + PB_GB:pb0 + PB_GB + 16], ALU.add)
            if DBG.get("c_skipg", 0):
                continue
            P.copy(ig.v(bi).ap(ig.t[:, t0:t0 + nt, :].rearrange("p n (d h) -> p n d h", d=2)),
                   gpre.raw(0, 128, 0, [[16, nt], [8, 2], [1, 4]]))
            P.act(ge.raw(0, 128, 0, [[8, nt], [4, 2], [1, 4]]), gpre.raw(0, 128, 4, [[16, nt], [8, 2], [1, 4]]),
                  AF.Exp, scale=-1.0)
            P.act(ge[:, 0:nt, :], ge[:, 0:nt, :], AF.Ln, bias=1.0)
            P.ts(lf.v(bi)[:, t0:t0 + nt, :], ge[:, 0:nt, :], -1.0, ALU.mult)
            for pr in range(0 if DBG.get("c_skipo", 0) else 2):
                po = psA[pr]
                for kc in range(8):
                    P.mm(po[:, 0:W], wC[:, kc, 768 + pr * 128:768 + (pr + 1) * 128], h_[:, kc, 0:W], start=(kc == 0), stop=(kc == 7))
                P.act(sigo.v(bi)[:, pr, t0 * 128:t0 * 128 + W], po[:, 0:W], AF.Sigmoid)
        P.scope_end()
        ab = [P.sbuf(f"ab{i}", [128, 128], F32) for i in range(2)]
        abT = P.sbuf("abT", [8, 128], F32)
        rhsE = P.sbuf("rhsE", [128, 8, 128], F32)
        Mbc = [P.sbuf(f"Mbc{i}", [128, 4, 128], F32) for i in range(2)]
        f32t = lambda nm: [P.sbuf(f"{nm}{i}", [128, 128], F32) for i in range(4)]
        bf16t = lambda nm: [P.sbuf(f"{nm}{i}", [128, 128], BF16) for i in range(4)]
        wT, dp, nd, dm, hs, sqh, rs = (f32t(n_) for n_ in ("wT", "dp", "nd", "dm", "hs", "sqh", "rs"))
        ST, qd, qz, kw = (bf16t(n_) for n_ in ("ST", "qd", "qz", "kw"))
        wk = [P.sbuf(f"wk{i}", [128, 2], F32) for i in range(4)]
        tmp4 = [P.sbuf(f"tmp4{i}", [128, 4, 128], F32) for i in range(2)]
        bend = [P.sbuf(f"bend{i}", [128, 4], F32) for i in range(2)]
        e14 = [P.sbuf(f"e14{i}", [64, 4, 128], F32) for i in range(2)]
        yb = [P.sbuf(f"ybc{i}", [128, 128], BF16) for i in range(4)]
        ps_bc = P.psum("ps_bc", [128, 1024], F32)
        psHd = [P.psum(f"psHd{i}", [128, 512], F32) for i in range(4)]
        psX = P.psum("psX", [128, 512], F32)
        for t_ in ab + [rhsE] + qd + qz + kw + sqh:
            P.memset(t_.v(None).ap(t_.t.ap()), 0.0)
        P.barrier()
        gml0 = l * PC_N + PC_GML
        it = 0
        for d in range(DBG.get("c_dirs", 2)):
            P.barrier()
            P.memset(mst[:, :], 0.0)
            P.memset(Cf[:, :, :], 0.0)
            P.memset(Cb[:, :, :], 0.0)
            P.barrier()
            order = list(range(NT)) if d == 0 else [1, 0] + list(range(NT - 1, 1, -1))
            last = 127 if d == 0 else 0
            def prologue(c, par, d=d):
                bi = 0 if c < 2 else 1 + (c - 2) // 4
                ab_ = ab[par]
                psBv = psX[:, 0:4]
                P.mm(psBv, tri[:, d, :], lf.v(bi)[:, c, d * 4:(d + 1) * 4])
                P.tt(ab_[:, 0:4], ig.v(bi)[:, c, d * 4:(d + 1) * 4], psBv, ALU.subtract)
                P.copy(ab_[:, 4:8], psBv, eng="act")
                P.tr(psX[:, 128:256], ab_[:, :], ident_f[:, :])
                P.copy(abT[:, :], psX[0:8, 128:256], eng="act")
                P.tt(rhsE[0:8, :, :], abT.raw(0, 8, 0, [[0, 8], [1, 128]]), eye8x[:, :, :], ALU.mult)
                P.mm(ps_bc[:, 0:512], ones_f[:, :], rhsE.raw(0, 128, 0, [[1, 512]]))
                P.mm(ps_bc[:, 512:1024], ones_f[:, :], rhsE.raw(0, 128, 512, [[1, 512]]))

            clist = order[:DBG.get("c_chunks", NT)]
            if clist:
                prologue(clist[0], it % 2)
            for ci_, c in enumerate(clist):
                bi = 0 if c < 2 else 1 + (c - 2) // 4
                cols = slice(c * 128, (c + 1) * 128)
                need_out = need_ctx or c >= 2
                ab_ = ab[it % 2]
                Mb = Mbc[it % 2]
                par = it % 2
                it += 1
                for h in range(4):
                    j = d * 4 + h
                    if d == 0:
                        o_ap = Mb[:, h, :]
                        a_ap = ps_bc[:, h * 128:(h + 1) * 128]
                    else:
                        o_ap = Mb.raw(0, 128, h * 128 + 127, [[-1, 128]])
                        a_ap = ps_bc.raw(0, 128, h * 128 + 127, [[-1, 128]])
                    P.scan(o_ap, zeros_f[:, :], a_ap, mst.v(j)[:, j:j + 1], ALU.add, ALU.max)
                P.copy(bend[par][:, :], ps_bc.raw(0, 128, 512 + last, [[128, 4]]))
                if need_out:
                    P.tt(tmp4[par][:, :, :], negm.raw(0, 128, d * 128, [[0, 4], [1, 128]]), Mb[:, :, :], ALU.subtract)
                    P.tt(e14[par][0:64, :, :], ps_bc.raw(0, 64, 512, [[128, 4], [1, 128]]), Mb[0:64, :, :], ALU.add)
                    P.act(e14[par][0:64, :, :], e14[par][0:64, :, :], AF.Exp, scale=-1.0)
                if ci_ + 1 < len(clist):
                    prologue(clist[ci_ + 1], 1 - par)

                def head_gen(h, d=d, c=c, bi=bi, cols=cols, need_out=need_out, ab_=ab_, Mb=Mb, par=par, last=last):
                    j = d * 4 + h
                    pair = h // 2
                    r0 = (h % 2) * 64
                    rr = slice(r0, r0 + 64)
                    Mend = Mb[:, h, last:last + 1]
                    if need_out:
                        P.copy(qz[h][rr, :], qT.v(c)[rr, pair, cols], eng="act")
                        P.act(dp[h][rr, :], Mb[rr, h, :], AF.Exp, scale=-1.0, bias=mst.v(j)[rr, j:j + 1])
                        yield
                        P.act(wT[h][:, :], tmp4[par][:, h, :], AF.Exp, bias=ab_[:, h:h + 1])
                        P.mm(psHd[h][:, 0:128], kT.v(c)[:, pair, cols], qz[h][:, :])
                        P.tt(qd[h][rr, :], qT.v(c)[rr, pair, cols], dp[h][rr, :], ALU.mult)
                        yield
                        P.tt(ST[h][:, :], psHd[h][:, 0:128], wT[h][:, :], ALU.mult)
                        yield
                        P.mm(psHd[h][0:65, 128:256], va.v(c)[:, c, h, :], ST[h][:, :], start=True, stop=False)
                        P.mm(psHd[h][0:65, 128:256], Cb.v(j)[:, j, :], qd[h][:, :], start=False, stop=True)
                    P.act(wk[h][:, 0:1], Mend, AF.Exp, scale=-1.0, bias=ab_[:, h:h + 1])
                    P.act(wk[h][:, 1:2], Mend, AF.Exp, scale=-1.0, bias=mst.v(j)[:, j:j + 1])
                    yield
                    P.tt(mst.v(j)[:, j:j + 1], bend[par][:, h:h + 1], Mend, ALU.add)
                    P.ts(kw[h][:, rr], ktok.v(c)[:, c, h * 64:(h + 1) * 64], wk[h][:, 0:1], ALU.mult)
                    yield
                    P.mm(psHd[h][:, 256:321], kw[h][:, :], va.v(c)[:, c, h, :])
                    yield
                    P.stt(Cf.v(j)[rr, j, :], Cf.v(j)[rr, j, :], wk[h][rr, 1:2], psHd[h][rr, 256:321], ALU.mult, ALU.add)
                    yield
                    P.copy(Cb.v(j)[rr, j, :], Cf.v(j)[rr, j, :], eng="act")
                    if need_out:
                        P.copy(nd[h][0:65, :], psHd[h][0:65, 128:256], eng="act")
                        yield
                        P.mm(psHd[h][:, 384:512], sel65[0:65, :], nd[h][0:65, :])
                        yield
                        P.act(dm[h][0:64, :], psHd[h][0:64, 384:512], AF.Abs)
                        yield
                        P.tt(dm[h][0:64, :], dm[h][0:64, :], e14[par][0:64, h, :], ALU.max)
                        yield
                        P.recip(dm[h][0:64, :], dm[h][0:64, :])
                        yield
                        if d == 0:
                            P.tt(hF.v(c)[rr, pair, cols], nd[h][0:64, :], dm[h][0:64, :], ALU.mult)
                        else:
                            hs_ = hs[h]
                            P.tt(hs_[rr, :], nd[h][0:64, :], dm[h][0:64, :], ALU.mult)
                            yield
                            P.tt(hs_[rr, :], hs_[rr, :], hF.v(c)[rr, pair, cols], ALU.add)
                            yield
                            P.act(sqh[h][rr, :], hs_[rr, :], AF.Square)
                            yield
                            P.mm(psHd[h][:, 384:512], onesblk[:, :], sqh[h][:, :])
                            yield
                            P.act(rs[h][rr, :], psHd[h][rr, 384:512], AF.Ln, scale=1.0 / 64, bias=EPS)
                            yield
                            P.act(rs[h][rr, :], rs[h][rr, :], AF.Exp, scale=-0.5)
                            yield
                            P.tt(hs_[rr, :], hs_[rr, :], rs[h][rr, :], ALU.mult)
                            yield
                            y_ = yb[par * 2 + pair]
                            P.stt(y_[rr, :], hs_[rr, :], pcol[rr, gml0 + pair:gml0 + pair + 1],
                                  sigo.v(bi)[rr, pair, cols], ALU.mult, ALU.mult)
                            if h % 2 == 1:
                                row0 = 512 + pair * 128
                                P.dma("sp", yTs.v((c, 4 + pair)).ap(yTs.t.ap()[row0:row0 + 128, cols]), y_[:, :])

                gens = [head_gen(h) for h in range(4)]
                while gens:
                    nxt = []
                    for g_ in gens:
                        try:
                            next(g_)
                            nxt.append(g_)
                        except StopIteration:
                            pass
                    gens = nxt
        P.scope_end()

    def phase_ssd(l, need_ctx):
        P.scope_begin()
        wD = P.sbuf("wD", [128, 8, 1032], BF16)
        wcast(wD, w_in, l, 2064, 3096)
        xT = P.sbuf("xTd", [128, 2, T], BF16)
        BT = P.sbuf("BTd", [128, 2, T], BF16)
        CT = P.sbuf("CTd", [128, 2, T], BF16)
        szT = P.sbuf("szT", [128, 2, T], BF16)
        yF = P.sbuf("yF", [128, 2, T], BF16)
        dtv = P.sbuf("dtv", [128, NT, 8], F32)
        dA = P.sbuf("dA", [128, NT, 8], F32)
        P.scope_begin()
        hb = [P.sbuf(f"hbd{i}", [128, 8, 514], BF16) for i in range(2)]
        pre = [P.sbuf(f"pre{i}", [128, 514], F32) for i in range(2)]
        acc = [P.sbuf(f"acc{i}", [128, 512], F32) for i in range(2)]
        dtx = P.sbuf("dtx", [128, 4, 8], F32)
        dty = P.sbuf("dty", [128, 4, 8], F32)
        psA = [P.psum(f"psAd{i}", [128, 512], F32) for i in range(2)]
        psH2 = P.psum("psH2", [128, 512], F32)
        pc0 = l * PC_N
        pb0 = l * PB_N
        dests = [xT, xT, BT, BT, CT, CT]
        colof = [0, 128, 512, 640, 768, 896]
        for bi, (t0, nt) in enumerate(BLOCKS[:DBG.get("d_blocks", 9)]):
            W = nt * 128
            h_ = hb[bi % 2]
            has_l = bi >= 2
            has_r = 1 <= bi < len(BLOCKS) - 1
            c_lo = t0 * 128 - (1 if has_l else 0)
            c_hi = t0 * 128 + W + (1 if has_r else 0)
            d_lo = 0 if has_l else 1
            P.dma("sp", h_[:, :, d_lo:d_lo + (c_hi - c_lo)], hTs.v(bi).ap(hT_view(c_lo, c_hi)))
            if not has_l:
                P.memset(h_[:, :, 0:1], 0.0, eng="pool")
            if not has_r:
                P.memset(h_[:, :, W + 1:W + 2], 0.0, eng="pool")
            n_main = min(512, W + 2)
            for cc in range(DBG.get("d_ncc", 6)):
                ps = psA[cc % 2]
                pr_ = pre[cc % 2]
                wc = colof[cc]
                for kc in range(8):
                    P.mm(ps[:, 0:n_main], wD[:, kc, wc:wc + 128], h_[:, kc, 0:n_main], start=(kc == 0), stop=(kc == 7))
                P.copy(pr_[:, 0:n_main], ps[:, 0:n_main], eng="act")
                if W + 2 > 512:
                    for kc in range(8):
                        P.mm(psH2[:, 0:2], wD[:, kc, wc:wc + 128], h_[:, kc, 512:514], start=(kc == 0), stop=(kc == 7))
                    P.copy(pr_[:, 512:514], psH2[:, 0:2], eng="act")
                a_ = acc[cc % 2]
                cw = lambda jj: pcol[:, pc0 + PC_CW + jj * 6 + cc:pc0 + PC_CW + jj * 6 + cc + 1]
                P.ts(a_[:, 0:W], pr_[:, 0:W], cw(0), ALU.mult)
                P.stt(a_[:, 0:W], pr_[:, 1:W + 1], cw(1), a_[:, 0:W], ALU.mult, ALU.add)
                P.stt(a_[:, 0:W], pr_[:, 2:W + 2], cw(2), a_[:, 0:W], ALU.mult, ALU.add)
                P.act(dests[cc].v(bi)[:, cc % 2, t0 * 128:t0 * 128 + W], a_[:, 0:W], AF.Silu,
                      bias=pcol[:, pc0 + PC_CB + cc:pc0 + PC_CB + cc + 1])
            for zc in range(DBG.get("d_nz", 2)):
                ps = psA[zc]
                for kc in range(8):
                    P.mm(ps[:, 0:W], wD[:, kc, 256 + zc * 128:256 + (zc + 1) * 128], h_[:, kc, 1:W + 1],
                         start=(kc == 0), stop=(kc == 7))
                P.act(szT.v(bi)[:, zc, t0 * 128:t0 * 128 + W], ps[:, 0:W], AF.Silu)
            if DBG.get("d_skipdt", 0) == 1:
                continue
            for n in range(nt):
                for kc in range(8):
                    P.mm(psH2[:, 16 + n * 8:24 + n * 8], h_[:, kc, 1 + n * 128:1 + (n + 1) * 128], wD[:, kc, 1024:1032],
                         start=(kc == 0), stop=(kc == 7))
            if DBG.get("d_skipdt", 0) == 2:
                continue
            dps = psH2.raw(0, 128, 16, [[8, nt], [1, 8]])
            dtops = [
                lambda: P.tt(dtx[:, 0:nt, :], dps, pbc.raw(0, 128, pb0 + PB_DTB, [[0, nt], [1, 8]]), ALU.add),
                lambda: P.act(dty[:, 0:nt, :], dtx[:, 0:nt, :], AF.Abs),
                lambda: P.act(dty[:, 0:nt, :], dty[:, 0:nt, :], AF.Exp, scale=-1.0),
                lambda: P.act(dty[:, 0:nt, :], dty[:, 0:nt, :], AF.Ln, bias=1.0),
                lambda: P.ts(dtx[:, 0:nt, :], dtx[:, 0:nt, :], 0.0, ALU.max),
                lambda: P.tt(dtv.v(bi)[:, t0:t0 + nt, :], dtx[:, 0:nt, :], dty[:, 0:nt, :], ALU.add),
                lambda: [P.tt(dA.v(bi)[:, t0 + n_, :], dtv.v(bi)[:, t0 + n_, :], abc[:, l * 8:(l + 1) * 8], ALU.mult)
                         for n_ in range(nt)],
            ]
            for f_ in dtops[:DBG.get("d_dtops", 99)]:
                f_()
        P.scope_end()
        hf = P.sbuf("hstf", [128, 8, 64], F32)
        hz = P.sbuf("hstz", [128, 8, 128], BF16)

        def mk(dd):
            B = {}
            B["cum"] = [P.sbuf(f"cum{dd}{i}", [128, 128], F32) for i in range(2)]
            B["cumT"] = P.sbuf(f"cumT{dd}", [8, 128], F32)
            B["ncum"] = [P.sbuf(f"ncum{dd}{i}", [128, 4], F32) for i in range(2)]
            B["cbc"] = [P.sbuf(f"cbc{dd}{i}", [128, 512], F32) for i in range(2)]
            B["xtB"] = [P.sbuf(f"xtB{dd}{i}", [128, 4, 128], BF16) for i in range(2)]
            B["rhsE"] = P.sbuf(f"rhsEd{dd}", [128, 4, 128], F32)
            B["ecum"] = [P.sbuf(f"ecum{dd}{i}", [128, 4, 128], BF16) for i in range(2)]
            B["cend"] = [P.sbuf(f"cend{dd}{i}", [128, 8], F32) for i in range(2)]
            B["tmpall"] = [P.sbuf(f"tmpall{dd}{i}", [128, 4, 128], F32) for i in range(2)]
            B["dec"] = [P.sbuf(f"dec{dd}{i}", [128, 128], F32) for i in range(4)]
            B["ST"] = [P.sbuf(f"STd{dd}{i}", [128, 128], BF16) for i in range(4)]
            B["xdtz"] = [P.sbuf(f"xdtz{dd}{i}", [128, 128], BF16) for i in range(4)]
            B["Cd"] = [P.sbuf(f"Cd{dd}{i}", [128, 128], BF16) for i in range(4)]
            B["wend"] = [P.sbuf(f"wend{dd}{i}", [128, 1], F32) for i in range(4)]
            B["Bw"] = [P.sbuf(f"Bw{dd}{i}", [128, 128], BF16) for i in range(4)]
            B["yz"] = [P.sbuf(f"yz{dd}{i}", [128, 2, 128], F32) for i in range(2)]
            B["sqy"] = P.sbuf(f"sqy{dd}", [128, 2, 128], F32)
            B["rs"] = P.sbuf(f"rsd{dd}", [128, 128], F32)
            B["yb"] = [P.sbuf(f"ybd{dd}{i}", [128, 2, 128], BF16) for i in range(2)]
            B["psTt"] = P.psum(f"psTtd{dd}", [128, 4, 128], BF16)
            B["ps_bc"] = P.psum(f"ps_bcd{dd}", [128, 512], F32)
            B["psX"] = P.psum(f"psXd{dd}", [128, 512], F32)
            for t_ in B["cum"] + [B["rhsE"]] + B["xdtz"]:
                P.memset(t_.v(None).ap(t_.t.ap()), 0.0)
            return B

        ndir = DBG.get("d_dirs", 2)
        DB = [mk(dd) for dd in range(ndir)]
        psY = [P.psum(f"psYd{i}", [128, 512], F32) for i in range(2)]
        P.memset(hf[:, :, :], 0.0)
        P.memset(hz[:, :, :], 0.0)
        P.barrier()
        orders = [list(range(NT)), [1, 0] + list(range(NT - 1, 1, -1))]
        steps = [{c: i for i, c in enumerate(o)} for o in orders]

        def dir_gen(d):
            B = DB[d]
            last = 127 if d == 0 else 0
            psX, ps_bc, psTt = B["psX"], B["ps_bc"], B["psTt"]
            clist = orders[d][:DBG.get('d_chunks', NT)]

            def prologue(c, k1):
                bi = 0 if c < 2 else 1 + (c - 2) // 4
                cols = slice(c * 128, (c + 1) * 128)
                need_out = need_ctx or c >= 2
                cum_ = B["cum"][k1]
                P.mm(psX[:, 0:4], tri[:, d, :], dA.v(bi)[:, c, d * 4:(d + 1) * 4])
                P.copy(cum_[:, 0:4], psX[:, 0:4], eng="act")
                P.ts(B["ncum"][k1][:, :], psX[:, 0:4], -1.0, ALU.mult)
                P.tr(psX[:, 128:256], cum_[:, :], ident_f[:, :])
                P.copy(B["cumT"][0:4, :], psX[0:4, 128:256], eng="act")
                P.tt(B["rhsE"][0:4, :, :], B["cumT"].raw(0, 4, 0, [[0, 4], [1, 128]]), eye8x[0:4, 0:4, :], ALU.mult)
                P.mm(ps_bc[:, :], ones_f[:, :], B["rhsE"].raw(0, 128, 0, [[1, 512]]))
                ce = B["cend"][k1]
                P.copy(B["cbc"][k1][:, :], ps_bc[:, :], eng="act")
                P.copy(ce[:, 0:4], ps_bc.raw(0, 128, last, [[128, 4]]))
                P.act(ce[:, 4:8], ce[:, 0:4], AF.Exp)
                P.tr(psTt[:, 0, :], xT.v(bi)[:, 0, cols], ident_b[:, :])
                P.tr(psTt[:, 1, :], xT.v(bi)[:, 1, cols], ident_b[:, :])
                P.tr(psTt[:, 2, :], BT.v(bi)[:, 0, cols], ident_b[:, :])
                P.tr(psTt[:, 3, :], BT.v(bi)[:, 1, cols], ident_b[:, :])
                P.copy(B["xtB"][k1][:, :, :], psTt[:, :, :], eng="act")
                if need_out:
                    P.act(B["ecum"][k1][:, :, :], B["cbc"][k1].raw(0, 128, 0, [[128, 4], [1, 128]]), AF.Exp)
                    P.tt(B["tmpall"][k1][:, :, :], negm.raw(0, 128, d * 128, [[0, 4], [1, 128]]),
                         B["cbc"][k1].raw(0, 128, 0, [[128, 4], [1, 128]]), ALU.add)

            if clist:
                prologue(clist[0], 0)
            for ci_, c in enumerate(clist):
                bi = 0 if c < 2 else 1 + (c - 2) // 4
                cols = slice(c * 128, (c + 1) * 128)
                need_out = need_ctx or c >= 2
                k1 = ci_ % 2
                second = (ndir == 2) and steps[1 - d][c] < ci_
                if ndir == 1:
                    second = False
                cum_ = B["cum"][k1]
                ce = B["cend"][k1]
                if ci_ + 1 < len(clist):
                    prologue(clist[ci_ + 1], 1 - k1)
                yz_ = B["yz"][k1]

                def pair_gen(p):
                    bank = psY[p]
                    yo = d * 256
                    gcol = slice(256 + p * 128, 384 + p * 128)
                    heads = [(2 * p + q, d * 4 + 2 * p + q, slice(q * 64, q * 64 + 64)) for q in range(2)]
                    xdtz, wend, Bw, dec, ST, Cd = B["xdtz"], B["wend"], B["Bw"], B["dec"], B["ST"], B["Cd"]
                    if need_out:
                        P.mm(psX[:, gcol], BT.v(bi)[:, p, cols], CT.v(bi)[:, p, cols])
                    for h, j, rr in heads:
                        P.ts(xdtz[h][:, rr], B["xtB"][k1][:, p, rr], dtv.v(bi)[:, c, j:j + 1], ALU.mult)
                        P.act(wend[h][:, :], cum_[:, h:h + 1], AF.Exp, scale=-1.0, bias=ce[:, h:h + 1])
                    yield
                    for h, j, rr in heads:
                        if need_out:
                            P.tt(Cd[h][:, :], CT.v(bi)[:, p, cols], B["ecum"][k1][:, h, :], ALU.mult, eng="pool")
                        P.ts(Bw[h][:, :], B["xtB"][k1][:, 2 + p, :], wend[h][:, 0:1], ALU.mult)
                    yield
                    if need_out:
                        for h, j, rr in heads:
                            P.act(dec[h][:, :], B["tmpall"][k1][:, h, :], AF.Exp, bias=B["ncum"][k1][:, h:h + 1])
                        yield
                        for h, j, rr in heads:
                            P.tt(ST[h][:, :], psX[:, gcol], dec[h][:, :], ALU.mult)
                        yield
                        (hA, jA, _), (hB, jB, _) = heads
                        yv = bank[:, yo:yo + 128]
                        P.mm(yv, xdtz[hA][:, :], ST[hA][:, :], start=True, stop=False)
                        P.mm(yv, hz.v(jA)[:, jA, :], Cd[hA][:, :], start=False, stop=False)
                        P.mm(yv, xdtz[hB][:, :], ST[hB][:, :], start=False, stop=False)
                        P.mm(yv, hz.v(jB)[:, jB, :], Cd[hB][:, :], start=False, stop=True)
                        yield
                        if not second:
                            P.copy(yF.v(c)[:, p, cols], yv, eng="act")
                        else:
                            P.tt(yz_[:, p, :], yv, yF.v(c)[:, p, cols], ALU.add)
                    for q, (h, j, rr) in enumerate(heads):
                        P.mm(bank[:, yo + 128 + q * 64:yo + 192 + q * 64], Bw[h][:, :], xdtz[h][:, rr])
                    yield
                    for q, (h, j, rr) in enumerate(heads):
                        P.stt(hf.v(j)[:, j, :], hf.v(j)[:, j, :], ce[:, 4 + h:5 + h],
                              bank[:, yo + 128 + q * 64:yo + 192 + q * 64], ALU.mult, ALU.add)
                    yield
                    for h, j, rr in heads:
                        P.copy(hz.v(j)[:, j, rr], hf.v(j)[:, j, :], eng="act")
                    if need_out and second:
                        P.stt(yz_[:, p, :], xT.v(bi)[:, p, cols], pcol[:, pc0 + PC_DSK + p:pc0 + PC_DSK + p + 1],
                              yz_[:, p, :], ALU.mult, ALU.add)
                        yield
                        P.tt(yz_[:, p, :], yz_[:, p, :], szT.v(bi)[:, p, cols], ALU.mult)

                gens = [pair_gen(p) for p in range(2)]
                while gens:
                    nxt = []
                    for g_ in gens:
                        try:
                            next(g_)
                            nxt.append(g_)
                        except StopIteration:
                            pass
                    gens = nxt
                    yield
                if need_out and second:
                    sqy, rs = B["sqy"], B["rs"]
                    P.act(sqy[:, :, :], yz_[:, :, :], AF.Square)
                    pSS = psX[:, 384:512]
                    P.mm(pSS, ones_f[:, :], sqy[:, 0, :], start=True, stop=False)
                    P.mm(pSS, ones_f[:, :], sqy[:, 1, :], start=False, stop=True)
                    P.act(rs[:, :], pSS, AF.Ln, scale=1.0 / 256, bias=EPS)
                    P.act(rs[:, :], rs[:, :], AF.Exp, scale=-0.5)
                    y_ = B["yb"][k1]
                    for pair in range(2):
                        P.stt(y_[:, pair, :], yz_[:, pair, :],
                              pcol[:, pc0 + PC_GSSM + pair:pc0 + PC_GSSM + pair + 1], rs[:, :], ALU.mult, ALU.mult)
                    P.dma("sp", yTs.v((c, 6)).ap(yTs.t.ap()[768:1024, cols].rearrange("(a p) t -> p a t", p=128)),
                          y_[:, :, :])
                yield

        dgens = [dir_gen(d) for d in range(ndir)]
        while dgens:
            nxt = []
            for g_ in dgens:
                try:
                    next(g_)
                    nxt.append(g_)
                except StopIteration:
                    pass
            dgens = nxt
        P.scope_end()

    def phase_oproj(l, need_ctx):
        P.scope_begin()
        wo = P.sbuf("wo", [128, 8, D], BF16)
        wcast(wo, w_out, l, 0, D)
        yTb = [P.sbuf(f"yTb{i}", [128, 8, 512], BF16) for i in range(2)]
        xb = [P.sbuf(f"xbo{i}", [128, 4, D], F32) for i in range(2)]
        junk = P.sbuf("junko", [128, D], BF16)
        xh = [P.sbuf(f"xho{i}", [128, D], F32) for i in range(2)]
        hb = [P.sbuf(f"hbo{i}", [128, 8, 512], BF16) for i in range(2)]
        ss = [P.sbuf(f"sso{i}", [128, 8], F32) for i in range(2)]
        t1 = [P.sbuf(f"t1o{i}", [128, 512], F32) for i in range(2)]
        psO = [P.psum(f"psOo{i}", [128, 512], F32) for i in range(2)]
        psT = [P.psum(f"psTo{i}", [128, D], F32) for i in range(2)]
        kk = [0]

        def oproj(bi):
            t0, nt = BLOCKS[bi]
            W = nt * 128
            ci = 1 if bi == 0 else 0
            X = xb[bi % 2]
            yb_ = yTb[bi % 2]
            P.dma("sp", X[:, 0:nt, :], x_src(l, bi))
            P.dma("sp", yb_[:, :, 0:W],
                  yTs.v(("o", bi)).ap(yTs.t.ap().rearrange("(kc p) t -> p kc t", p=128)[:, :, t0 * 128:t0 * 128 + W]))
            for n in range(nt):
                for half in range(2):
                    ps = psO[kk[0] % 2]
                    t_ = t1[kk[0] % 2]
                    kk[0] += 1
                    for fc in range(8):
                        P.mm(ps[:, :], yb_[:, fc, n * 128:(n + 1) * 128], wo[:, fc, half * 512:(half + 1) * 512],
                             start=(fc == 0), stop=(fc == 7))
                    P.tt(t_[:, :], ps[:, :], gtb[:, 0, ci, half * 512:(half + 1) * 512], ALU.mult)
                    P.tt(X[:, n, half * 512:(half + 1) * 512], X[:, n, half * 512:(half + 1) * 512], t_[:, :], ALU.add,
                         eng="pool")
            P.dma("sp", Xs.v(bi).ap(Xs.t.ap()[t0 * 128:(t0 + nt) * 128, :].rearrange("(n p) d -> p n d", p=128)),
                  X[:, 0:nt, :])

        def onorm(bi):
            t0, nt = BLOCKS[bi]
            W = nt * 128
            ci = 1 if bi == 0 else 0
            norm_block(xb[bi % 2], nt, ci, 2, hb[bi % 2], ss[bi % 2], xh, psT, junk)
            P.dma("sp", hTs.v(bi).ap(hT_view(t0 * 128, t0 * 128 + W)), hb[bi % 2][:, :, 0:W])

        blist = [bi for bi in range(len(BLOCKS)) if not (bi == 0 and not need_ctx)]
        oproj(blist[0])
        for i_, bi in enumerate(blist):
            if i_ + 1 < len(blist):
                oproj(blist[i_ + 1])
            onorm(bi)
        P.scope_end()

    def phase_ffn(l, fh, need_ctx, final):
        P.scope_begin()
        w1 = P.sbuf("w1", [128, 8, 2048], BF16)
        w2 = P.sbuf("w2", [128, 16, D], BF16)
        wcast(w1, w_ff1, l, fh * 2048, (fh + 1) * 2048)
        src2 = w_ff2.t.ap()[l].rearrange("(fc p) n -> p fc n", p=128)
        for fc0 in range(0, 16, 4):
            for hh in range(2):
                P.dma("pool", w2[:, fc0:fc0 + 4, hh * 512:(hh + 1) * 512],
                      w_ff2.v(l).ap(src2[:, fh * 16 + fc0:fh * 16 + fc0 + 4, hh * 512:(hh + 1) * 512]))
        hb = [P.sbuf(f"hbf{i}", [128, 8, 512], BF16) for i in range(2)]
        xb = [P.sbuf(f"xbf{i}", [128, 4, D], F32) for i in range(2)]
        aT = P.sbuf("aT", [128, 16, 512], BF16)
        r = [P.sbuf(f"r{i}", [128, 512], F32) for i in range(2)]
        t1 = [P.sbuf(f"t1f{i}", [128, 512], F32) for i in range(2)]
        gf = P.sbuf("gf", [128, D], F32)
        junk = P.sbuf("junkf", [128, D], BF16)
        ssf = P.sbuf("ssf", [128, 8], F32)
        ps1 = [P.psum(f"ps1{i}", [128, 512], F32) for i in range(3)]
        ps2 = [P.psum(f"ps2{i}", [128, 512], F32) for i in range(3)]
        if final:
            P.dma("sp", gf[:, :], gfin_d[:, :])
        k1 = 0
        k2 = 0
        loaded = set()

        def ffn_load(bj):
            t0_, nt_ = BLOCKS[bj]
            W_ = nt_ * 128
            loaded.add(bj)
            P.dma("sp", hb[bj % 2][:, :, 0:W_], hTs.v(bj).ap(hT_view(t0_ * 128, t0_ * 128 + W_)))
            P.dma("sp", xb[bj % 2][:, 0:nt_, :],
                  Xs.v(bj).ap(Xs.t.ap()[t0_ * 128:(t0_ + nt_) * 128, :].rearrange("(n p) d -> p n d", p=128)))

        for bi, (t0, nt) in enumerate(BLOCKS):
            if bi == 0 and not need_ctx:
                continue
            W = nt * 128
            ci = 1 if bi == 0 else 0
            h_ = hb[bi % 2]
            X = xb[bi % 2]
            xdr = Xs.v(bi).ap(Xs.t.ap()[t0 * 128:(t0 + nt) * 128, :].rearrange("(n p) d -> p n d", p=128))
            if bi not in loaded:
                ffn_load(bi)
            nb_ = bi + 1
            if nb_ < len(BLOCKS):
                ffn_load(nb_)
            for fc in range(16):
                ps = ps1[k1 % 3]
                r_ = r[k1 % 2]
                k1 += 1
                for kc in range(8):
                    P.mm(ps[:, 0:W], w1[:, kc, fc * 128:(fc + 1) * 128], h_[:, kc, 0:W], start=(kc == 0), stop=(kc == 7))
                P.act(r_[:, 0:W], ps[:, 0:W], AF.Relu)
                P.tt(aT[:, fc, 0:W], r_[:, 0:W], r_[:, 0:W], ALU.mult)
            for n in range(nt):
                for half in range(2):
                    ps = ps2[k2 % 3]
                    t_ = t1[k2 % 2]
                    k2 += 1
                    for fc in range(16):
                        P.mm(ps[:, :], aT[:, fc, n * 128:(n + 1) * 128], w2[:, fc, half * 512:(half + 1) * 512],
                             start=(fc == 0), stop=(fc == 15))
                    P.tt(t_[:, :], ps[:, :], gtb[:, 1, ci, half * 512:(half + 1) * 512], ALU.mult)
                    P.tt(X[:, n, half * 512:(half + 1) * 512], X[:, n, half * 512:(half + 1) * 512], t_[:, :], ALU.add,
                         eng="pool")
            if final:
                for n in range(nt):
                    P.act(junk[:, :], X[:, n, :], AF.Square, accum_out=ssf[:, n:n + 1])
                P.act(ssf[:, 4:4 + nt], ssf[:, 0:nt], AF.Ln, scale=1.0 / D, bias=EPS)
                P.act(ssf[:, 4:4 + nt], ssf[:, 4:4 + nt], AF.Exp, scale=-0.5)
                for n in range(nt):
                    P.stt(X[:, n, :], X[:, n, :], ssf[:, 4 + n:5 + n], gf[:, :], ALU.mult, ALU.mult)
                l0 = (t0 - 2) * 128
                P.dma("sp", out_d.v(bi).ap(out_d.t.ap()[l0:l0 + W, :].rearrange("(n p) d -> p n d", p=128)),
                      X[:, 0:nt, :])
            else:
                P.dma("sp", xdr, X[:, 0:nt, :])
        P.scope_end()

    P.barrier()
    done = False
    for l in range(nlayers):
        need_ctx = l < L - 1
        steps = [("M", lambda: phase_mod(l)), ("N", lambda: phase_norm(l)),
                 ("A", lambda: phase_attn(l, 0, need_ctx)), ("B", lambda: phase_attn(l, 1, need_ctx)),
                 ("C", lambda: phase_mlstm(l, need_ctx)), ("D", lambda: phase_ssd(l, need_ctx)),
                 ("O", lambda: phase_oproj(l, need_ctx)),
                 ("F0", lambda: phase_ffn(l, 0, need_ctx, False)),
                 ("F1", lambda: phase_ffn(l, 1, need_ctx, l == L - 1))]
        for name, fn in steps:
            fn()
            if stop == (l, name):
                done = True
                break
        if done:
            break
    P.finish()
    return nc, P


def _consts():
    s = np.arange(128)[:, None]
    t = np.arange(128)[None, :]
    negm = np.zeros((128, 2, 128), np.float32)
    negm[:, 0, :] = np.where(s <= t, 0.0, -1e9)
    negm[:, 1, :] = np.where(s >= t, 0.0, -1e9)
    tri = (negm == 0).astype(np.float32)
    sel = np.zeros((128, 128), np.float32)
    sel[64, :] = 1.0
    oblk = np.zeros((128, 128), np.float32)
    oblk[:64, :64] = 1.0
    oblk[64:, 64:] = 1.0
    eye8x = np.zeros((8, 8, 128), np.float32)
    for k in range(8):
        eye8x[k, k, :] = 1.0
    rows = NLAT // 64
    row = np.repeat(np.arange(rows, dtype=np.float32), 64)
    col = np.tile(np.arange(64, dtype=np.float32), rows)
    inv = (10000.0 ** (-np.arange(0, 32, 2, dtype=np.float32) / 32)).astype(np.float32)
    ang = np.concatenate([row[:, None] * inv, col[:, None] * inv], axis=-1).astype(np.float32)
    cos = np.concatenate([np.ones((NCTX, 32), np.float32), np.cos(ang).astype(np.float32)], 0)
    sin = np.concatenate([np.zeros((NCTX, 32), np.float32), np.sin(ang).astype(np.float32)], 0)
    rope = np.concatenate([cos, sin], -1).reshape(NT, 128, 64).transpose(1, 0, 2).reshape(128, NT * 64)
    wm = np.zeros((128, 6, 512), np.float32)
    q = np.arange(512)[None, :]
    for r in range(-1, 5):
        kpos = r * 128 + np.arange(128)[:, None]
        wm[:, r + 1, :] = np.where(np.abs(q - kpos) <= 128, 0.0, NEG)
    return dict(ident_f=np.eye(128, dtype=np.float32), negm=negm.reshape(128, 256), tri=tri.reshape(128, 256),
                sel65=sel, onesblk=oblk, eye8x=eye8x.reshape(8, 1024), rope=np.ascontiguousarray(rope),
                wmask=wm.reshape(128, 6 * 512))


def _colform(v):
    return np.ascontiguousarray(np.asarray(v, np.float32).reshape(-1, 128).T)


def _prep_shared(inp):
    f = lambda a: np.ascontiguousarray(np.asarray(a, np.float32))
    pcol = np.zeros((128, L * PC_N), np.float32)
    pbc = np.zeros((128, L * PB_N), np.float32)
    for l in range(L):
        o = l * PC_N
        pcol[:, o + PC_G1:o + PC_G1 + 8] = _colform(inp["g_norm1"][l])
        pcol[:, o + PC_G2:o + PC_G2 + 8] = _colform(inp["g_norm2"][l])
        for j in range(3):
            pcol[:, o + PC_CW + j * 6:o + PC_CW + j * 6 + 6] = _colform(inp["conv_w"][l][j])
        pcol[:, o + PC_CB:o + PC_CB + 6] = _colform(inp["conv_b"][l])
        pcol[:, o + PC_GML:o + PC_GML + 2] = _colform(inp["g_mlstm"][l])
        pcol[:, o + PC_GSSM:o + PC_GSSM + 2] = _colform(inp["g_ssm"][l])
        pcol[:, o + PC_DSK:o + PC_DSK + 2] = _colform(np.repeat(np.asarray(inp["d_skip"][l], np.float32), 64))
        o = l * PB_N
        pbc[:, o + PB_GQ:o + PB_GQ + 64] = np.asarray(inp["g_q_b"][l], np.float32)[None, :]
        pbc[:, o + PB_GK:o + PB_GK + 64] = np.asarray(inp["g_k_b"][l], np.float32)[None, :]
        pbc[:, o + PB_SINK:o + PB_SINK + 4] = np.asarray(inp["sink_a"][l], np.float32)[None, :]
        bi_ = np.asarray(inp["b_igate"][l], np.float32)
        bf_ = np.asarray(inp["b_fgate"][l], np.float32)
        pbc[:, o + PB_GB:o + PB_GB + 16] = np.concatenate([bi_[0], bf_[0], bi_[1], bf_[1]])[None, :]
        pbc[:, o + PB_ALOG:o + PB_ALOG + 8] = np.asarray(inp["a_log"][l], np.float32).reshape(-1)[None, :]
        pbc[:, o + PB_DTB:o + PB_DTB + 8] = np.asarray(inp["dt_bias"][l], np.float32).reshape(-1)[None, :]
    b_ada = np.asarray(inp["b_ada"], np.float32)
    bada_col = np.concatenate([_colform(b_ada[l]) for l in range(L)], axis=1)
    bada_gt = np.concatenate([np.concatenate([b_ada[l, 2 * D:3 * D], b_ada[l, 5 * D:6 * D]]) for l in range(L)])
    bada_gt = np.ascontiguousarray(np.broadcast_to(bada_gt[None, :], (128, L * 2 * D)))
    gfin = np.ascontiguousarray(np.broadcast_to(np.asarray(inp["g_final"], np.float32)[None, :], (128, D)))
    sh = dict(w_ada=f(inp["w_ada"]), bada_col=np.ascontiguousarray(bada_col), bada_gt=bada_gt, w_in=f(inp["w_in"]),
              w_out=f(inp["w_out"]), w_ff1=f(inp["w_ff1"]), w_ff2=f(inp["w_ff2"]), pcol=pcol, pbc=pbc, gfin=gfin)
    sh.update(_consts())
    return sh


def make_in_maps(inp, cores):
    sh = _prep_shared(inp)
    maps = []
    for b in cores:
        m = dict(sh)
        m["xin"] = np.ascontiguousarray(np.concatenate([np.asarray(inp["ctx"][b], np.float32),
                                                        np.asarray(inp["x"][b], np.float32)], 0))
        m["ccol"] = np.ascontiguousarray(np.concatenate([_colform(inp["c"][b]), _colform(inp["c_ctx"])], 1))
        maps.append(m)
    return maps


def kernel(**inputs):
    nc, _ = build_program()
    maps = make_in_maps(inputs, list(range(8)))
    res = run_bass_kernel_spmd(nc, maps, core_ids=list(range(8)))
    return np.stack([np.asarray(r["out"], np.float32) for r in res.results], 0)
```
